# Optimizing a Trainium2 kernel written in Bass

```python
import math
import jax, jax.numpy as jnp
from jax import lax
import numpy as np

D_MODEL = 1024
BATCH = 4
SEQ = 4096
DEPTH = 1

DN_HEADS = 4
DN_HEAD_DIM = 128
DN_WIDTH = DN_HEADS * DN_HEAD_DIM
DN_QKV_WIDTH = 3 * DN_WIDTH
DN_CONV = 4
DN_CHUNK = 64
SWA_HEADS = 8
SWA_KV_HEADS = 2
SWA_HEAD_DIM = 64
SWA_GROUP = SWA_HEADS // SWA_KV_HEADS
SWA_WIDTH = SWA_HEADS * SWA_HEAD_DIM
SWA_KV_WIDTH = SWA_KV_HEADS * SWA_HEAD_DIM
WINDOW = 128
SWA_BLOCK = 128
REL_BUCKETS = 32
REL_MAX_DIST = 128
N_BRANCHES = 2
D_FF = ((8 * D_MODEL // 3 + 255) // 256) * 256
D_IN = DN_QKV_WIDTH + DN_WIDTH + 2 * DN_HEADS + SWA_WIDTH + 2 * SWA_KV_WIDTH + N_BRANCHES * D_MODEL
EPS = 1e-6

kernel_name = "hybrid_gdn_swa_gated_merge"


def rms_norm(x, gain):
    xf = x.astype(jnp.float32)
    y = xf * lax.rsqrt(jnp.mean(xf * xf, axis=-1, keepdims=True) + EPS)
    return (y * gain.astype(jnp.float32)).astype(x.dtype)


def l2_norm(x):
    xf = x.astype(jnp.float32)
    return xf * lax.rsqrt(jnp.sum(xf * xf, axis=-1, keepdims=True) + EPS)


def causal_dwconv(x, w):
    k = w.shape[0]
    return lax.conv_general_dilated(
        x, w[:, None, :], window_strides=(1,), padding=((k - 1, 0),),
        dimension_numbers=("NWC", "WIO", "NWC"), feature_group_count=x.shape[-1])


def chunk_gated_delta_rule(q, k, v, g, beta):
    B, S, H, DK = q.shape
    DV = v.shape[-1]
    C = DN_CHUNK
    NC = S // C
    q = q * (DK ** -0.5)

    def to_chunks(t):
        return t.reshape(B, NC, C, H, t.shape[-1]).transpose(0, 3, 1, 2, 4)

    qc, kc, vc = to_chunks(q), to_chunks(k), to_chunks(v)
    gc = jnp.cumsum(g.reshape(B, NC, C, H).transpose(0, 3, 1, 2), axis=-1)
    bc = beta.reshape(B, NC, C, H).transpose(0, 3, 1, 2)[..., None]
    k_beta = kc * bc
    v_beta = vc * bc

    idx = jnp.arange(C)
    lower_incl = idx[:, None] >= idx[None, :]
    strict_lower = idx[:, None] > idx[None, :]
    decay = jnp.exp(jnp.where(lower_incl, gc[..., :, None] - gc[..., None, :], -jnp.inf))

    a_mat = jnp.where(strict_lower, jnp.einsum("bhncd,bhnsd->bhncs", k_beta, kc) * decay, 0.0)
    lhs = a_mat + jnp.eye(C, dtype=a_mat.dtype)
    rhs = jnp.concatenate([v_beta, k_beta * jnp.exp(gc)[..., None]], axis=-1)
    sol = lax.linalg.triangular_solve(lhs, rhs, left_side=True, lower=True, unit_diagonal=True)
    u, w = sol[..., :DV], sol[..., DV:]
    qk = jnp.einsum("bhncd,bhnsd->bhncs", qc, kc) * decay

    def step(state, inp):
        q_i, k_i, u_i, w_i, g_i, qk_i = inp
        v_new = u_i - jnp.einsum("bhck,bhkv->bhcv", w_i, state)
        o_i = (jnp.einsum("bhck,bhkv->bhcv", q_i * jnp.exp(g_i)[..., None], state)
               + jnp.einsum("bhcs,bhsv->bhcv", qk_i, v_new))
        g_last = g_i[..., -1]
        k_dec = k_i * jnp.exp(g_last[..., None] - g_i)[..., None]
        state = state * jnp.exp(g_last)[..., None, None] + jnp.einsum("bhck,bhcv->bhkv", k_dec, v_new)
        return state, o_i

    xs = tuple(jnp.moveaxis(t, 2, 0) for t in (qc, kc, u, w, gc, qk))
    state0 = jnp.zeros((B, H, DK, DV), jnp.float32)
    _, o = lax.scan(step, state0, xs)
    return o.transpose(1, 0, 3, 2, 4).reshape(B, S, H, DV)


def gated_deltanet_branch(qkv, z, b_raw, a_raw, conv_w, a_log, dt_bias, out_gain):
    B, S, _ = qkv.shape
    qkv = jax.nn.silu(causal_dwconv(qkv, conv_w))
    q, k, v = jnp.split(qkv, 3, axis=-1)
    q = l2_norm(q.reshape(B, S, DN_HEADS, DN_HEAD_DIM))
    k = l2_norm(k.reshape(B, S, DN_HEADS, DN_HEAD_DIM))
    v = v.reshape(B, S, DN_HEADS, DN_HEAD_DIM).astype(jnp.float32)
    beta = jax.nn.sigmoid(b_raw.astype(jnp.float32))
    g = -jnp.exp(a_log.astype(jnp.float32)) * jax.nn.softplus(a_raw.astype(jnp.float32) + dt_bias.astype(jnp.float32))
    o = chunk_gated_delta_rule(q, k, v, g, beta)
    o = rms_norm(o, out_gain) * jax.nn.silu(z.reshape(B, S, DN_HEADS, DN_HEAD_DIM).astype(jnp.float32))
    return o.reshape(B, S, DN_WIDTH).astype(qkv.dtype)


def t5_causal_bucket(dist):
    n = jnp.maximum(dist, 0)
    max_exact = REL_BUCKETS // 2
    nf = jnp.maximum(n, 1).astype(jnp.float32)
    large = max_exact + (jnp.log(nf / max_exact) / math.log(REL_MAX_DIST / max_exact)
                         * (REL_BUCKETS - max_exact)).astype(jnp.int32)
    large = jnp.minimum(large, REL_BUCKETS - 1)
    return jnp.where(n < max_exact, n, large)


def sliding_window_branch(q, k, v, q_gain, k_gain, sinks, rel_bias):
    B, S, _ = q.shape
    NB = S // SWA_BLOCK
    q = rms_norm(q.reshape(B, S, SWA_HEADS, SWA_HEAD_DIM), q_gain)
    k = rms_norm(k.reshape(B, S, SWA_KV_HEADS, SWA_HEAD_DIM), k_gain)
    v = v.reshape(B, S, SWA_KV_HEADS, SWA_HEAD_DIM)
    qb = q.reshape(B, NB, SWA_BLOCK, SWA_KV_HEADS, SWA_GROUP, SWA_HEAD_DIM).astype(jnp.float32)

    def band(t):
        tp = jnp.pad(t, ((0, 0), (SWA_BLOCK, 0), (0, 0), (0, 0)))
        tp = tp.reshape(B, NB + 1, SWA_BLOCK, SWA_KV_HEADS, SWA_HEAD_DIM)
        return jnp.concatenate([tp[:, :-1], tp[:, 1:]], axis=2).astype(jnp.float32)

    kb, vb = band(k), band(v)
    logits = jnp.einsum("bnqkgd,bnskd->bnkgqs", qb, kb) * (SWA_HEAD_DIM ** -0.5)

    qi = jnp.arange(SWA_BLOCK)[:, None]
    kj = jnp.arange(2 * SWA_BLOCK)[None, :]
    dist = SWA_BLOCK + qi - kj
    in_window = (dist >= 0) & (dist < WINDOW)
    key_pos = (jnp.arange(NB)[:, None] - 1) * SWA_BLOCK + jnp.arange(2 * SWA_BLOCK)[None, :]
    mask = in_window[None, :, :] & (key_pos >= 0)[:, None, :]

    bias = rel_bias.astype(jnp.float32)[t5_causal_bucket(dist)]
    bias = bias.transpose(2, 0, 1).reshape(SWA_KV_HEADS, SWA_GROUP, SWA_BLOCK, 2 * SWA_BLOCK)
    logits = jnp.where(mask[None, :, None, None], logits + bias, -jnp.inf)

    sink = sinks.astype(jnp.float32).reshape(SWA_KV_HEADS, SWA_GROUP)[None, None, :, :, None, None]
    m = jnp.maximum(jnp.max(logits, axis=-1, keepdims=True), sink)
    p = jnp.exp(logits - m)
    denom = jnp.sum(p, axis=-1, keepdims=True) + jnp.exp(sink - m)
    out = jnp.einsum("bnkgqs,bnskd->bnqkgd", p / denom, vb)
    return out.reshape(B, S, SWA_WIDTH).astype(q.dtype)


def split_combined(p):
    sizes = (DN_QKV_WIDTH, DN_WIDTH, DN_HEADS, DN_HEADS, SWA_WIDTH, SWA_KV_WIDTH, SWA_KV_WIDTH,
             N_BRANCHES * D_MODEL)
    return jnp.split(p, np.cumsum(sizes)[:-1].tolist(), axis=-1)


def setup_inputs(seed: int = 0) -> dict:
    key = jax.random.key(seed)
    ks = jax.random.split(key, 20)
    f32 = jnp.float32
    L = DEPTH

    def nrm(k, shape, scale):
        return jax.random.normal(k, shape, f32) * scale

    dt = jnp.exp(jax.random.uniform(ks[5], (L, DN_HEADS), f32, math.log(1e-3), math.log(1e-1)))
    return {
        "x": nrm(ks[0], (BATCH, SEQ, D_MODEL), 1.0),
        "attn_norm": 1.0 + nrm(ks[1], (L, D_MODEL), 0.02),
        "w_in": nrm(ks[2], (L, D_MODEL, D_IN), D_MODEL ** -0.5),
        "dn_conv": nrm(ks[3], (L, DN_CONV, DN_QKV_WIDTH), DN_CONV ** -0.5),
        "dn_a_log": jnp.log(jax.random.uniform(ks[4], (L, DN_HEADS), f32, 1.0, 16.0)),
        "dn_dt_bias": dt + jnp.log(-jnp.expm1(-dt)),
        "dn_out_norm": 1.0 + nrm(ks[6], (L, DN_HEAD_DIM), 0.02),
        "swa_q_norm": 1.0 + nrm(ks[7], (L, SWA_HEAD_DIM), 0.02),
        "swa_k_norm": 1.0 + nrm(ks[8], (L, SWA_HEAD_DIM), 0.02),
        "swa_sinks": nrm(ks[9], (L, SWA_HEADS), 0.5),
        "rel_bias": nrm(ks[10], (REL_BUCKETS, SWA_HEADS), 0.1),
        "w_branch_dn": nrm(ks[11], (L, DN_WIDTH, D_MODEL), DN_WIDTH ** -0.5),
        "w_branch_swa": nrm(ks[12], (L, SWA_WIDTH, D_MODEL), SWA_WIDTH ** -0.5),
        "w_out": nrm(ks[13], (L, D_MODEL, D_MODEL), D_MODEL ** -0.5),
        "ffn_norm": 1.0 + nrm(ks[14], (L, D_MODEL), 0.02),
        "w_gate": nrm(ks[15], (L, D_MODEL, D_FF), D_MODEL ** -0.5),
        "w_up": nrm(ks[16], (L, D_MODEL, D_FF), D_MODEL ** -0.5),
        "w_down": nrm(ks[17], (L, D_FF, D_MODEL), D_FF ** -0.5),
    }


def reference(x, attn_norm, w_in, dn_conv, dn_a_log, dn_dt_bias, dn_out_norm, swa_q_norm,
              swa_k_norm, swa_sinks, rel_bias, w_branch_dn, w_branch_swa, w_out, ffn_norm,
              w_gate, w_up, w_down):
    B, S, _ = x.shape
    for l in range(DEPTH):
        h = rms_norm(x, attn_norm[l])
        proj = h @ w_in[l]
        dn_qkv, dn_z, dn_b, dn_a, sq, sk, sv, gate_raw = split_combined(proj)
        y_dn = gated_deltanet_branch(dn_qkv, dn_z, dn_b, dn_a, dn_conv[l], dn_a_log[l],
                                     dn_dt_bias[l], dn_out_norm[l])
        y_swa = sliding_window_branch(sq, sk, sv, swa_q_norm[l], swa_k_norm[l], swa_sinks[l], rel_bias)
        gates = jax.nn.sigmoid(gate_raw.astype(jnp.float32)).astype(x.dtype)
        gates = gates.reshape(B, S, N_BRANCHES, D_MODEL)
        merged = gates[:, :, 0] * (y_dn @ w_branch_dn[l]) + gates[:, :, 1] * (y_swa @ w_branch_swa[l])
        x = x + merged @ w_out[l]
        h2 = rms_norm(x, ffn_norm[l])
        x = x + (jax.nn.silu(h2 @ w_gate[l]) * (h2 @ w_up[l])) @ w_down[l]
    return x
```

```python
import contextlib
import os
import math
import numpy as np
import ml_dtypes
import concourse.bass as bass
import concourse.mybir as mybir
from concourse.bass_utils import run_bass_kernel_spmd

F32 = mybir.dt.float32
BF16 = mybir.dt.bfloat16
AF = mybir.ActivationFunctionType
ALU = mybir.AluOpType
AX = mybir.AxisListType

D = 1024
DIN = 4872
DFF = 2816
NF = 22
NPRE = 16
NOWN = 16
NBLK = NPRE + NOWN
EPS = 1e-6
NEGBIG = -30000.0


class Op:
    __slots__ = ("eng", "fn", "dma", "dma_val", "waits", "signal", "sigval", "uid", "group")

    def __init__(self, eng, fn, dma):
        self.eng = eng
        self.fn = fn
        self.dma = dma
        self.dma_val = 0
        self.waits = []
        self.signal = False
        self.sigval = 0
        self.group = False


class Prog:
    ENGS = ("pe", "act", "dve", "pool", "sp")

    def __init__(self, tag):
        self.tag = tag
        self.ops = {e: [] for e in self.ENGS}
        self.res = {}
        self.dma_cnt = {}
        self.dma_last = {}
        self.dma_uses = {}
        self.bank_last = {}
        self.uid = 0

    def _st(self, name):
        st = self.res.get(name)
        if st is None:
            st = {"w": {}, "r": {}}
            self.res[name] = st
        return st

    def _dep(self, op, prev, kind):
        if prev is op:
            return
        if prev.dma is None and op.dma is None and prev.eng == op.eng and kind != "raw":
            return
        if prev in op.waits:
            return
        op.waits.append(prev)
        if prev.dma is None:
            prev.signal = True

    def add(self, eng, fn, r=(), w=(), pw=(), dma=None, group=False):
        if dma is not None and not group:
            n = self.dma_uses.get(dma, 0)
            self.dma_uses[dma] = n + 1
            dma = "%s_%d" % (dma, n // 14)
        op = Op(eng, fn, dma)
        op.group = group
        self.uid += 1
        op.uid = self.uid
        key = eng if dma is None else ("dma", self.uid)
        for name in r:
            for p in self._st(name)["w"].values():
                self._dep(op, p, "raw")
        for name in tuple(w) + tuple(pw):
            st = self._st(name)
            for p in st["w"].values():
                self._dep(op, p, "waw")
            for p in st["r"].values():
                self._dep(op, p, "war")
        banks = set()
        for name in tuple(r) + tuple(w) + tuple(pw):
            if name == "BT":
                banks.add("T")
            elif name[0] in "Bb" and name[1:2].isdigit():
                banks.add(name[1])
        for bk in banks:
            bl = self.bank_last.setdefault(bk, {})
            for e2, p in bl.items():
                if e2 != eng:
                    self._dep(op, p, "bank")
            bl[eng] = op
        for name in r:
            self._st(name)["r"][key] = op
        for name in w:
            st = self._st(name)
            st["w"] = {key: op}
            st["r"] = {}
        for name in pw:
            st = self._st(name)
            st["w"][key] = op
            st["r"] = {}
        if dma is not None:
            self.dma_cnt[dma] = self.dma_cnt.get(dma, 0) + 16
            op.dma_val = self.dma_cnt[dma]
            self.dma_last[dma] = op
        self.ops[eng].append(op)
        return op

    def barrier(self):
        lasts = {}
        for e in self.ENGS:
            for op in reversed(self.ops[e]):
                if op.fn is not None and op.dma is None:
                    lasts[e] = op
                    break
        dl = list(self.dma_last.values())
        for e in self.ENGS:
            b = Op(e, None, None)
            for e2, op in lasts.items():
                if e2 != e:
                    b.waits.append(op)
                    op.signal = True
            b.waits.extend(dl)
            self.ops[e].append(b)

    def emit(self, nc, semf):
        for e in self.ENGS:
            cnt = 0
            for op in self.ops[e]:
                if op.dma is None and op.signal:
                    cnt += 1
                    op.sigval = cnt
        if os.environ.get('K_DBG'):
            print('SIG', self.tag, {e: max([o.sigval for o in self.ops[e]] + [0]) for e in self.ENGS}, {e: len(self.ops[e]) for e in self.ENGS}, self.dma_cnt)
        esem = {e: semf(self.tag + "e" + e) for e in self.ENGS}
        dsem = {k: semf(self.tag + "d" + str(k)) for k in self.dma_cnt}
        prog = self

        def run(en, eng):
            known = {}
            for op in prog.ops[en]:
                for p in op.waits:
                    if p.dma is None:
                        sem, val, k = esem[p.eng], p.sigval, "e" + p.eng
                    else:
                        sem, val, k = dsem[p.dma], (prog.dma_cnt[p.dma] if p.group else p.dma_val), "d" + str(p.dma)
                    if known.get(k, 0) >= val:
                        continue
                    eng.wait_ge(sem, val)
                    known[k] = val
                if op.fn is None:
                    continue
                ins = op.fn(eng)
                if op.dma is not None:
                    ins.then_inc(dsem[op.dma], 16)
                elif op.signal:
                    ins.then_inc(esem[en], 1)

        with nc.Block() as block:
            @block.tensor
            def _(t):
                run("pe", t)

            @block.scalar
            def _(t):
                run("act", t)

            @block.vector
            def _(t):
                run("dve", t)

            @block.gpsimd
            def _(t):
                run("pool", t)

            @block.sync
            def _(t):
                run("sp", t)


def MM(out, lhsT, rhs, st=True, sp=True, skip=False):
    if skip:
        return lambda e: e.matmul(out, lhsT=lhsT, rhs=rhs, start=st, stop=sp, skip_group_check=True)
    return lambda e: e.matmul(out, lhsT=lhsT, rhs=rhs, start=st, stop=sp)


def TR(out, in_, ident):
    return lambda e: e.transpose(out=out, in_=in_, identity=ident)


def ACT(out, in_, func, **kw):
    return lambda e: e.activation(out=out, in_=in_, func=func, **kw)


def TT(out, a, b, op):
    return lambda e: e.tensor_tensor(out=out, in0=a, in1=b, op=op)


def TS(out, a, s1, op0, s2=None, op1=None):
    if op1 is None:
        return lambda e: e.tensor_scalar(out=out, in0=a, scalar1=s1, scalar2=None, op0=op0)
    return lambda e: e.tensor_scalar(out=out, in0=a, scalar1=s1, scalar2=s2, op0=op0, op1=op1)


def STT(out, a, s, b, op0, op1):
    return lambda e: e.scalar_tensor_tensor(out=out, in0=a, scalar=s, in1=b, op0=op0, op1=op1)


def CP(out, in_):
    return lambda e: e.tensor_copy(out=out, in_=in_)


def RC(out, in_):
    return lambda e: e.reciprocal(out=out, in_=in_)


def RED(out, in_):
    return lambda e: e.tensor_reduce(out=out, in_=in_, axis=AX.X, op=ALU.add)


def DMA(out, in_, **kw):
    return lambda e: e.dma_start(out=out, in_=in_, **kw)


def MSET(ap, v):
    return lambda e: e.memset(ap, v)


def v4(ap, h=4):
    return ap.rearrange("p (h d) -> p h d", h=h)


def build():
    nc = bass.Bass("TRN2", target_bir_lowering=False)

    def din(name, shape):
        return nc.dram_tensor(name, shape, F32, kind="ExternalInput").ap()

    x_d = din("x", [NBLK * 128, D])
    w_in_d = din("w_in", [D, DIN])
    w_a_d = din("w_a", [512, D])
    w_b_d = din("w_b", [512, D])
    w_out_d = din("w_out", [D, D])
    w_gate_d = din("w_gate", [D, DFF])
    w_up_d = din("w_up", [D, DFF])
    w_down_d = din("w_down", [DFF, D])
    attn_norm_d = din("attn_norm", [1, D])
    ffn_norm_d = din("ffn_norm", [1, D])
    conv_d = din("dn_conv", [4, 1536])
    alog_d = din("a_log", [1, 4])
    dtb_d = din("dt_bias", [1, 4])
    onorm_d = din("out_norm", [1, 128])
    qg_d = din("q_norm", [1, 64])
    kg_d = din("k_norm", [1, 64])
    sinks_d = din("sinks", [1, 8])
    biasP_d = din("biasP", [128, 1024])
    biasC_d = din("biasC", [128, 1024])
    pm_d = din("pm", [128, 1])
    ident_d = din("ident", [128, 128])
    U_d = din("U", [128, 128])
    Urev_d = din("Urev", [128, 128])
    NEG_d = din("NEGM", [128, 128])
    STR_d = din("STRICT", [128, 128])
    m01_d = din("m01", [128, 2])
    out_d = nc.dram_tensor("out", [NOWN * 128, D], F32, kind="ExternalOutput").ap()
    wcache = nc.dram_tensor("wcache", [NF * 128, 16 * 128], BF16, kind="Internal").ap()
    wocache = nc.dram_tensor("wocache", [128, 8 * D], BF16, kind="Internal").ap()
    wdcache = nc.dram_tensor("wdcache", [128, NF * D], BF16, kind="Internal").ap()

    top = contextlib.ExitStack()
    with top:
        def sbt(es, name, shape, dt):
            return es.enter_context(nc.sbuf_tensor("s_" + name, shape, dt))

        def semf(name):
            return top.enter_context(nc.semaphore(name))

        B = [top.enter_context(nc.psum_tensor("B%d" % i, [128, 512], F32)) for i in range(7)]
        BT = top.enter_context(nc.psum_tensor("BT", [128, 1024], BF16))

        hTm = sbt(top, "hTm", [128, 8, NOWN * 128], BF16)
        identf = sbt(top, "identf", [128, 128], F32)
        identb = sbt(top, "identb", [128, 128], BF16)

        s1 = contextlib.ExitStack()
        with s1:
            yAT = sbt(s1, "yAT", [128, 4, NOWN * 128], BF16)
            yBT = sbt(s1, "yBT", [128, 4, NOWN * 128], BF16)

            sa = contextlib.ExitStack()
            with sa:
                P = Prog("a")
                NWT = 1288
                CZ, CBA, CQ, CKV = 0, 512, 520, 1032
                winT = sbt(sa, "winT", [128, 8, NWT], BF16)
                wqs = [sbt(sa, "wqs%d" % i, [128, 8, 128], BF16) for i in range(3)]
                Uc = sbt(sa, "Uc", [128, 128], F32)
                Urev = sbt(sa, "Urev", [128, 128], F32)
                NEGM = sbt(sa, "NEGM", [128, 128], F32)
                STR = sbt(sa, "STR", [128, 128], F32)
                m01 = sbt(sa, "m01", [128, 2], F32)
                onesb = sbt(sa, "onesb", [128, 128], BF16)
                gainT = sbt(sa, "gainT", [128, 8], F32)
                wconv = sbt(sa, "wconv", [128, 4, 12], F32)
                alog = sbt(sa, "alog", [128, 4], F32)
                dtb = sbt(sa, "dtb", [128, 4], F32)
                negA = sbt(sa, "negA", [128, 4], F32)
                onorm = sbt(sa, "onorm", [128, 128], F32)
                qg = sbt(sa, "qg", [128, 64], F32)
                kg = sbt(sa, "kg", [128, 64], F32)
                sinks = sbt(sa, "sinks", [128, 8], F32)
                esink = sbt(sa, "esink", [128, 8], F32)
                biasP8 = sbt(sa, "biasP8", [128, 1024], BF16)
                biasC8 = sbt(sa, "biasC8", [128, 1024], BF16)
                pm = sbt(sa, "pm", [128, 1], F32)
                dg = [sbt(sa, "dg%d" % i, [128, 4, 128], BF16) for i in range(3)]
                xin = sbt(sa, "xin", [128, D], F32)
                hbf = sbt(sa, "hbf", [128, D], BF16)
                xct = [sbt(sa, "xct%d" % i, [128, 516], BF16) for i in range(2)]
                halo = sbt(sa, "halo", [128, 12, 4], BF16)
                sil = [sbt(sa, "sil%d" % i, [128, 512], F32) for i in range(4)]
                vT = sbt(sa, "vT", [128, 4, 512], BF16)
                sqs = [sbt(sa, "sq%d" % i, [128, 512], BF16) for i in range(2)]
                rt = [sbt(sa, "rt%d" % i, [128, 512], F32) for i in range(2)]
                qkn = sbt(sa, "qkn", [128, 8, 512], BF16)
                kvraws = [sbt(sa, "kvraw%d" % i, [128, 256], F32) for i in range(2)]
                qraws = [sbt(sa, "qraw%d" % i, [128, 512], F32) for i in range(2)]
                Fb = [sbt(sa, "F%d" % i, [128, 4, 128], F32) for i in range(4)]
                gstage = Fb[0][0:8, 0, :]
                cstage = Fb[1][0:48, 0, :]
                SBt = sbt(sa, "SBt", [128, 4, 128], F32)
                Gs = [sbt(sa, "G%d" % i, [128, 4, 128], F32) for i in range(4)]
                EGBs = [sbt(sa, "EGB%d" % i, [128, 4, 128], F32) for i in range(2)]
                Xc = [sbt(sa, "Xc%d" % i, [128, 4, 128], F32) for i in range(2)]
                Yc = [sbt(sa, "Yc%d" % i, [128, 4, 128], F32) for i in range(2)]
                Pm = sbt(sa, "Pm", [128, 4, 128], F32)
                zss = [sbt(sa, "zs%d" % i, [128, 4, 128], F32) for i in range(2)]
                KgTs = [sbt(sa, "KgT%d" % i, [128, 4, 128], BF16) for i in range(2)]
                qgTs = [sbt(sa, "qgT%d" % i, [128, 4, 128], BF16) for i in range(2)]
                QKTs = [sbt(sa, "QKT%d" % i, [128, 4, 128], BF16) for i in range(2)]
                TTbs = [sbt(sa, "TTb%d" % i, [128, 4, 128], BF16) for i in range(2)]
                kvtoks = [sbt(sa, "kvtok%d" % i, [128, 8, 128], BF16) for i in range(2)]
                Rb = sbt(sa, "Rb", [128, 4, 128], BF16)
                vd = sbt(sa, "vd", [128, 4, 128], BF16)
                vn = sbt(sa, "vn", [128, 4, 128], BF16)
                S = sbt(sa, "S", [128, 4, 128], F32)
                Sbf = sbt(sa, "Sbf", [128, 4, 128], BF16)
                yA = sbt(sa, "yA", [128, 512], BF16)
                qn1 = sbt(sa, "qn0", [128, 512], BF16)
                qns = [qn1, qn1]
                kt = sbt(sa, "kt", [128, 128], F32)
                kpad1 = sbt(sa, "kpad0", [128, 2, 2, 128], BF16)
                kpads = [kpad1, kpad1]
                V1 = [sbt(sa, "V1_%d" % i, [128, 2, 66], BF16) for i in range(3)]
                kTp = [sbt(sa, "kTp%d" % i, [128, 4, 128], BF16) for i in range(2)]
                qTp = sbt(sa, "qTp", [128, 4, 128], BF16)
                pT = [sbt(sa, "pT%d" % i, [128, 512], BF16) for i in range(2)]
                yB = sbt(sa, "yB", [128, 512], BF16)

                def colt(name, n):
                    return sbt(sa, name, [128, 16], F32)[:, 0:n]
                c_ss, c_rt, c_rstd = colt("c_ss", 1), colt("c_rt", 1), colt("c_rstd", 1)
                betas = [colt("beta0", 4), colt("beta1", 4)]
                gcol = colt("gcol", 4)
                xa = colt("xa", 4)
                ngc = colt("ngc", 4)
                dk = colt("dk", 4)
                bdms = [[colt("bdm0_0", 4), colt("bdm1_0", 4)], [colt("bdm0_1", 4), colt("bdm1_1", 4)]]
                oss = colt("oss", 4)
                ors = colt("ors", 4)
                qss = colt("qss", 8)
                qrs = colt("qrs", 8)
                kss = colt("kss", 2)
                krs = colt("krs", 2)
                den = colt("den", 4)
                if os.environ.get("K_DBG"):
                    print("SBUF remaining phase A", nc.sbuf_bytes_remaining)

                P.add("sp", DMA(identf[:], ident_d), w=["identf"], dma="cs0", group=True)
                P.add("sp", DMA(Uc[:], U_d), w=["Uc"], dma="cs0", group=True)
                P.add("sp", DMA(Urev[:], Urev_d), w=["Urev"], dma="cs0", group=True)
                P.add("sp", DMA(NEGM[:], NEG_d), w=["NEGM"], dma="cs0", group=True)
                P.add("sp", DMA(STR[:], STR_d), w=["STR"], dma="cs0", group=True)
                P.add("sp", DMA(m01[:], m01_d), w=["m01"], dma="cs0", group=True)
                P.add("sp", DMA(pm[:], pm_d), w=["pm"], dma="cs0", group=True)
                P.add("sp", DMA(gstage, attn_norm_d.rearrange("o (kc p) -> (o kc) p", p=128)), w=["gstage", "F0"], dma="cs0", group=True)
                P.add("sp", DMA(cstage, conv_d.rearrange("j (c p) -> (j c) p", p=128)), w=["cstage", "F1"], dma="cs0", group=True)
                P.add("pe", TR(B[6][:, 0:8], gstage, identf[0:8, 0:8]), r=["gstage", "identf"], pw=["B6"])
                P.add("pe", TR(B[6][:, 64:112], cstage, identf[0:48, 0:48]), r=["cstage", "identf"], pw=["B6"])
                P.add("act", ACT(gainT[:], B[6][:, 0:8], AF.Copy), r=["B6"], w=["gainT"])
                P.add("act", ACT(wconv[:].rearrange("p j c -> p (j c)"), B[6][:, 64:112], AF.Copy), r=["B6"], w=["wconv"])
                P.add("sp", DMA(alog[:], alog_d[0:1, :].partition_broadcast(128)), w=["alog"], dma="cs0", group=True)
                P.add("sp", DMA(dtb[:], dtb_d[0:1, :].partition_broadcast(128)), w=["dtb"], dma="cs0", group=True)
                P.add("sp", DMA(onorm[:], onorm_d[0:1, :].partition_broadcast(128)), w=["onorm"], dma="cs0", group=True)
                P.add("sp", DMA(qg[:], qg_d[0:1, :].partition_broadcast(128)), w=["qg"], dma="cs1", group=True)
                P.add("sp", DMA(kg[:], kg_d[0:1, :].partition_broadcast(128)), w=["kg"], dma="cs1", group=True)
                P.add("sp", DMA(sinks[:], sinks_d[0:1, :].partition_broadcast(128)), w=["sinks"], dma="cs1", group=True)
                for i_, (bd_, b8_, b8n_) in enumerate(((biasP_d, biasP8, "biasP8"), (biasC_d, biasC8, "biasC8"))):
                    for hf_ in range(2):
                        g_ = Gs[i_ * 2 + hf_]
                        gn_ = "G%d" % (i_ * 2 + hf_)
                        P.add("sp", DMA(g_[:].rearrange("p h d -> p (h d)"), bd_[:, hf_ * 512:(hf_ + 1) * 512]), w=[gn_], dma="cs1", group=True)
                        P.add("dve", TS(b8_[:, hf_ * 512:(hf_ + 1) * 512], g_[:].rearrange("p h d -> p (h d)"), 8.0, ALU.mult),
                              r=[gn_], pw=[b8n_])
                w_in_v = w_in_d.rearrange("(kc p) n -> p kc n", p=128)
                for i, c0 in enumerate(range(0, NWT, 512)):
                    c1 = min(c0 + 512, NWT)
                    P.add("pool", DMA(winT[:, :, c0:c1], w_in_v[:, :, 1536 + c0:1536 + c1]), pw=["winT"], dma="w%d" % i)
                P.add("dve", CP(identb[:], identf[:]), r=["identf"], w=["identb"])
                P.add("dve", MSET(onesb[:], 1.0), w=["onesb"])
                P.add("act", ACT(negA[:], alog[:], AF.Exp), r=["alog"], w=["negA"])
                P.add("dve", TS(negA[:], negA[:], -1.0, ALU.mult), r=["negA"], w=["negA"])
                P.add("act", ACT(esink[:], sinks[:], AF.Exp), r=["sinks"], w=["esink"])
                P.add("pool", MSET(kpad1[:], 0.0), w=["kpad0"])
                for i in range(2):
                    P.add("pool", MSET(kTp[i][:], 0.0), w=["kTp%d" % i])
                for i in range(3):
                    P.add("pool", MSET(V1[i][:], 1.0), w=["V1_%d" % i])
                P.add("pool", MSET(S[:], 0.0), w=["S"])
                P.add("pool", MSET(Sbf[:], 0.0), w=["Sbf"])
                P.add("pool", MSET(Rb[:], 0.0), w=["Rb"])
                P.add("pool", MSET(vd[:], 0.0), w=["vd"])
                P.add("pool", MSET(vn[:], 0.0), w=["vn"])
                P.add("pool", MSET(halo[:], 0.0), w=["halo"])

                NEG4 = NEGM[:].unsqueeze(1).to_broadcast([128, 4, 128])
                STR4 = STR[:].unsqueeze(1).to_broadcast([128, 4, 128])
                ID4 = identf[:].unsqueeze(1).to_broadcast([128, 4, 128])
                gainO4 = onorm[:].unsqueeze(1).to_broadcast([128, 4, 128])
                gainQ8 = qg[:].unsqueeze(1).to_broadcast([128, 8, 64])
                gainK2 = kg[:].unsqueeze(1).to_broadcast([128, 2, 64])
                GT8 = gainT[:].unsqueeze(2).to_broadcast([128, 8, 128])
                NSB = int(os.environ.get('K_NSB', NBLK // 4))
                b4 = lambda col: col.unsqueeze(2).to_broadcast([128, 4, 128])

                class Thr:
                    def __init__(self):
                        self.q = []

                    def add(self, eng, fn, glue=False, **kw):
                        self.q.append((eng, fn, kw, glue))

                def take(q, n):
                    k_ = 0
                    while q and (k_ < n or q[0][3]):
                        e_, f_, kw_, _g = q.pop(0)
                        P.add(e_, f_, **kw_)
                        k_ += 1

                def groups(q):
                    out = []
                    for op in q:
                        if out and (op[3] or op[0] == out[-1][-1][0]):
                            out[-1].append(op)
                        else:
                            out.append([op])
                    return out

                def merge(*thrs):
                    gs = [groups(t.q) for t in thrs]
                    for t in thrs:
                        t.q = []
                    idx = [0] * len(gs)
                    while True:
                        best, bestf = -1, 2.0
                        for i_, g_ in enumerate(gs):
                            if idx[i_] < len(g_):
                                f_ = idx[i_] / float(len(g_))
                                if f_ < bestf:
                                    best, bestf = i_, f_
                        if best < 0:
                            break
                        for e_, f_, kw_, _g in gs[best][idx[best]]:
                            P.add(e_, f_, **kw_)
                        idx[best] += 1

                def hT_of(sbk):
                    if sbk >= NPRE // 4:
                        r_ = sbk - NPRE // 4
                    else:
                        r_ = sbk % 2
                    return hTm[:, :, r_ * 512:(r_ + 1) * 512], "hTr%d" % r_

                def stage_X(T, sbk, js=(0, 1, 2, 3)):
                    hT, hTn = hT_of(sbk)
                    for j in js:
                        b = sbk * 4 + j
                        jsl = slice(j * 128, (j + 1) * 128)
                        T.add("sp", DMA(xin[:], x_d[b * 128:(b + 1) * 128, :]), w=["xin"], dma="xin")
                        T.add("act", ACT(hbf[:], xin[:], AF.Square, accum_out=c_ss), r=["xin"], w=["hbf", "c_ss"])
                        T.add("act", ACT(c_rt, c_ss, AF.Ln, scale=1.0 / D, bias=EPS), r=["c_ss"], w=["c_rt"])
                        T.add("act", ACT(c_rstd, c_rt, AF.Exp, scale=-0.5), r=["c_rt"], w=["c_rstd"])
                        T.add("act", ACT(hbf[:], xin[:], AF.Copy, scale=c_rstd), r=["xin", "c_rstd"], w=["hbf"])
                        for kc in range(8):
                            T.add("pe", TR(BT[:, kc * 128:(kc + 1) * 128], hbf[:, kc * 128:(kc + 1) * 128], identb[:]),
                                  glue=(kc > 0), r=["hbf", "identb"], pw=["BT"])
                        T.add("dve", TT(hT[:, :, jsl], v4(BT[:], 8), GT8, ALU.mult), glue=True, r=["BT", "gainT"], pw=[hTn])

                fcnt = [0]

                def stage_F(T, sbk):
                    hT, hTn = hT_of(sbk)
                    own_sb = sbk >= NPRE // 4
                    clist = list(range(12) if (own_sb or sbk == NPRE // 4 - 1) else range(4, 12))
                    cis = []
                    for c in clist:
                        cis.append(fcnt[0])
                        fcnt[0] += 1

                    def st0(c, ci):
                        p2, p3 = ci % 2, ci % 3
                        wq, wqn = wqs[p3], "wqs%d" % p3
                        T.add("pool", DMA(wq[:], w_in_v[:, :, c * 128:(c + 1) * 128]), w=[wqn], dma=wqn)
                        dgi, dgn = dg[p3], "dg%d" % p3
                        for j in range(4):
                            T.add("dve", TS(dgi[:, j, :], identf[:], wconv[:, j, c:c + 1], ALU.mult),
                                  r=["identf", "wconv"], pw=[dgn])
                        pf, pfn = B[2 + p2], "B%d" % (2 + p2)
                        for kc in range(8):
                            T.add("pe", MM(pf[:], wq[:, kc, :], hT[:, kc, :], kc == 0, kc == 7), r=[wqn, hTn], pw=[pfn])

                    def st1(c, ci):
                        p2 = ci % 2
                        pf, pfn = B[2 + p2], "B%d" % (2 + p2)
                        xt, xtn = xct[p2], "xct%d" % p2
                        if ci % 3 == 2:
                            T.add("dve", CP(xt[:, 3:515], pf[:]), r=[pfn], pw=[xtn])
                        else:
                            T.add("act", ACT(xt[:, 3:515], pf[:], AF.Copy), r=[pfn], pw=[xtn])
                        T.add("dve", CP(xt[:, 0:3], halo[:, c, 0:3]), r=["halo", "halo%d" % c], pw=[xtn])

                    def st2(c, ci):
                        p2, p3 = ci % 2, ci % 3
                        dgi, dgn = dg[p3], "dg%d" % p3
                        xt, xtn = xct[p2], "xct%d" % p2
                        pc, pcn = B[4 + p2], "B%d" % (4 + p2)
                        for j in range(4):
                            T.add("pe", MM(pc[:], dgi[:, j, :], xt[:, j:j + 512], j == 0, j == 3), r=[dgn, xtn], pw=[pcn])
                        T.add("dve", CP(halo[:, c, 0:3], xt[:, 512:515]), r=[xtn], w=["halo%d" % c])

                    def st3(c, ci):
                        p2, p4 = ci % 2, ci % 4
                        pc, pcn = B[4 + p2], "B%d" % (4 + p2)
                        sl, sn = sil[p4], "sil%d" % p4
                        T.add("act", ACT(sl[:], pc[:], AF.Exp, scale=-1.0), r=[pcn], w=[sn])
                        T.add("act", ACT(sl[:], sl[:], AF.Ln, bias=1.0), r=[sn], w=[sn])
                        T.add("act", ACT(sl[:], sl[:], AF.Exp, scale=-1.0), r=[sn], w=[sn])

                    def st4(c, ci):
                        p2, p4 = ci % 2, ci % 4
                        pc, pcn = B[4 + p2], "B%d" % (4 + p2)
                        sl, sn = sil[p4], "sil%d" % p4
                        if c >= 8:
                            T.add("dve", TT(vT[:, c - 8, :], sl[:], pc[:], ALU.mult), r=[sn, pcn], pw=["vT"])
                        else:
                            T.add("dve", TT(sl[:], sl[:], pc[:], ALU.mult), r=[sn, pcn], w=[sn])
                            T.add("dve", TT(sqs[p2][:], sl[:], sl[:], ALU.mult), r=[sn], w=["sq%d" % p2])

                    def st5(c, ci):
                        if c >= 8:
                            return
                        p2 = ci % 2
                        T.add("pe", MM(B[6][:], onesb[:], sqs[p2][:]), r=["onesb", "sq%d" % p2], pw=["B6"])
                        rti, rtn = rt[p2], "rt%d" % p2
                        T.add("act", ACT(rti[:], B[6][:], AF.Ln, bias=EPS), r=["B6"], w=[rtn])
                        T.add("act", ACT(rti[:], rti[:], AF.Exp, scale=-0.5), r=[rtn], w=[rtn])

                    def st6(c, ci):
                        if c >= 8:
                            return
                        p2, p4 = ci % 2, ci % 4
                        sl, sn = sil[p4], "sil%d" % p4
                        rti, rtn = rt[p2], "rt%d" % p2
                        scl = (128.0 ** -0.5) if c < 4 else 1.0
                        T.add("dve", STT(qkn[:, c, :], sl[:], scl, rti[:], ALU.mult, ALU.mult), r=[sn, rtn], pw=["qkn"])

                    stages = (st0, st1, st2, st3, st4, st5, st6)
                    n = len(clist)
                    for i in range(n + len(stages) - 1):
                        for k in range(len(stages) - 1, -1, -1):
                            if 0 <= i - k < n:
                                stages[k](clist[i - k], cis[i - k])

                def stage_T1(T, sbk, j):
                    b = sbk * 4 + j
                    own = b >= NPRE
                    par = b % 2
                    jsl = slice(j * 128, (j + 1) * 128)
                    hTf, hTn = hT_of(sbk)
                    hT = hTf[:, :, jsl]
                    has_kv = own or b == NPRE - 1
                    beta, bdm = betas[par], bdms[par]
                    zs, KgT, qgT, QKT, TTb, EGB = zss[par], KgTs[par], qgTs[par], QKTs[par], TTbs[par], EGBs[par]
                    zsn, KgTn, qgTn, QKTn, TTbn, EGBn = ["%s%d" % (n_, par) for n_ in ("zs", "KgT", "qgT", "QKT", "TTb", "EGB")]
                    betan = "beta%d" % par
                    qn, qnn = qns[par], "qn0"
                    kpad, kpadn = kpads[par], "kpad0"
                    kvtok, kvtokn = kvtoks[par], "kvtok%d" % par
                    v1i = b % 3
                    side = []

                    def SD(eng, fn, **kw):
                        side.append((eng, fn, kw))

                    def drain(n):
                        for _ in range(min(n, len(side))):
                            e_, f_, kw_ = side.pop(0)
                            T.add(e_, f_, **kw_)
                    for kc in range(8):
                        T.add("pe", MM(B[2][:, 256:264], hT[:, kc, :], winT[:, kc, CBA:CBA + 8], kc == 0, kc == 7),
                              r=["winT", hTn], pw=["B2"])
                    if has_kv:
                        for kc in range(8):
                            T.add("pe", MM(B[2][:, 0:256], hT[:, kc, :], winT[:, kc, CKV:CKV + 256], kc == 0, kc == 7),
                                  r=["winT", hTn], pw=["B2"])
                    if own:
                        for kc in range(8):
                            T.add("pe", MM(B[3][:], hT[:, kc, :], winT[:, kc, CQ:CQ + 512], kc == 0, kc == 7),
                                  r=["winT", hTn], pw=["B3"])
                        for kc in range(8):
                            T.add("pe", MM(B[4][:], hT[:, kc, :], winT[:, kc, CZ:CZ + 512], kc == 0, kc == 7),
                                  r=["winT", hTn], pw=["B4"])
                    T.add("act", ACT(beta, B[2][:, 256:260], AF.Exp, scale=-1.0), r=["B2"], w=[betan])
                    T.add("act", ACT(beta, beta, AF.Ln, bias=1.0), r=[betan], w=[betan])
                    T.add("act", ACT(beta, beta, AF.Exp, scale=-1.0), r=[betan], w=[betan])
                    T.add("dve", TT(xa, B[2][:, 260:264], dtb[:], ALU.add), r=["B2", "dtb"], w=["xa"])
                    T.add("act", ACT(xa, xa, AF.Exp), r=["xa"], w=["xa"])
                    T.add("act", ACT(xa, xa, AF.Ln, bias=1.0), r=["xa"], w=["xa"])
                    T.add("dve", TT(gcol, xa, negA[:], ALU.mult), r=["xa", "negA"], w=["gcol"])
                    T.add("dve", TT(SBt[:], STR4, b4(beta), ALU.mult), r=["STR", betan], w=["SBt"])

                    Gb, gm, ET, EsT = Fb[0], Fb[1], Fb[2], Fb[3]
                    T.add("dve", CP(Gb[:], b4(gcol)), r=["gcol"], w=["F0"])
                    for h in range(4):
                        T.add("pe", MM(B[5][:, h * 128:(h + 1) * 128], Gb[:, h, :], Uc[:]), r=["F0", "Uc"], pw=["B5"])
                    T.add("pe", MM(B[6][:, 0:4], Uc[:], gcol), r=["Uc", "gcol"], pw=["B6"])
                    T.add("pe", MM(B[6][:, 4:8], Urev[:], gcol), r=["Urev", "gcol"], pw=["B6"])
                    T.add("act", ACT(EGB[:], v4(B[5][:]), AF.Exp), r=["B5"], w=[EGBn])
                    T.add("dve", TS(ngc, B[6][:, 0:4], -1.0, ALU.mult), r=["B6"], w=["ngc"])
                    T.add("act", ACT(dk, B[6][:, 4:8], AF.Exp), r=["B6"], w=["dk"])
                    T.add("dve", TT(gm[:], v4(B[5][:]), NEG4, ALU.add), r=["B5", "NEGM"], w=["F1"])
                    T.add("dve", TT(gm[:], gm[:], b4(ngc), ALU.add), r=["F1", "ngc"], w=["F1"])
                    T.add("act", ACT(ET[:], gm[:], AF.Exp), r=["F1"], w=["F2"])
                    if own:
                        T.add("act", ACT(zs[:], v4(B[4][:]), AF.Copy), r=["B4"], w=[zsn])
                        T.add("act", ACT(qraws[par][:], B[3][:], AF.Copy), r=["B3"], w=["qraw%d" % par])
                    if has_kv:
                        T.add("act", ACT(kvraws[par][:], B[2][:, 0:256], AF.Copy), r=["B2"], w=["kvraw%d" % par])
                    T.add("dve", TT(EsT[:], ET[:], SBt[:], ALU.mult), r=["F2", "SBt"], w=["F3"])
                    T.add("dve", TT(KgT[:], qkn[:, 4:8, jsl], EGB[:], ALU.mult), r=["qkn", EGBn], w=[KgTn])
                    if own:
                        T.add("dve", TT(qgT[:], qkn[:, 0:4, jsl], EGB[:], ALU.mult), r=["qkn", EGBn], w=[qgTn])
                    for h in range(4):
                        T.add("pe", MM(B[6][:, h * 128:(h + 1) * 128], qkn[:, 4 + h, jsl], qkn[:, 4 + h, jsl]),
                              r=["qkn"], pw=["B6"])
                    if own:
                        for h in range(4):
                            T.add("pe", MM(B[3][:, h * 128:(h + 1) * 128], qkn[:, 4 + h, jsl], qkn[:, h, jsl]),
                                  r=["qkn"], pw=["B3"])
                    X0, Y0 = Xc[0], Yc[0]
                    T.add("dve", TT(X0[:], v4(B[6][:]), EsT[:], ALU.mult), r=["B6", "F3"], w=["Xc0"])
                    if own:
                        T.add("dve", TT(QKT[:], v4(B[3][:]), ET[:], ALU.mult), r=["B3", "F2"], w=[QKTn])
                    for c in range(2):
                        T.add("dve", STT(bdm[c], dk, m01[:, c:c + 1], beta, ALU.mult, ALU.mult),
                              r=["dk", "m01", betan], w=["bdm%d_%d" % (c, par)])
                    for h in range(4):
                        T.add("pe", TR(B[6][:, h * 128:(h + 1) * 128], X0[:, h, :], identf[:]), r=["Xc0", "identf"], pw=["B6"])
                    T.add("act", ACT(Y0[:], v4(B[6][:]), AF.Copy), r=["B6"], w=["Yc0"])
                    T.add("dve", TT(Pm[:], ID4, X0[:], ALU.subtract), r=["identf", "Xc0"], w=["Pm"])
                    for h in range(4):
                        T.add("pe", TR(BT[:, h * 128:(h + 1) * 128], qkn[:, 4 + h, jsl], identb[:]), glue=(h > 0),
                              r=["qkn", "identb"], pw=["BT"])
                    for h in range(4):
                        T.add("pe", TR(BT[:, 512 + h * 128:512 + (h + 1) * 128], vT[:, h, jsl], identb[:]), glue=True,
                              r=["vT", "identb"], pw=["BT"])
                    T.add("act", ACT(kvtok[:], v4(BT[:], 8), AF.Copy), glue=True, r=["BT"], w=[kvtokn])
                    def emit_P(k):
                        Yn_, ynn_ = Yc[(k + 1) % 2], "Yc%d" % ((k + 1) % 2)
                        for h in range(4):
                            T.add("pe", MM(B[2][:, h * 128:(h + 1) * 128], Yn_[:, h, :], Pm[:, h, :]), r=[ynn_, "Pm"], pw=["B2"])
                        if k < 4:
                            T.add("dve", TT(Pm[:], v4(B[2][:]), Pm[:], ALU.add), r=["B2", "Pm"], w=["Pm"])
                        else:
                            T.add("dve", TT(TTb[:], v4(B[2][:]), Pm[:], ALU.add), r=["B2", "Pm"], w=[TTbn])

                    for k in range(5):
                        Xk, Yk = Xc[k % 2], Yc[k % 2]
                        Xn, Yn = Xc[(k + 1) % 2], Yc[(k + 1) % 2]
                        xkn, ykn = "Xc%d" % (k % 2), "Yc%d" % (k % 2)
                        xnn, ynn = "Xc%d" % ((k + 1) % 2), "Yc%d" % ((k + 1) % 2)
                        for h in range(4):
                            T.add("pe", MM(B[6][:, h * 128:(h + 1) * 128], Xk[:, h, :], Yk[:, h, :]), r=[xkn, ykn], pw=["B6"])
                        if k < 4:
                            for h in range(4):
                                T.add("pe", MM(B[5][:, h * 128:(h + 1) * 128], Yk[:, h, :], Xk[:, h, :]), r=[xkn, ykn], pw=["B5"])
                        T.add("dve", CP(Yn[:], v4(B[6][:])), r=["B6"], w=[ynn])
                        if k < 4:
                            T.add("act", ACT(Xn[:], v4(B[5][:]), AF.Copy), r=["B5"], w=[xnn])
                        if k > 0:
                            emit_P(k - 1)
                        drain(3)
                    emit_P(4)
                    drain(len(side))

                def stage_T2(T, sbk, j):
                    b = sbk * 4 + j
                    own = b >= NPRE
                    ob = b - NPRE
                    par = b % 2
                    tsl = slice(ob * 128, (ob + 1) * 128)
                    has_kv = own or b == NPRE - 1
                    beta, bdm = betas[par], bdms[par]
                    zs, KgT, qgT, QKT, TTb, EGB = zss[par], KgTs[par], qgTs[par], QKTs[par], TTbs[par], EGBs[par]
                    zsn, KgTn, qgTn, QKTn, TTbn, EGBn = ["%s%d" % (n_, par) for n_ in ("zs", "KgT", "qgT", "QKT", "TTb", "EGB")]
                    betan = "beta%d" % par
                    qn, qnn = qns[par], "qn0"
                    kpad, kpadn = kpads[par], "kpad0"
                    kvtok, kvtokn = kvtoks[par], "kvtok%d" % par
                    osb = Gs[0]
                    if own:
                        T.add("act", ACT(Gs[2][:], zs[:], AF.Exp, scale=-1.0), r=[zsn], w=["G2"])
                        T.add("act", ACT(Gs[2][:], Gs[2][:], AF.Ln, bias=1.0), r=["G2"], w=["G2"])
                        T.add("act", ACT(Gs[2][:], Gs[2][:], AF.Exp, scale=-1.0), r=["G2"], w=["G2"])
                        T.add("pool", TT(Gs[2][:], Gs[2][:], zs[:], ALU.mult), r=["G2", zsn], w=["G2"])
                        T.add("pool", TT(Gs[2][:], Gs[2][:], gainO4, ALU.mult), r=["G2", "onorm"], w=["G2"])
                    for c in range(2):
                        for h in range(4):
                            T.add("pe", MM(B[1][:, h * 128:(h + 1) * 128], KgT[:, h, :], Sbf[:, h, :]), r=[KgTn, "Sbf"], pw=["B1"])
                        T.add("dve", TT(Rb[:], kvtok[:, 4:8, :], v4(B[1][:]), ALU.subtract), r=[kvtokn, "B1"], w=["Rb"])
                        for h in range(4):
                            T.add("pe", MM(B[1][:, h * 128:(h + 1) * 128], TTb[:, h, :], Rb[:, h, :]), r=[TTbn, "Rb"], pw=["B1"])
                        T.add("dve", TT(vd[:], v4(B[1][:]), b4(bdm[c]), ALU.mult), r=["B1", "bdm%d_%d" % (c, par)], w=["vd"])
                        if own:
                            T.add("dve", TT(vn[:], v4(B[1][:]), b4(beta), ALU.mult), r=["B1", betan], w=["vn"])
                            for h in range(4):
                                T.add("pe", MM(B[1][:, h * 128:(h + 1) * 128], qgT[:, h, :], Sbf[:, h, :], True, False),
                                      r=[qgTn, "Sbf"], pw=["B1"])
                                T.add("pe", MM(B[1][:, h * 128:(h + 1) * 128], QKT[:, h, :], vn[:, h, :], False, True),
                                      glue=True, r=[QKTn, "vn"], pw=["B1"])
                            rs = slice(c * 64, (c + 1) * 64)
                            T.add("act", ACT(osb[rs].rearrange("p h d -> p (h d)"), B[1][rs, :], AF.Copy), r=["B1"], pw=["G0"])
                        for h in range(4):
                            T.add("pe", MM(B[1][:, h * 128:(h + 1) * 128], kvtok[:, h, :], vd[:, h, :]), r=[kvtokn, "vd"], pw=["B1"])
                        col = 63 + 64 * c
                        T.add("dve", TT(S[:], S[:], EGB[:, :, col:col + 1].to_broadcast([128, 4, 128]), ALU.mult),
                              r=["S", EGBn], w=["S"])
                        T.add("dve", TT(S[:], S[:], v4(B[1][:]), ALU.add), r=["S", "B1"], w=["S"])
                        T.add("act", ACT(Sbf[:], S[:], AF.Copy), r=["S"], w=["Sbf"])
                    if own:
                        T.add("act", ACT(Gs[1][:], osb[:], AF.Square), r=["G0"], w=["G1"])
                        T.add("dve", RED(oss, Gs[1][:]), r=["G1"], w=["oss"])
                        T.add("act", ACT(oss, oss, AF.Ln, scale=1.0 / 128, bias=EPS), r=["oss"], w=["oss"])
                        T.add("act", ACT(ors, oss, AF.Exp, scale=-0.5), r=["oss"], w=["ors"])
                        T.add("dve", TT(Gs[3][:], osb[:], b4(ors), ALU.mult), r=["G0", "ors"], w=["G3"])
                        T.add("dve", TT(v4(yA[:]), Gs[3][:], Gs[2][:], ALU.mult), r=["G3", "G2"], w=["yA"])
                        for p_ in range(4):
                            T.add("pe", TR(BT[:, p_ * 128:(p_ + 1) * 128], yA[:, p_ * 128:(p_ + 1) * 128], identb[:]), glue=(p_ > 0),
                                  r=["yA", "identb"], pw=["BT"])
                        T.add("act", ACT(yAT[:, :, tsl], v4(BT[:, 0:512]), AF.Copy), glue=True, r=["BT"], w=["yAT%d" % ob])

                def stage_T2b(T, sbk, j):
                    b = sbk * 4 + j
                    own = b >= NPRE
                    ob = b - NPRE
                    par = b % 2
                    tsl = slice(ob * 128, (ob + 1) * 128)
                    has_kv = own or b == NPRE - 1
                    if not has_kv:
                        return
                    qn, qnn = qns[par], "qn0"
                    kpad, kpadn = kpads[par], "kpad0"
                    kcur, kprev = b % 2, (b - 1) % 2
                    v1c, v1p = b % 3, (b - 1) % 3
                    kvr, kvrn = kvraws[par], "kvraw%d" % par
                    k2 = lambda ap: ap.rearrange("p (h d) -> p h d", h=2)
                    for kv_ in range(2):
                        T.add("act", ACT(pT[0][:, kv_ * 64:(kv_ + 1) * 64], kvr[:, kv_ * 64:(kv_ + 1) * 64], AF.Square,
                                         accum_out=kss[:, kv_:kv_ + 1]), r=[kvrn], pw=["pT0", "kss"])
                    T.add("act", ACT(kss, kss, AF.Ln, scale=1.0 / 64, bias=EPS), r=["kss"], w=["kss"])
                    T.add("act", ACT(krs, kss, AF.Exp, scale=-0.5), r=["kss"], w=["krs"])
                    T.add("pool", TT(k2(kt[:]), k2(kvr[:, 0:128]), krs.unsqueeze(2).to_broadcast([128, 2, 64]), ALU.mult),
                          r=[kvrn, "krs"], w=["kt"])
                    T.add("pool", TT(kpad[:, :, 0, 0:64], k2(kt[:]), gainK2, ALU.mult), r=["kt", "kg"], pw=[kpadn])
                    T.add("pool", TT(kpad[:, :, 1, 64:128], k2(kt[:]), gainK2, ALU.mult), r=["kt", "kg"], pw=[kpadn])
                    T.add("act", ACT(V1[v1c][:, :, 0:64], k2(kvr[:, 128:256]), AF.Copy), r=[kvrn], pw=["V1_%d" % v1c])
                    if own:
                        qr, qrn = qraws[par], "qraw%d" % par
                        q8 = lambda ap: ap.rearrange("p (h d) -> p h d", h=8)
                        for h_ in range(8):
                            T.add("act", ACT(pT[0][:, h_ * 64:(h_ + 1) * 64], qr[:, h_ * 64:(h_ + 1) * 64], AF.Square,
                                             accum_out=qss[:, h_:h_ + 1]), r=[qrn], pw=["pT0", "qss"])
                        T.add("act", ACT(qss, qss, AF.Ln, scale=1.0 / 64, bias=EPS), r=["qss"], w=["qss"])
                        T.add("act", ACT(qrs, qss, AF.Exp, scale=-0.5), r=["qss"], w=["qrs"])
                        T.add("pool", TT(q8(qr[:]), q8(qr[:]), qrs.unsqueeze(2).to_broadcast([128, 8, 64]), ALU.mult),
                              r=[qrn, "qrs"], w=[qrn])
                        T.add("pool", TT(q8(qn[:]), q8(qr[:]), gainQ8, ALU.mult), r=[qrn, "qg"], w=[qnn])
                    first = True
                    for g in range(2):
                        for var in range(2):
                            i = g * 2 + var
                            T.add("pe", TR(BT[:, i * 128:(i + 1) * 128], kpad[:, g, var, :], identb[:]), glue=(not first),
                                  r=[kpadn, "identb"], pw=["BT"])
                            first = False
                    if not own:
                        T.add("act", ACT(kTp[kcur][:], v4(BT[:, 0:512]), AF.Copy), glue=True, r=["BT"], w=["kTp%d" % kcur])
                        return
                    for p_ in range(4):
                        T.add("pe", TR(BT[:, 512 + p_ * 128:512 + (p_ + 1) * 128], qn[:, p_ * 128:(p_ + 1) * 128], identb[:]), glue=True,
                              r=[qnn, "identb"], pw=["BT"])
                    T.add("act", ACT(kTp[kcur][:], v4(BT[:, 0:512]), AF.Copy), glue=True, r=["BT"], w=["kTp%d" % kcur])
                    T.add("act", ACT(qTp[:], v4(BT[:, 512:1024]), AF.Copy), glue=True, r=["BT"], w=["qTp"])
                    for g in range(2):
                        for (kk, b8, b8n, pTi, pTn, is_prev) in ((kprev, biasP8, "biasP8", pT[0], "pT0", True),
                                                               (kcur, biasC8, "biasC8", pT[1], "pT1", False)):
                            for j_ in range(4):
                                h = 4 * g + j_
                                T.add("pe", MM(B[0][:, j_ * 128:(j_ + 1) * 128], kTp[kk][:, g * 2 + h % 2, :], qTp[:, h // 2, :],
                                               j_ == 0, False, skip=True), r=["kTp%d" % kk, "qTp"], pw=["B0"])
                            T.add("pe", MM(B[0][:], identb[:], b8[:, g * 512:(g + 1) * 512], False, True, skip=True),
                                  r=["identb", b8n], pw=["B0"])
                            if is_prev and ob == 0:
                                T.add("act", ACT(pTi[:], B[0][:], AF.Exp, scale=0.125, bias=pm[:, 0:1]), r=["B0", "pm"], w=[pTn])
                            else:
                                T.add("act", ACT(pTi[:], B[0][:], AF.Exp, scale=0.125), r=["B0"], w=[pTn])
                        for j_ in range(4):
                            T.add("pe", MM(B[0][:, j_ * 65:(j_ + 1) * 65], pT[0][:, j_ * 128:(j_ + 1) * 128], V1[v1p][:, g, 0:65], True, False),
                                  r=["pT0", "V1_%d" % v1p], pw=["B0"])
                            T.add("pe", MM(B[0][:, j_ * 65:(j_ + 1) * 65], pT[1][:, j_ * 128:(j_ + 1) * 128], V1[v1c][:, g, 0:65], False, True),
                                  glue=True, r=["pT1", "V1_%d" % v1c], pw=["B0"])
                        pvv = B[0][:, 0:260].rearrange("p (h d) -> p h d", h=4)
                        T.add("dve", TT(den.unsqueeze(2), pvv[:, :, 64:65], esink[:, 4 * g:4 * g + 4].unsqueeze(2), ALU.add),
                              r=["B0", "esink"], w=["den"])
                        T.add("dve", RC(den, den), r=["den"], w=["den"])
                        T.add("dve", TT(yB[:, g * 256:(g + 1) * 256].rearrange("p (h d) -> p h d", h=4), pvv[:, :, 0:64],
                                        den.unsqueeze(2).to_broadcast([128, 4, 64]), ALU.mult), r=["B0", "den"], pw=["yB"])
                    for p_ in range(4):
                        T.add("pe", TR(BT[:, p_ * 128:(p_ + 1) * 128], yB[:, p_ * 128:(p_ + 1) * 128], identb[:]), glue=(p_ > 0),
                              r=["yB", "identb"], pw=["BT"])
                    T.add("act", ACT(yBT[:, :, tsl], v4(BT[:, 0:512]), AF.Copy), glue=True, r=["BT"], w=["yBT%d" % ob])

                prev = None
                for sbk in range(NSB):
                    x_in_t2 = (sbk + 1 < NSB)
                    for j in range(4):
                        t1 = Thr()
                        if j == 0:
                            if sbk == 0:
                                stage_X(t1, 0)
                            stage_F(t1, sbk)
                            if (not x_in_t2) and sbk + 1 < NSB:
                                stage_X(t1, sbk + 1)
                        stage_T1(t1, sbk, j)
                        if prev is None:
                            merge(t1)
                        else:
                            merge(t1, prev[0], prev[1])
                        prev = (Thr(), Thr())
                        stage_T2(prev[0], sbk, j)
                        stage_T2b(prev[1], sbk, j)
                        if x_in_t2 and j < 3:
                            stage_X(prev[0], sbk + 1, (j,) if j < 2 else (2, 3))
                if prev is not None:
                    merge(prev[0], prev[1])
                P.barrier()
                P.emit(nc, semf)

            sb1 = contextlib.ExitStack()
            with sb1:
                P = Prog("b")
                wgA = sbt(sb1, "wgA", [128, 8, D], BF16)
                wgB = sbt(sb1, "wgB", [128, 8, D], BF16)
                wA = sbt(sb1, "wA", [128, 4, D], BF16)
                wB = sbt(sb1, "wB", [128, 4, D], BF16)
                sA = sbt(sb1, "sA", [128, 512], F32)
                sB = sbt(sb1, "sB", [128, 512], F32)
                t1 = sbt(sb1, "t1", [128, 512], F32)
                t2 = sbt(sb1, "t2", [128, 512], F32)
                mtiles = [sbt(sb1, "mtile%d" % i, [128, 8, 512], BF16) for i in range(2)]
                w_in_v = w_in_d.rearrange("(kc p) n -> p kc n", p=128)
                w_a_v = w_a_d.rearrange("(kc p) n -> p kc n", p=128)
                w_b_v = w_b_d.rearrange("(kc p) n -> p kc n", p=128)
                pieces = ((0, 128), (128, 512), (512, 1024))
                def piece_of(m):
                    return 0 if m == 0 else (1 if m < 4 else 2)
                for pi, (c0, c1) in enumerate(pieces):
                    cs = slice(c0, c1)
                    P.add("pool", DMA(wgA[:, :, cs], w_in_v[:, :, 2824 + c0:2824 + c1]), w=["wgA%d" % pi], dma="wgA%d" % pi)
                    P.add("pool", DMA(wgB[:, :, cs], w_in_v[:, :, 3848 + c0:3848 + c1]), w=["wgB%d" % pi], dma="wgB%d" % pi)
                    P.add("pool", DMA(wA[:, :, cs], w_a_v[:, :, cs]), w=["wA%d" % pi], dma="wA%d" % pi)
                    P.add("pool", DMA(wB[:, :, cs], w_b_v[:, :, cs]), w=["wB%d" % pi], dma="wB%d" % pi)
                wg_vb = w_gate_d.rearrange("(kc p) n -> p kc n", p=128)
                wu_vb = w_up_d.rearrange("(kc p) n -> p kc n", p=128)
                for fb in range(NF):
                    fsl = slice(fb * 128, (fb + 1) * 128)
                    wcr = wcache[fb * 128:(fb + 1) * 128, :]
                    P.add("pool", DMA(wcr[:, 0:1024].rearrange("p (kc n) -> p kc n", kc=8), wg_vb[:, :, fsl]), dma="wcf")
                    P.add("pool", DMA(wcr[:, 1024:2048].rearrange("p (kc n) -> p kc n", kc=8), wu_vb[:, :, fsl]), dma="wcf")
                wo_vb = w_out_d.rearrange("(kc p) n -> p kc n", p=128)
                wd_vb = w_down_d.rearrange("(f p) n -> p f n", p=128)
                woc3 = wocache.rearrange("p (kc n) -> p kc n", kc=8)
                wdc3 = wdcache.rearrange("p (f n) -> p f n", f=NF)
                for hf in range(2):
                    cs_ = slice(hf * 512, (hf + 1) * 512)
                    P.add("pool", DMA(woc3[:, :, cs_], wo_vb[:, :, cs_]), dma="wcf")
                for hf in range(2):
                    cs_ = slice(hf * 512, (hf + 1) * 512)
                    for (f0, f1) in ((0, 8), (8, 16), (16, 22)):
                        P.add("pool", DMA(wdc3[:, f0:f1, cs_], wd_vb[:, f0:f1, cs_]), dma="wcf")
                it = 0
                for tt_ in range(int(os.environ.get('K_NB1', 4))):
                    ts_ = slice(tt_ * 512, (tt_ + 1) * 512)
                    for m in range(8):
                        ms = slice(m * 128, (m + 1) * 128)
                        hf = piece_of(m)
                        if it % 2 == 0:
                            bgA, bgB, bpA, bpB = B[0], B[1], B[2], B[6]
                            nA, nB, nPA, nPB = "b0", "b1", "b2", "b6"
                        else:
                            bgA, bgB, bpA, bpB = B[3], B[4], B[5], B[6]
                            nA, nB, nPA, nPB = "b3", "b4", "b5", "b6"
                        for kc in range(8):
                            P.add("pe", MM(bgA[:], wgA[:, kc, ms], hTm[:, kc, ts_], kc == 0, kc == 7), r=["wgA%d" % hf, "hTm%d" % tt_], pw=[nA])
                        for kc in range(8):
                            P.add("pe", MM(bgB[:], wgB[:, kc, ms], hTm[:, kc, ts_], kc == 0, kc == 7), r=["wgB%d" % hf, "hTm%d" % tt_], pw=[nB])
                        for kc in range(4):
                            P.add("pe", MM(bpA[:], wA[:, kc, ms], yAT[:, kc, ts_], kc == 0, kc == 3), r=["wA%d" % hf], pw=[nPA])
                        for kc in range(4):
                            P.add("pe", MM(bpB[:], wB[:, kc, ms], yBT[:, kc, ts_], kc == 0, kc == 3), r=["wB%d" % hf], pw=[nPB])
                        P.add("act", ACT(sA[:], bgA[:], AF.Sigmoid), r=[nA], w=["sA"])
                        P.add("act", ACT(sB[:], bgB[:], AF.Sigmoid), r=[nB], w=["sB"])
                        P.add("dve", TT(t1[:], sA[:], bpA[:], ALU.mult), r=["sA", nPA], w=["t1"])
                        P.add("dve", TT(t2[:], sB[:], bpB[:], ALU.mult), r=["sB", nPB], w=["t2"])
                        P.add("dve", TT(mtiles[tt_ % 2][:, m, :], t1[:], t2[:], ALU.add), r=["t1", "t2"], pw=["mtile%d" % (tt_ % 2)])
                        it += 1
                    P.add("act", ACT(hTm[:, :, ts_], mtiles[tt_ % 2][:], AF.Copy), r=["mtile%d" % (tt_ % 2)], w=["hTm%d" % tt_])
                P.barrier()
                P.emit(nc, semf)

        s2 = contextlib.ExitStack()
        with s2:
            P = Prog("c")
            wout = sbt(s2, "wout", [128, 8, D], BF16)
            wdown = sbt(s2, "wdown", [128, NF, D], BF16)
            gainF = sbt(s2, "gainF", [128, D], F32)
            xin2 = [sbt(s2, "xin2_%d" % i, [128, D], F32) for i in range(2)]
            x1 = [sbt(s2, "x1_%d" % i, [128, 4, D], F32) for i in range(2)]
            h2bf = [sbt(s2, "h2bf%d" % i, [128, D], BF16) for i in range(2)]
            h2T = [sbt(s2, "h2T%d" % i, [128, 8, 512], BF16) for i in range(2)]
            actT = sbt(s2, "actT", [128, NF, 512], BF16)
            wgu = [sbt(s2, "wgu%d" % i, [128, 16, 128], BF16) for i in range(3)]
            sg = [sbt(s2, "sg%d" % i, [128, 512], F32) for i in range(2)]
            ost = [sbt(s2, "ost%d" % i, [128, D], F32) for i in range(2)]
            cc = [sbt(s2, "cc%d" % i, [128, 16], F32) for i in range(3)]
            P.add("sp", DMA(gainF[:], ffn_norm_d[0:1, :].partition_broadcast(128)), w=["gainF"], dma="gF")
            wo_v = w_out_d.rearrange("(kc p) n -> p kc n", p=128)
            P.add("sp", DMA(wout[:].rearrange("p a b -> p (a b)"), wocache[:, :]), w=["wout"], dma="wout")
            wd_v = w_down_d.rearrange("(f p) n -> p f n", p=128)
            wg_v = w_gate_d.rearrange("(kc p) n -> p kc n", p=128)
            wu_v = w_up_d.rearrange("(kc p) n -> p kc n", p=128)
            NT2 = int(os.environ.get('K_NB2', 4))

            def x1_partA(t, blk):
                tb = t * 4 + blk
                xi, xn = xin2[tb % 2], "xin2_%d" % (tb % 2)
                x1t, hb = x1[t % 2], h2bf[tb % 2]
                x1n, hbn = "x1_%d_%d" % (t % 2, blk), "h2bf%d" % (tb % 2)
                P.add("sp", DMA(xi[:], x_d[(NPRE + tb) * 128:(NPRE + tb + 1) * 128, :]), w=[xn], dma=xn)
                for half in range(2):
                    hs = slice(half * 512, (half + 1) * 512)
                    for kc in range(8):
                        P.add("pe", MM(B[half][:], hTm[:, kc, tb * 128:(tb + 1) * 128], wout[:, kc, hs], kc == 0, kc == 7),
                              r=["wout"], pw=["b%d" % half])
                    P.add("dve", TT(x1t[:, blk, hs], xi[:, hs], B[half][:], ALU.add), r=[xn, "b%d" % half], pw=[x1n])
                P.add("act", ACT(hb[:], x1t[:, blk, :], AF.Square, accum_out=cc[0][:, 0:1]), r=[x1n], w=[hbn, "cc0"])
                P.add("act", ACT(cc[1][:, 0:1], cc[0][:, 0:1], AF.Sqrt, scale=1.0 / D, bias=EPS), r=["cc0"], w=["cc1"])
                P.add("dve", RC(cc[2][:, 0:1], cc[1][:, 0:1]), r=["cc1"], w=["cc2"])
                P.add("dve", STT(hb[:], x1t[:, blk, :], cc[2][:, 0:1], gainF[:], ALU.mult, ALU.mult),
                      r=[x1n, "cc2", "gainF"], w=[hbn])

            def x1_partB(t, blk):
                tb = t * 4 + blk
                hb, hbn = h2bf[tb % 2], "h2bf%d" % (tb % 2)
                for kc in range(8):
                    P.add("pe", TR(BT[:, kc * 128:(kc + 1) * 128], hb[:, kc * 128:(kc + 1) * 128], identb[:]),
                          r=[hbn], pw=["BT"])
                P.add("act", ACT(h2T[t % 2][:, :, blk * 128:(blk + 1) * 128], v4(BT[:], 8), AF.Copy), r=["BT"], pw=["h2T%d" % (t % 2)])

            gu_it = [0]

            def gu(t, f):
                it = gu_it[0]
                gu_it[0] += 1
                sl_ = it % 3
                wg, wn = wgu[sl_], "wgu%d" % sl_
                fs = slice(f * 128, (f + 1) * 128)
                wgf = wg[:].rearrange("p a b -> p (a b)")
                P.add("sp", DMA(wgf, wcache[f * 128:(f + 1) * 128, :]), w=[wn], dma=wn)
                bg, bu = (B[2], B[3]) if it % 2 == 0 else (B[4], B[5])
                ng, nu = ("b2", "b3") if it % 2 == 0 else ("b4", "b5")
                hT2, hT2n = h2T[t % 2], "h2T%d" % (t % 2)
                for kc in range(8):
                    P.add("pe", MM(bg[:], wg[:, kc, :], hT2[:, kc, :], kc == 0, kc == 7), r=[wn, hT2n], pw=[ng])
                for kc in range(8):
                    P.add("pe", MM(bu[:], wg[:, 8 + kc, :], hT2[:, kc, :], kc == 0, kc == 7), r=[wn, hT2n], pw=[nu])
                sgi = sg[it % 2]
                P.add("act", ACT(sgi[:], bg[:], AF.Silu), r=[ng], w=["sg%d" % (it % 2)])
                P.add("dve", TT(actT[:, f, :], sgi[:], bu[:], ALU.mult), r=["sg%d" % (it % 2), nu], pw=["actT"])

            def down(t, blk):
                tb = t * 4 + blk
                oi, on = ost[tb % 2], "ost%d" % (tb % 2)
                for half in range(2):
                    hs = slice(half * 512, (half + 1) * 512)
                    for f in range(NF):
                        P.add("pe", MM(B[half][:], actT[:, f, blk * 128:(blk + 1) * 128], wdown[:, f, hs], f == 0, f == NF - 1),
                              r=["actT", "wdown"], pw=["b%d" % half])
                    P.add("dve", TT(oi[:, hs], x1[t % 2][:, blk, hs], B[half][:], ALU.add),
                          r=["x1_%d_%d" % (t % 2, blk), "b%d" % half], pw=[on])
                P.add("sp", DMA(out_d[tb * 128:(tb + 1) * 128, :], oi[:]), r=[on], dma=on)

            for blk in range(4):
                x1_partA(0, blk)
                x1_partB(0, blk)
            wdf = wdown[:].rearrange("p a b -> p (a b)")
            for i_, (c0_, c1_) in enumerate(((0, 8 * D), (8 * D, 16 * D), (16 * D, NF * D))):
                P.add("sp", DMA(wdf[:, c0_:c1_], wdcache[:, c0_:c1_]), pw=["wdown"], dma="wd%d" % i_)
            for t in range(NT2):
                for f in range(NF):
                    gu(t, f)
                    if t + 1 < NT2:
                        if f in (1, 6, 11, 16):
                            x1_partA(t + 1, (f - 1) // 5)
                        if f in (4, 9, 14, 19):
                            x1_partB(t + 1, (f - 4) // 5)
                for blk in range(4):
                    down(t, blk)
            P.barrier()
            P.emit(nc, semf)
    return nc


def _t5_bucket(dist):
    n = np.maximum(dist, 0)
    max_exact = 16
    nf = np.maximum(n, 1).astype(np.float32)
    large = max_exact + (np.log(nf / max_exact) / math.log(128 / max_exact) * (32 - max_exact)).astype(np.int32)
    large = np.minimum(large, 31)
    return np.where(n < max_exact, n, large)


_NC_CACHE = {}


def kernel(x, attn_norm, w_in, dn_conv, dn_a_log, dn_dt_bias, dn_out_norm, swa_q_norm, swa_k_norm,
           swa_sinks, rel_bias, w_branch_dn, w_branch_swa, w_out, ffn_norm, w_gate, w_up, w_down):
    f = lambda a: np.ascontiguousarray(np.asarray(a, dtype=np.float32))
    x = f(x)
    if "nc" not in _NC_CACHE:
        _NC_CACHE["nc"] = build()
    nc = _NC_CACHE["nc"]
    idx = np.arange(128)
    same = (idx[:, None] // 64) == (idx[None, :] // 64)
    U = ((idx[:, None] <= idx[None, :]) & same).astype(np.float32)
    Urev = ((idx[:, None] > idx[None, :]) & same).astype(np.float32)
    NEGM = np.where((idx[None, :] >= idx[:, None]) & same, 0.0, -1e5).astype(np.float32)
    STRICT = ((idx[None, :] > idx[:, None]) & same).astype(np.float32)
    m01 = np.stack([(idx < 64), (idx >= 64)], 1).astype(np.float32)
    ident = np.eye(128, dtype=np.float32)
    rb = f(rel_bias)
    s_i = idx[:, None]
    q_i = idx[None, :]
    dist_prev = 128 + q_i - s_i
    dist_cur = q_i - s_i
    biasP = np.empty((128, 8, 128), np.float32)
    biasC = np.empty((128, 8, 128), np.float32)
    bk_p = _t5_bucket(dist_prev)
    bk_c = _t5_bucket(dist_cur)
    ok_p = (dist_prev >= 0) & (dist_prev < 128)
    ok_c = (dist_cur >= 0) & (dist_cur < 128)
    for h in range(8):
        biasP[:, h, :] = np.where(ok_p, rb[bk_p, h], NEGBIG)
        biasC[:, h, :] = np.where(ok_c, rb[bk_c, h], NEGBIG)
    biasP = biasP.reshape(128, 1024)
    biasC = biasC.reshape(128, 1024)
    common = {
        "w_in": f(w_in[0]), "w_a": f(w_branch_dn[0]), "w_b": f(w_branch_swa[0]), "w_out": f(w_out[0]),
        "w_gate": f(w_gate[0]), "w_up": f(w_up[0]), "w_down": f(w_down[0]),
        "attn_norm": f(attn_norm), "ffn_norm": f(ffn_norm), "dn_conv": f(dn_conv[0]),
        "a_log": f(dn_a_log), "dt_bias": f(dn_dt_bias), "out_norm": f(dn_out_norm),
        "q_norm": f(swa_q_norm), "k_norm": f(swa_k_norm), "sinks": f(swa_sinks),
        "biasP": biasP, "biasC": biasC, "ident": ident, "U": U, "Urev": Urev, "NEGM": NEGM,
        "STRICT": STRICT, "m01": m01,
    }
    in_maps = []
    for core in range(8):
        bt, half = core // 2, core % 2
        xe = np.zeros((NBLK * 128, D), np.float32)
        if half == 1:
            xe[:] = x[bt]
        else:
            xe[NPRE * 128:] = x[bt, :NOWN * 128]
        pmv = np.full((128, 1), NEGBIG if half == 0 else 0.0, np.float32)
        d = dict(common)
        d["x"] = xe
        d["pm"] = pmv
        in_maps.append(d)
    res = run_bass_kernel_spmd(nc, in_maps, core_ids=list(range(8)))
    out = np.empty((4, 4096, D), np.float32)
    for core in range(8):
        bt, half = core // 2, core % 2
        out[bt, half * 2048:(half + 1) * 2048] = res.results[core]["out"]
    return out
```

```python
import contextlib
import os
import math
import numpy as np
import ml_dtypes
import concourse.bass as bass
import concourse.mybir as mybir
from concourse.bass_utils import run_bass_kernel_spmd

F32 = mybir.dt.float32
BF16 = mybir.dt.bfloat16
AF = mybir.ActivationFunctionType
ALU = mybir.AluOpType
AX = mybir.AxisListType

D = 1024
DIN = 4872
DFF = 2816
NF = 22
NPRE = 16
NOWN = 16
NBLK = NPRE + NOWN
EPS = 1e-6
NEGBIG = -30000.0


class Op:
    __slots__ = ("eng", "fn", "dma", "dma_val", "waits", "signal", "sigval", "uid", "group")

    def __init__(self, eng, fn, dma):
        self.eng = eng
        self.fn = fn
        self.dma = dma
        self.dma_val = 0
        self.waits = []
        self.signal = False
        self.sigval = 0
        self.group = False


class Prog:
    ENGS = ("pe", "act", "dve", "pool", "sp")

    def __init__(self, tag):
        self.tag = tag
        self.ops = {e: [] for e in self.ENGS}
        self.res = {}
        self.dma_cnt = {}
        self.dma_last = {}
        self.dma_uses = {}
        self.bank_last = {}
        self.uid = 0

    def _st(self, name):
        st = self.res.get(name)
        if st is None:
            st = {"w": {}, "r": {}}
            self.res[name] = st
        return st

    def _dep(self, op, prev, kind):
        if prev is op:
            return
        if prev.dma is None and op.dma is None and prev.eng == op.eng and kind != "raw":
            return
        if prev in op.waits:
            return
        op.waits.append(prev)
        if prev.dma is None:
            prev.signal = True

    def add(self, eng, fn, r=(), w=(), pw=(), dma=None, group=False):
        if dma is not None and not group:
            n = self.dma_uses.get(dma, 0)
            self.dma_uses[dma] = n + 1
            dma = "%s_%d" % (dma, n // 14)
        op = Op(eng, fn, dma)
        op.group = group
        self.uid += 1
        op.uid = self.uid
        key = eng if dma is None else ("dma", self.uid)
        for name in r:
            for p in self._st(name)["w"].values():
                self._dep(op, p, "raw")
        for name in tuple(w) + tuple(pw):
            st = self._st(name)
            for p in st["w"].values():
                self._dep(op, p, "waw")
            for p in st["r"].values():
                self._dep(op, p, "war")
        banks = set()
        for name in tuple(r) + tuple(w) + tuple(pw):
            if name == "BT":
                banks.add("T")
            elif name[0] in "Bb" and name[1:2].isdigit():
                banks.add(name[1])
        for bk in banks:
            bl = self.bank_last.setdefault(bk, {})
            for e2, p in bl.items():
                if e2 != eng:
                    self._dep(op, p, "bank")
            bl[eng] = op
        for name in r:
            self._st(name)["r"][key] = op
        for name in w:
            st = self._st(name)
            st["w"] = {key: op}
            st["r"] = {}
        for name in pw:
            st = self._st(name)
            st["w"][key] = op
            st["r"] = {}
        if dma is not None:
            self.dma_cnt[dma] = self.dma_cnt.get(dma, 0) + 16
            op.dma_val = self.dma_cnt[dma]
            self.dma_last[dma] = op
        self.ops[eng].append(op)
        return op

    def barrier(self):
        lasts = {}
        for e in self.ENGS:
            for op in reversed(self.ops[e]):
                if op.fn is not None and op.dma is None:
                    lasts[e] = op
                    break
        dl = list(self.dma_last.values())
        for e in self.ENGS:
            b = Op(e, None, None)
            for e2, op in lasts.items():
                if e2 != e:
                    b.waits.append(op)
                    op.signal = True
            b.waits.extend(dl)
            self.ops[e].append(b)

    def emit(self, nc, semf):
        for e in self.ENGS:
            cnt = 0
            for op in self.ops[e]:
                if op.dma is None and op.signal:
                    cnt += 1
                    op.sigval = cnt
        if os.environ.get('K_DBG'):
            print('SIG', self.tag, {e: max([o.sigval for o in self.ops[e]] + [0]) for e in self.ENGS}, {e: len(self.ops[e]) for e in self.ENGS}, self.dma_cnt)
        esem = {e: semf(self.tag + "e" + e) for e in self.ENGS}
        dsem = {k: semf(self.tag + "d" + str(k)) for k in self.dma_cnt}
        prog = self

        def run(en, eng):
            known = {}
            for op in prog.ops[en]:
                for p in op.waits:
                    if p.dma is None:
                        sem, val, k = esem[p.eng], p.sigval, "e" + p.eng
                    else:
                        sem, val, k = dsem[p.dma], (prog.dma_cnt[p.dma] if p.group else p.dma_val), "d" + str(p.dma)
                    if known.get(k, 0) >= val:
                        continue
                    eng.wait_ge(sem, val)
                    known[k] = val
                if op.fn is None:
                    continue
                ins = op.fn(eng)
                if op.dma is not None:
                    ins.then_inc(dsem[op.dma], 16)
                elif op.signal:
                    ins.then_inc(esem[en], 1)

        with nc.Block() as block:
            @block.tensor
            def _(t):
                run("pe", t)

            @block.scalar
            def _(t):
                run("act", t)

            @block.vector
            def _(t):
                run("dve", t)

            @block.gpsimd
            def _(t):
                run("pool", t)

            @block.sync
            def _(t):
                run("sp", t)


def MM(out, lhsT, rhs, st=True, sp=True, skip=False):
    if skip:
        return lambda e: e.matmul(out, lhsT=lhsT, rhs=rhs, start=st, stop=sp, skip_group_check=True)
    return lambda e: e.matmul(out, lhsT=lhsT, rhs=rhs, start=st, stop=sp)


def TR(out, in_, ident):
    return lambda e: e.transpose(out=out, in_=in_, identity=ident)


def ACT(out, in_, func, **kw):
    return lambda e: e.activation(out=out, in_=in_, func=func, **kw)


def TT(out, a, b, op):
    return lambda e: e.tensor_tensor(out=out, in0=a, in1=b, op=op)


def TS(out, a, s1, op0, s2=None, op1=None):
    if op1 is None:
        return lambda e: e.tensor_scalar(out=out, in0=a, scalar1=s1, scalar2=None, op0=op0)
    return lambda e: e.tensor_scalar(out=out, in0=a, scalar1=s1, scalar2=s2, op0=op0, op1=op1)


def STT(out, a, s, b, op0, op1):
    return lambda e: e.scalar_tensor_tensor(out=out, in0=a, scalar=s, in1=b, op0=op0, op1=op1)


def CP(out, in_):
    return lambda e: e.tensor_copy(out=out, in_=in_)


def RC(out, in_):
    return lambda e: e.reciprocal(out=out, in_=in_)


def RED(out, in_):
    return lambda e: e.tensor_reduce(out=out, in_=in_, axis=AX.X, op=ALU.add)


def DMA(out, in_, **kw):
    return lambda e: e.dma_start(out=out, in_=in_, **kw)


def MSET(ap, v):
    return lambda e: e.memset(ap, v)


def v4(ap, h=4):
    return ap.rearrange("p (h d) -> p h d", h=h)


def build():
    nc = bass.Bass("TRN2", target_bir_lowering=False)

    def din(name, shape):
        return nc.dram_tensor(name, shape, F32, kind="ExternalInput").ap()

    x_d = din("x", [NBLK * 128, D])
    w_in_d = din("w_in", [D, DIN])
    w_a_d = din("w_a", [512, D])
    w_b_d = din("w_b", [512, D])
    w_out_d = din("w_out", [D, D])
    w_gate_d = din("w_gate", [D, DFF])
    w_up_d = din("w_up", [D, DFF])
    w_down_d = din("w_down", [DFF, D])
    attn_norm_d = din("attn_norm", [1, D])
    ffn_norm_d = din("ffn_norm", [1, D])
    conv_d = din("dn_conv", [4, 1536])
    alog_d = din("a_log", [1, 4])
    dtb_d = din("dt_bias", [1, 4])
    onorm_d = din("out_norm", [1, 128])
    qg_d = din("q_norm", [1, 64])
    kg_d = din("k_norm", [1, 64])
    sinks_d = din("sinks", [1, 8])
    biasP_d = din("biasP", [128, 1024])
    biasC_d = din("biasC", [128, 1024])
    pm_d = din("pm", [128, 1])
    ident_d = din("ident", [128, 128])
    U_d = din("U", [128, 128])
    Urev_d = din("Urev", [128, 128])
    NEG_d = din("NEGM", [128, 128])
    STR_d = din("STRICT", [128, 128])
    m01_d = din("m01", [128, 2])
    out_d = nc.dram_tensor("out", [NOWN * 128, D], F32, kind="ExternalOutput").ap()
    wcache = nc.dram_tensor("wcache", [NF * 128, 16 * 128], BF16, kind="Internal").ap()
    wocache = nc.dram_tensor("wocache", [128, 8 * D], BF16, kind="Internal").ap()
    wdcache = nc.dram_tensor("wdcache", [128, NF * D], BF16, kind="Internal").ap()

    top = contextlib.ExitStack()
    with top:
        def sbt(es, name, shape, dt):
            return es.enter_context(nc.sbuf_tensor("s_" + name, shape, dt))

        def semf(name):
            return top.enter_context(nc.semaphore(name))

        B = [top.enter_context(nc.psum_tensor("B%d" % i, [128, 512], F32)) for i in range(7)]
        BT = top.enter_context(nc.psum_tensor("BT", [128, 1024], BF16))

        hTm = sbt(top, "hTm", [128, 8, NOWN * 128], BF16)
        identf = sbt(top, "identf", [128, 128], F32)
        identb = sbt(top, "identb", [128, 128], BF16)

        s1 = contextlib.ExitStack()
        with s1:
            yAT = sbt(s1, "yAT", [128, 4, NOWN * 128], BF16)
            yBT = sbt(s1, "yBT", [128, 4, NOWN * 128], BF16)

            sa = contextlib.ExitStack()
            with sa:
                P = Prog("a")
                NWT = 1288
                CZ, CBA, CQ, CKV = 0, 512, 520, 1032
                winT = sbt(sa, "winT", [128, 8, NWT], BF16)
                wqs = [sbt(sa, "wqs%d" % i, [128, 8, 128], BF16) for i in range(3)]
                Uc = sbt(sa, "Uc", [128, 128], F32)
                Urev = sbt(sa, "Urev", [128, 128], F32)
                NEGM = sbt(sa, "NEGM", [128, 128], F32)
                STR = sbt(sa, "STR", [128, 128], F32)
                m01 = sbt(sa, "m01", [128, 2], F32)
                onesb = sbt(sa, "onesb", [128, 128], BF16)
                gainT = sbt(sa, "gainT", [128, 8], F32)
                wconv = sbt(sa, "wconv", [128, 4, 12], F32)
                alog = sbt(sa, "alog", [128, 4], F32)
                dtb = sbt(sa, "dtb", [128, 4], F32)
                negA = sbt(sa, "negA", [128, 4], F32)
                onorm = sbt(sa, "onorm", [128, 128], F32)
                qg = sbt(sa, "qg", [128, 64], F32)
                kg = sbt(sa, "kg", [128, 64], F32)
                sinks = sbt(sa, "sinks", [128, 8], F32)
                esink = sbt(sa, "esink", [128, 8], F32)
                biasP8 = sbt(sa, "biasP8", [128, 1024], BF16)
                biasC8 = sbt(sa, "biasC8", [128, 1024], BF16)
                pm = sbt(sa, "pm", [128, 1], F32)
                dg = [sbt(sa, "dg%d" % i, [128, 4, 128], BF16) for i in range(3)]
                xin = sbt(sa, "xin", [128, D], F32)
                hbf = sbt(sa, "hbf", [128, D], BF16)
                xct = [sbt(sa, "xct%d" % i, [128, 516], BF16) for i in range(2)]
                halo = sbt(sa, "halo", [128, 12, 4], BF16)
                sil = [sbt(sa, "sil%d" % i, [128, 512], F32) for i in range(4)]
                vT = sbt(sa, "vT", [128, 4, 512], BF16)
                sqs = [sbt(sa, "sq%d" % i, [128, 512], BF16) for i in range(2)]
                rt = [sbt(sa, "rt%d" % i, [128, 512], F32) for i in range(2)]
                qkn = sbt(sa, "qkn", [128, 8, 512], BF16)
                kvraws = [sbt(sa, "kvraw%d" % i, [128, 256], F32) for i in range(2)]
                qraws = [sbt(sa, "qraw%d" % i, [128, 512], F32) for i in range(2)]
                Fb = [sbt(sa, "F%d" % i, [128, 4, 128], F32) for i in range(4)]
                gstage = Fb[0][0:8, 0, :]
                cstage = Fb[1][0:48, 0, :]
                SBt = sbt(sa, "SBt", [128, 4, 128], F32)
                Gs = [sbt(sa, "G%d" % i, [128, 4, 128], F32) for i in range(4)]
                EGBs = [sbt(sa, "EGB%d" % i, [128, 4, 128], F32) for i in range(2)]
                Xc = [sbt(sa, "Xc%d" % i, [128, 4, 128], F32) for i in range(2)]
                Yc = [sbt(sa, "Yc%d" % i, [128, 4, 128], F32) for i in range(2)]
                Pm = sbt(sa, "Pm", [128, 4, 128], F32)
                zss = [sbt(sa, "zs%d" % i, [128, 4, 128], F32) for i in range(2)]
                KgTs = [sbt(sa, "KgT%d" % i, [128, 4, 128], BF16) for i in range(2)]
                qgTs = [sbt(sa, "qgT%d" % i, [128, 4, 128], BF16) for i in range(2)]
                QKTs = [sbt(sa, "QKT%d" % i, [128, 4, 128], BF16) for i in range(2)]
                TTbs = [sbt(sa, "TTb%d" % i, [128, 4, 128], BF16) for i in range(2)]
                kvtoks = [sbt(sa, "kvtok%d" % i, [128, 8, 128], BF16) for i in range(2)]
                Rb = sbt(sa, "Rb", [128, 4, 128], BF16)
                vd = sbt(sa, "vd", [128, 4, 128], BF16)
                vn = sbt(sa, "vn", [128, 4, 128], BF16)
                S = sbt(sa, "S", [128, 4, 128], F32)
                Sbf = sbt(sa, "Sbf", [128, 4, 128], BF16)
                yA = sbt(sa, "yA", [128, 512], BF16)
                qn1 = sbt(sa, "qn0", [128, 512], BF16)
                qns = [qn1, qn1]
                kt = sbt(sa, "kt", [128, 128], F32)
                kpad1 = sbt(sa, "kpad0", [128, 2, 2, 128], BF16)
                kpads = [kpad1, kpad1]
                V1 = [sbt(sa, "V1_%d" % i, [128, 2, 66], BF16) for i in range(3)]
                kTp = [sbt(sa, "kTp%d" % i, [128, 4, 128], BF16) for i in range(2)]
                qTp = sbt(sa, "qTp", [128, 4, 128], BF16)
                pT = [sbt(sa, "pT%d" % i, [128, 512], BF16) for i in range(2)]
                yB = sbt(sa, "yB", [128, 512], BF16)

                def colt(name, n):
                    return sbt(sa, name, [128, 16], F32)[:, 0:n]
                c_ss, c_rt, c_rstd = colt("c_ss", 1), colt("c_rt", 1), colt("c_rstd", 1)
                betas = [colt("beta0", 4), colt("beta1", 4)]
                gcol = colt("gcol", 4)
                xa = colt("xa", 4)
                ngc = colt("ngc", 4)
                dk = colt("dk", 4)
                bdms = [[colt("bdm0_0", 4), colt("bdm1_0", 4)], [colt("bdm0_1", 4), colt("bdm1_1", 4)]]
                oss = colt("oss", 4)
                ors = colt("ors", 4)
                qss = colt("qss", 8)
                qrs = colt("qrs", 8)
                kss = colt("kss", 2)
                krs = colt("krs", 2)
                den = colt("den", 4)
                if os.environ.get("K_DBG"):
                    print("SBUF remaining phase A", nc.sbuf_bytes_remaining)

                P.add("sp", DMA(identf[:], ident_d), w=["identf"], dma="cs0", group=True)
                P.add("sp", DMA(Uc[:], U_d), w=["Uc"], dma="cs0", group=True)
                P.add("sp", DMA(Urev[:], Urev_d), w=["Urev"], dma="cs0", group=True)
                P.add("sp", DMA(NEGM[:], NEG_d), w=["NEGM"], dma="cs0", group=True)
                P.add("sp", DMA(STR[:], STR_d), w=["STR"], dma="cs0", group=True)
                P.add("sp", DMA(m01[:], m01_d), w=["m01"], dma="cs0", group=True)
                P.add("sp", DMA(pm[:], pm_d), w=["pm"], dma="cs0", group=True)
                P.add("sp", DMA(gstage, attn_norm_d.rearrange("o (kc p) -> (o kc) p", p=128)), w=["gstage", "F0"], dma="cs0", group=True)
                P.add("sp", DMA(cstage, conv_d.rearrange("j (c p) -> (j c) p", p=128)), w=["cstage", "F1"], dma="cs0", group=True)
                P.add("pe", TR(B[6][:, 0:8], gstage, identf[0:8, 0:8]), r=["gstage", "identf"], pw=["B6"])
                P.add("pe", TR(B[6][:, 64:112], cstage, identf[0:48, 0:48]), r=["cstage", "identf"], pw=["B6"])
                P.add("act", ACT(gainT[:], B[6][:, 0:8], AF.Copy), r=["B6"], w=["gainT"])
                P.add("act", ACT(wconv[:].rearrange("p j c -> p (j c)"), B[6][:, 64:112], AF.Copy), r=["B6"], w=["wconv"])
                P.add("sp", DMA(alog[:], alog_d[0:1, :].partition_broadcast(128)), w=["alog"], dma="cs0", group=True)
                P.add("sp", DMA(dtb[:], dtb_d[0:1, :].partition_broadcast(128)), w=["dtb"], dma="cs0", group=True)
                P.add("sp", DMA(onorm[:], onorm_d[0:1, :].partition_broadcast(128)), w=["onorm"], dma="cs0", group=True)
                P.add("sp", DMA(qg[:], qg_d[0:1, :].partition_broadcast(128)), w=["qg"], dma="cs1", group=True)
                P.add("sp", DMA(kg[:], kg_d[0:1, :].partition_broadcast(128)), w=["kg"], dma="cs1", group=True)
                P.add("sp", DMA(sinks[:], sinks_d[0:1, :].partition_broadcast(128)), w=["sinks"], dma="cs1", group=True)
                for i_, (bd_, b8_, b8n_) in enumerate(((biasP_d, biasP8, "biasP8"), (biasC_d, biasC8, "biasC8"))):
                    for hf_ in range(2):
                        g_ = Gs[i_ * 2 + hf_]
                        gn_ = "G%d" % (i_ * 2 + hf_)
                        P.add("sp", DMA(g_[:].rearrange("p h d -> p (h d)"), bd_[:, hf_ * 512:(hf_ + 1) * 512]), w=[gn_], dma="cs1", group=True)
                        P.add("dve", TS(b8_[:, hf_ * 512:(hf_ + 1) * 512], g_[:].rearrange("p h d -> p (h d)"), 8.0, ALU.mult),
                              r=[gn_], pw=[b8n_])
                w_in_v = w_in_d.rearrange("(kc p) n -> p kc n", p=128)
                for i, c0 in enumerate(range(0, NWT, 512)):
                    c1 = min(c0 + 512, NWT)
                    P.add("pool", DMA(winT[:, :, c0:c1], w_in_v[:, :, 1536 + c0:1536 + c1]), pw=["winT"], dma="w%d" % i)
                P.add("dve", CP(identb[:], identf[:]), r=["identf"], w=["identb"])
                P.add("dve", MSET(onesb[:], 1.0), w=["onesb"])
                P.add("act", ACT(negA[:], alog[:], AF.Exp), r=["alog"], w=["negA"])
                P.add("dve", TS(negA[:], negA[:], -1.0, ALU.mult), r=["negA"], w=["negA"])
                P.add("act", ACT(esink[:], sinks[:], AF.Exp), r=["sinks"], w=["esink"])
                P.add("pool", MSET(kpad1[:], 0.0), w=["kpad0"])
                for i in range(2):
                    P.add("pool", MSET(kTp[i][:], 0.0), w=["kTp%d" % i])
                for i in range(3):
                    P.add("pool", MSET(V1[i][:], 1.0), w=["V1_%d" % i])
                P.add("pool", MSET(S[:], 0.0), w=["S"])
                P.add("pool", MSET(Sbf[:], 0.0), w=["Sbf"])
                P.add("pool", MSET(Rb[:], 0.0), w=["Rb"])
                P.add("pool", MSET(vd[:], 0.0), w=["vd"])
                P.add("pool", MSET(vn[:], 0.0), w=["vn"])
                P.add("pool", MSET(halo[:], 0.0), w=["halo"])

                NEG4 = NEGM[:].unsqueeze(1).to_broadcast([128, 4, 128])
                STR4 = STR[:].unsqueeze(1).to_broadcast([128, 4, 128])
                ID4 = identf[:].unsqueeze(1).to_broadcast([128, 4, 128])
                gainO4 = onorm[:].unsqueeze(1).to_broadcast([128, 4, 128])
                gainQ8 = qg[:].unsqueeze(1).to_broadcast([128, 8, 64])
                gainK2 = kg[:].unsqueeze(1).to_broadcast([128, 2, 64])
                GT8 = gainT[:].unsqueeze(2).to_broadcast([128, 8, 128])
                NSB = int(os.environ.get('K_NSB', NBLK // 4))
                b4 = lambda col: col.unsqueeze(2).to_broadcast([128, 4, 128])

                class Thr:
                    def __init__(self):
                        self.q = []

                    def add(self, eng, fn, glue=False, **kw):
                        self.q.append((eng, fn, kw, glue))

                def take(q, n):
                    k_ = 0
                    while q and (k_ < n or q[0][3]):
                        e_, f_, kw_, _g = q.pop(0)
                        P.add(e_, f_, **kw_)
                        k_ += 1

                def groups(q):
                    out = []
                    for op in q:
                        if out and (op[3] or op[0] == out[-1][-1][0]):
                            out[-1].append(op)
                        else:
                            out.append([op])
                    return out

                def merge(*thrs):
                    gs = [groups(t.q) for t in thrs]
                    for t in thrs:
                        t.q = []
                    idx = [0] * len(gs)
                    while True:
                        best, bestf = -1, 2.0
                        for i_, g_ in enumerate(gs):
                            if idx[i_] < len(g_):
                                f_ = idx[i_] / float(len(g_))
                                if f_ < bestf:
                                    best, bestf = i_, f_
                        if best < 0:
                            break
                        for e_, f_, kw_, _g in gs[best][idx[best]]:
                            P.add(e_, f_, **kw_)
                        idx[best] += 1

                def hT_of(sbk):
                    if sbk >= NPRE // 4:
                        r_ = sbk - NPRE // 4
                    else:
                        r_ = sbk % 2
                    return hTm[:, :, r_ * 512:(r_ + 1) * 512], "hTr%d" % r_

                def stage_X(T, sbk, js=(0, 1, 2, 3)):
                    hT, hTn = hT_of(sbk)
                    for j in js:
                        b = sbk * 4 + j
                        jsl = slice(j * 128, (j + 1) * 128)
                        T.add("sp", DMA(xin[:], x_d[b * 128:(b + 1) * 128, :]), w=["xin"], dma="xin")
                        T.add("act", ACT(hbf[:], xin[:], AF.Square, accum_out=c_ss), r=["xin"], w=["hbf", "c_ss"])
                        T.add("act", ACT(c_rt, c_ss, AF.Ln, scale=1.0 / D, bias=EPS), r=["c_ss"], w=["c_rt"])
                        T.add("act", ACT(c_rstd, c_rt, AF.Exp, scale=-0.5), r=["c_rt"], w=["c_rstd"])
                        T.add("act", ACT(hbf[:], xin[:], AF.Copy, scale=c_rstd), r=["xin", "c_rstd"], w=["hbf"])
                        for kc in range(8):
                            T.add("pe", TR(BT[:, kc * 128:(kc + 1) * 128], hbf[:, kc * 128:(kc + 1) * 128], identb[:]),
                                  glue=(kc > 0), r=["hbf", "identb"], pw=["BT"])
                        T.add("dve", TT(hT[:, :, jsl], v4(BT[:], 8), GT8, ALU.mult), glue=True, r=["BT", "gainT"], pw=[hTn])

                fcnt = [0]

                def stage_F(T, sbk):
                    hT, hTn = hT_of(sbk)
                    own_sb = sbk >= NPRE // 4
                    clist = list(range(12) if (own_sb or sbk == NPRE // 4 - 1) else range(4, 12))
                    cis = []
                    for c in clist:
                        cis.append(fcnt[0])
                        fcnt[0] += 1

                    def st0(c, ci):
                        p2, p3 = ci % 2, ci % 3
                        wq, wqn = wqs[p3], "wqs%d" % p3
                        T.add("pool", DMA(wq[:], w_in_v[:, :, c * 128:(c + 1) * 128]), w=[wqn], dma=wqn)
                        dgi, dgn = dg[p3], "dg%d" % p3
                        for j in range(4):
                            T.add("dve", TS(dgi[:, j, :], identf[:], wconv[:, j, c:c + 1], ALU.mult),
                                  r=["identf", "wconv"], pw=[dgn])
                        pf, pfn = B[2 + p2], "B%d" % (2 + p2)
                        for kc in range(8):
                            T.add("pe", MM(pf[:], wq[:, kc, :], hT[:, kc, :], kc == 0, kc == 7), r=[wqn, hTn], pw=[pfn])

                    def st1(c, ci):
                        p2 = ci % 2
                        pf, pfn = B[2 + p2], "B%d" % (2 + p2)
                        xt, xtn = xct[p2], "xct%d" % p2
                        if ci % 3 == 2:
                            T.add("dve", CP(xt[:, 3:515], pf[:]), r=[pfn], pw=[xtn])
                        else:
                            T.add("act", ACT(xt[:, 3:515], pf[:], AF.Copy), r=[pfn], pw=[xtn])
                        T.add("dve", CP(xt[:, 0:3], halo[:, c, 0:3]), r=["halo", "halo%d" % c], pw=[xtn])

                    def st2(c, ci):
                        p2, p3 = ci % 2, ci % 3
                        dgi, dgn = dg[p3], "dg%d" % p3
                        xt, xtn = xct[p2], "xct%d" % p2
                        pc, pcn = B[4 + p2], "B%d" % (4 + p2)
                        for j in range(4):
                            T.add("pe", MM(pc[:], dgi[:, j, :], xt[:, j:j + 512], j == 0, j == 3), r=[dgn, xtn], pw=[pcn])
                        T.add("dve", CP(halo[:, c, 0:3], xt[:, 512:515]), r=[xtn], w=["halo%d" % c])

                    def st3(c, ci):
                        p2, p4 = ci % 2, ci % 4
                        pc, pcn = B[4 + p2], "B%d" % (4 + p2)
                        sl, sn = sil[p4], "sil%d" % p4
                        T.add("act", ACT(sl[:], pc[:], AF.Exp, scale=-1.0), r=[pcn], w=[sn])
                        T.add("act", ACT(sl[:], sl[:], AF.Ln, bias=1.0), r=[sn], w=[sn])
                        T.add("act", ACT(sl[:], sl[:], AF.Exp, scale=-1.0), r=[sn], w=[sn])

                    def st4(c, ci):
                        p2, p4 = ci % 2, ci % 4
                        pc, pcn = B[4 + p2], "B%d" % (4 + p2)
                        sl, sn = sil[p4], "sil%d" % p4
                        if c >= 8:
                            T.add("dve", TT(vT[:, c - 8, :], sl[:], pc[:], ALU.mult), r=[sn, pcn], pw=["vT"])
                        else:
                            T.add("dve", TT(sl[:], sl[:], pc[:], ALU.mult), r=[sn, pcn], w=[sn])
                            T.add("dve", TT(sqs[p2][:], sl[:], sl[:], ALU.mult), r=[sn], w=["sq%d" % p2])

                    def st5(c, ci):
                        if c >= 8:
                            return
                        p2 = ci % 2
                        T.add("pe", MM(B[6][:], onesb[:], sqs[p2][:]), r=["onesb", "sq%d" % p2], pw=["B6"])
                        rti, rtn = rt[p2], "rt%d" % p2
                        T.add("act", ACT(rti[:], B[6][:], AF.Ln, bias=EPS), r=["B6"], w=[rtn])
                        T.add("act", ACT(rti[:], rti[:], AF.Exp, scale=-0.5), r=[rtn], w=[rtn])

                    def st6(c, ci):
                        if c >= 8:
                            return
                        p2, p4 = ci % 2, ci % 4
                        sl, sn = sil[p4], "sil%d" % p4
                        rti, rtn = rt[p2], "rt%d" % p2
                        scl = (128.0 ** -0.5) if c < 4 else 1.0
                        T.add("dve", STT(qkn[:, c, :], sl[:], scl, rti[:], ALU.mult, ALU.mult), r=[sn, rtn], pw=["qkn"])

                    stages = (st0, st1, st2, st3, st4, st5, st6)
                    n = len(clist)
                    for i in range(n + len(stages) - 1):
                        for k in range(len(stages) - 1, -1, -1):
                            if 0 <= i - k < n:
                                stages[k](clist[i - k], cis[i - k])

                def stage_T1(T, sbk, j):
                    b = sbk * 4 + j
                    own = b >= NPRE
                    par = b % 2
                    jsl = slice(j * 128, (j + 1) * 128)
                    hTf, hTn = hT_of(sbk)
                    hT = hTf[:, :, jsl]
                    has_kv = own or b == NPRE - 1
                    beta, bdm = betas[par], bdms[par]
                    zs, KgT, qgT, QKT, TTb, EGB = zss[par], KgTs[par], qgTs[par], QKTs[par], TTbs[par], EGBs[par]
                    zsn, KgTn, qgTn, QKTn, TTbn, EGBn = ["%s%d" % (n_, par) for n_ in ("zs", "KgT", "qgT", "QKT", "TTb", "EGB")]
                    betan = "beta%d" % par
                    qn, qnn = qns[par], "qn0"
                    kpad, kpadn = kpads[par], "kpad0"
                    kvtok, kvtokn = kvtoks[par], "kvtok%d" % par
                    v1i = b % 3
                    side = []

                    def SD(eng, fn, **kw):
                        side.append((eng, fn, kw))

                    def drain(n):
                        for _ in range(min(n, len(side))):
                            e_, f_, kw_ = side.pop(0)
                            T.add(e_, f_, **kw_)
                    for kc in range(8):
                        T.add("pe", MM(B[2][:, 256:264], hT[:, kc, :], winT[:, kc, CBA:CBA + 8], kc == 0, kc == 7),
                              r=["winT", hTn], pw=["B2"])
                    if has_kv:
                        for kc in range(8):
                            T.add("pe", MM(B[2][:, 0:256], hT[:, kc, :], winT[:, kc, CKV:CKV + 256], kc == 0, kc == 7),
                                  r=["winT", hTn], pw=["B2"])
                    if own:
                        for kc in range(8):
                            T.add("pe", MM(B[3][:], hT[:, kc, :], winT[:, kc, CQ:CQ + 512], kc == 0, kc == 7),
                                  r=["winT", hTn], pw=["B3"])
                        for kc in range(8):
                            T.add("pe", MM(B[4][:], hT[:, kc, :], winT[:, kc, CZ:CZ + 512], kc == 0, kc == 7),
                                  r=["winT", hTn], pw=["B4"])
                    T.add("act", ACT(beta, B[2][:, 256:260], AF.Exp, scale=-1.0), r=["B2"], w=[betan])
                    T.add("act", ACT(beta, beta, AF.Ln, bias=1.0), r=[betan], w=[betan])
                    T.add("act", ACT(beta, beta, AF.Exp, scale=-1.0), r=[betan], w=[betan])
                    T.add("dve", TT(xa, B[2][:, 260:264], dtb[:], ALU.add), r=["B2", "dtb"], w=["xa"])
                    T.add("act", ACT(xa, xa, AF.Exp), r=["xa"], w=["xa"])
                    T.add("act", ACT(xa, xa, AF.Ln, bias=1.0), r=["xa"], w=["xa"])
                    T.add("dve", TT(gcol, xa, negA[:], ALU.mult), r=["xa", "negA"], w=["gcol"])
                    T.add("dve", TT(SBt[:], STR4, b4(beta), ALU.mult), r=["STR", betan], w=["SBt"])

                    Gb, gm, ET, EsT = Fb[0], Fb[1], Fb[2], Fb[3]
                    T.add("dve", CP(Gb[:], b4(gcol)), r=["gcol"], w=["F0"])
                    for h in range(4):
                        T.add("pe", MM(B[5][:, h * 128:(h + 1) * 128], Gb[:, h, :], Uc[:]), r=["F0", "Uc"], pw=["B5"])
                    T.add("pe", MM(B[6][:, 0:4], Uc[:], gcol), r=["Uc", "gcol"], pw=["B6"])
                    T.add("pe", MM(B[6][:, 4:8], Urev[:], gcol), r=["Urev", "gcol"], pw=["B6"])
                    T.add("act", ACT(EGB[:], v4(B[5][:]), AF.Exp), r=["B5"], w=[EGBn])
                    T.add("dve", TS(ngc, B[6][:, 0:4], -1.0, ALU.mult), r=["B6"], w=["ngc"])
                    T.add("act", ACT(dk, B[6][:, 4:8], AF.Exp), r=["B6"], w=["dk"])
                    for c in range(2):
                        T.add("dve", STT(bdm[c], dk, m01[:, c:c + 1], beta, ALU.mult, ALU.mult),
                              r=["dk", "m01", betan], w=["bdm%d_%d" % (c, par)])
                    T.add("dve", TT(gm[:], v4(B[5][:]), NEG4, ALU.add), r=["B5", "NEGM"], w=["F1"])
                    T.add("dve", TT(gm[:], gm[:], b4(ngc), ALU.add), r=["F1", "ngc"], w=["F1"])
                    for h in range(4):
                        T.add("pe", MM(B[6][:, h * 128:(h + 1) * 128], qkn[:, 4 + h, jsl], qkn[:, 4 + h, jsl]),
                              r=["qkn"], pw=["B6"])
                    T.add("act", ACT(ET[:], gm[:], AF.Exp), r=["F1"], w=["F2"])
                    T.add("dve", TT(EsT[:], v4(B[6][:]), SBt[:], ALU.mult), r=["B6", "SBt"], w=["F3"])
                    if own:
                        T.add("act", ACT(zs[:], v4(B[4][:]), AF.Copy), r=["B4"], w=[zsn])
                        T.add("act", ACT(qraws[par][:], B[3][:], AF.Copy), r=["B3"], w=["qraw%d" % par])
                    if has_kv:
                        T.add("act", ACT(kvraws[par][:], B[2][:, 0:256], AF.Copy), r=["B2"], w=["kvraw%d" % par])
                    T.add("dve", TT(KgT[:], qkn[:, 4:8, jsl], EGB[:], ALU.mult), r=["qkn", EGBn], w=[KgTn])
                    if own:
                        T.add("dve", TT(qgT[:], qkn[:, 0:4, jsl], EGB[:], ALU.mult), r=["qkn", EGBn], w=[qgTn])
                    if own:
                        for h in range(4):
                            T.add("pe", MM(B[3][:, h * 128:(h + 1) * 128], qkn[:, 4 + h, jsl], qkn[:, h, jsl]),
                                  r=["qkn"], pw=["B3"])
                    X0, Y0 = Xc[0], Yc[0]
                    T.add("dve", TT(X0[:], ET[:], EsT[:], ALU.mult), r=["F2", "F3"], w=["Xc0"])
                    if own:
                        T.add("dve", TT(QKT[:], v4(B[3][:]), ET[:], ALU.mult), r=["B3", "F2"], w=[QKTn])
                    for h in range(4):
                        T.add("pe", TR(B[6][:, h * 128:(h + 1) * 128], X0[:, h, :], identf[:]), r=["Xc0", "identf"], pw=["B6"])
                    T.add("act", ACT(Y0[:], v4(B[6][:]), AF.Copy), r=["B6"], w=["Yc0"])
                    T.add("dve", TT(Pm[:], ID4, X0[:], ALU.subtract), r=["identf", "Xc0"], w=["Pm"])
                    for h in range(4):
                        T.add("pe", TR(BT[:, h * 128:(h + 1) * 128], qkn[:, 4 + h, jsl], identb[:]), glue=(h > 0),
                              r=["qkn", "identb"], pw=["BT"])
                    for h in range(4):
                        T.add("pe", TR(BT[:, 512 + h * 128:512 + (h + 1) * 128], vT[:, h, jsl], identb[:]), glue=True,
                              r=["vT", "identb"], pw=["BT"])
                    T.add("act", ACT(kvtok[:], v4(BT[:], 8), AF.Copy), glue=True, r=["BT"], w=[kvtokn])
                    def emit_P(k):
                        Yn_, ynn_ = Yc[(k + 1) % 2], "Yc%d" % ((k + 1) % 2)
                        for h in range(4):
                            T.add("pe", MM(B[2][:, h * 128:(h + 1) * 128], Yn_[:, h, :], Pm[:, h, :]), r=[ynn_, "Pm"], pw=["B2"])
                        if k < 4:
                            T.add("dve", TT(Pm[:], v4(B[2][:]), Pm[:], ALU.add), r=["B2", "Pm"], w=["Pm"])
                        else:
                            T.add("dve", TT(TTb[:], v4(B[2][:]), Pm[:], ALU.add), r=["B2", "Pm"], w=[TTbn])

                    for k in range(5):
                        Xk, Yk = Xc[k % 2], Yc[k % 2]
                        Xn, Yn = Xc[(k + 1) % 2], Yc[(k + 1) % 2]
                        xkn, ykn = "Xc%d" % (k % 2), "Yc%d" % (k % 2)
                        xnn, ynn = "Xc%d" % ((k + 1) % 2), "Yc%d" % ((k + 1) % 2)
                        for h in range(4):
                            T.add("pe", MM(B[6][:, h * 128:(h + 1) * 128], Xk[:, h, :], Yk[:, h, :]), r=[xkn, ykn], pw=["B6"])
                        if k < 4:
                            for h in range(4):
                                T.add("pe", MM(B[5][:, h * 128:(h + 1) * 128], Yk[:, h, :], Xk[:, h, :]), r=[xkn, ykn], pw=["B5"])
                        T.add("dve", CP(Yn[:], v4(B[6][:])), r=["B6"], w=[ynn])
                        if k < 4:
                            T.add("act", ACT(Xn[:], v4(B[5][:]), AF.Copy), r=["B5"], w=[xnn])
                        if k > 0:
                            emit_P(k - 1)
                        drain(3)
                    emit_P(4)
                    drain(len(side))

                def stage_T2(T, sbk, j):
                    b = sbk * 4 + j
                    own = b >= NPRE
                    ob = b - NPRE
                    par = b % 2
                    tsl = slice(ob * 128, (ob + 1) * 128)
                    has_kv = own or b == NPRE - 1
                    beta, bdm = betas[par], bdms[par]
                    zs, KgT, qgT, QKT, TTb, EGB = zss[par], KgTs[par], qgTs[par], QKTs[par], TTbs[par], EGBs[par]
                    zsn, KgTn, qgTn, QKTn, TTbn, EGBn = ["%s%d" % (n_, par) for n_ in ("zs", "KgT", "qgT", "QKT", "TTb", "EGB")]
                    betan = "beta%d" % par
                    qn, qnn = qns[par], "qn0"
                    kpad, kpadn = kpads[par], "kpad0"
                    kvtok, kvtokn = kvtoks[par], "kvtok%d" % par
                    osb = Gs[0]
                    if own:
                        T.add("act", ACT(Gs[2][:], zs[:], AF.Exp, scale=-1.0), r=[zsn], w=["G2"])
                        T.add("act", ACT(Gs[2][:], Gs[2][:], AF.Ln, bias=1.0), r=["G2"], w=["G2"])
                        T.add("act", ACT(Gs[2][:], Gs[2][:], AF.Exp, scale=-1.0), r=["G2"], w=["G2"])
                        T.add("pool", TT(Gs[2][:], Gs[2][:], zs[:], ALU.mult), r=["G2", zsn], w=["G2"])
                        T.add("pool", TT(Gs[2][:], Gs[2][:], gainO4, ALU.mult), r=["G2", "onorm"], w=["G2"])
                    for c in range(2):
                        for h in range(4):
                            T.add("pe", MM(B[1][:, h * 128:(h + 1) * 128], KgT[:, h, :], Sbf[:, h, :]), r=[KgTn, "Sbf"], pw=["B1"])
                        T.add("dve", TT(Rb[:], kvtok[:, 4:8, :], v4(B[1][:]), ALU.subtract), r=[kvtokn, "B1"], w=["Rb"])
                        for h in range(4):
                            T.add("pe", MM(B[1][:, h * 128:(h + 1) * 128], TTb[:, h, :], Rb[:, h, :]), r=[TTbn, "Rb"], pw=["B1"])
                        T.add("dve", TT(vd[:], v4(B[1][:]), b4(bdm[c]), ALU.mult), r=["B1", "bdm%d_%d" % (c, par)], w=["vd"])
                        if own:
                            T.add("dve", TT(vn[:], v4(B[1][:]), b4(beta), ALU.mult), r=["B1", betan], w=["vn"])
                            for h in range(4):
                                T.add("pe", MM(B[1][:, h * 128:(h + 1) * 128], qgT[:, h, :], Sbf[:, h, :], True, False),
                                      r=[qgTn, "Sbf"], pw=["B1"])
                                T.add("pe", MM(B[1][:, h * 128:(h + 1) * 128], QKT[:, h, :], vn[:, h, :], False, True),
                                      glue=True, r=[QKTn, "vn"], pw=["B1"])
                            rs = slice(c * 64, (c + 1) * 64)
                            T.add("act", ACT(osb[rs].rearrange("p h d -> p (h d)"), B[1][rs, :], AF.Copy), r=["B1"], pw=["G0"])
                        for h in range(4):
                            T.add("pe", MM(B[1][:, h * 128:(h + 1) * 128], kvtok[:, h, :], vd[:, h, :]), r=[kvtokn, "vd"], pw=["B1"])
                        col = 63 + 64 * c
                        T.add("dve", TT(S[:], S[:], EGB[:, :, col:col + 1].to_broadcast([128, 4, 128]), ALU.mult),
                              r=["S", EGBn], w=["S"])
                        T.add("dve", TT(S[:], S[:], v4(B[1][:]), ALU.add), r=["S", "B1"], w=["S"])
                        T.add("act", ACT(Sbf[:], S[:], AF.Copy), r=["S"], w=["Sbf"])
                    if own:
                        T.add("act", ACT(Gs[1][:], osb[:], AF.Square), r=["G0"], w=["G1"])
                        T.add("dve", RED(oss, Gs[1][:]), r=["G1"], w=["oss"])
                        T.add("act", ACT(oss, oss, AF.Ln, scale=1.0 / 128, bias=EPS), r=["oss"], w=["oss"])
                        T.add("act", ACT(ors, oss, AF.Exp, scale=-0.5), r=["oss"], w=["ors"])
                        T.add("dve", TT(Gs[3][:], osb[:], b4(ors), ALU.mult), r=["G0", "ors"], w=["G3"])
                        T.add("dve", TT(v4(yA[:]), Gs[3][:], Gs[2][:], ALU.mult), r=["G3", "G2"], w=["yA"])
                        for p_ in range(4):
                            T.add("pe", TR(BT[:, p_ * 128:(p_ + 1) * 128], yA[:, p_ * 128:(p_ + 1) * 128], identb[:]), glue=(p_ > 0),
                                  r=["yA", "identb"], pw=["BT"])
                        T.add("act", ACT(yAT[:, :, tsl], v4(BT[:, 0:512]), AF.Copy), glue=True, r=["BT"], w=["yAT%d" % ob])

                def stage_T2b(T, sbk, j):
                    b = sbk * 4 + j
                    own = b >= NPRE
                    ob = b - NPRE
                    par = b % 2
                    tsl = slice(ob * 128, (ob + 1) * 128)
                    has_kv = own or b == NPRE - 1
                    if not has_kv:
                        return
                    qn, qnn = qns[par], "qn0"
                    kpad, kpadn = kpads[par], "kpad0"
                    kcur, kprev = b % 2, (b - 1) % 2
                    v1c, v1p = b % 3, (b - 1) % 3
                    kvr, kvrn = kvraws[par], "kvraw%d" % par
                    k2 = lambda ap: ap.rearrange("p (h d) -> p h d", h=2)
                    for kv_ in range(2):
                        T.add("act", ACT(pT[0][:, kv_ * 64:(kv_ + 1) * 64], kvr[:, kv_ * 64:(kv_ + 1) * 64], AF.Square,
                                         accum_out=kss[:, kv_:kv_ + 1]), r=[kvrn], pw=["pT0", "kss"])
                    T.add("act", ACT(kss, kss, AF.Ln, scale=1.0 / 64, bias=EPS), r=["kss"], w=["kss"])
                    T.add("act", ACT(krs, kss, AF.Exp, scale=-0.5), r=["kss"], w=["krs"])
                    T.add("pool", TT(k2(kt[:]), k2(kvr[:, 0:128]), krs.unsqueeze(2).to_broadcast([128, 2, 64]), ALU.mult),
                          r=[kvrn, "krs"], w=["kt"])
                    T.add("pool", TT(kpad[:, :, 0, 0:64], k2(kt[:]), gainK2, ALU.mult), r=["kt", "kg"], pw=[kpadn])
                    T.add("pool", TT(kpad[:, :, 1, 64:128], k2(kt[:]), gainK2, ALU.mult), r=["kt", "kg"], pw=[kpadn])
                    T.add("act", ACT(V1[v1c][:, :, 0:64], k2(kvr[:, 128:256]), AF.Copy), r=[kvrn], pw=["V1_%d" % v1c])
                    if own:
                        qr, qrn = qraws[par], "qraw%d" % par
                        q8 = lambda ap: ap.rearrange("p (h d) -> p h d", h=8)
                        for h_ in range(8):
                            T.add("act", ACT(pT[0][:, h_ * 64:(h_ + 1) * 64], qr[:, h_ * 64:(h_ + 1) * 64], AF.Square,
                                             accum_out=qss[:, h_:h_ + 1]), r=[qrn], pw=["pT0", "qss"])
                        T.add("act", ACT(qss, qss, AF.Ln, scale=1.0 / 64, bias=EPS), r=["qss"], w=["qss"])
                        T.add("act", ACT(qrs, qss, AF.Exp, scale=-0.5), r=["qss"], w=["qrs"])
                        T.add("pool", TT(q8(qr[:]), q8(qr[:]), qrs.unsqueeze(2).to_broadcast([128, 8, 64]), ALU.mult),
                              r=[qrn, "qrs"], w=[qrn])
                        T.add("pool", TT(q8(qn[:]), q8(qr[:]), gainQ8, ALU.mult), r=[qrn, "qg"], w=[qnn])
                    first = True
                    for g in range(2):
                        for var in range(2):
                            i = g * 2 + var
                            T.add("pe", TR(BT[:, i * 128:(i + 1) * 128], kpad[:, g, var, :], identb[:]), glue=(not first),
                                  r=[kpadn, "identb"], pw=["BT"])
                            first = False
                    if not own:
                        T.add("act", ACT(kTp[kcur][:], v4(BT[:, 0:512]), AF.Copy), glue=True, r=["BT"], w=["kTp%d" % kcur])
                        return
                    for p_ in range(4):
                        T.add("pe", TR(BT[:, 512 + p_ * 128:512 + (p_ + 1) * 128], qn[:, p_ * 128:(p_ + 1) * 128], identb[:]), glue=True,
                              r=[qnn, "identb"], pw=["BT"])
                    T.add("act", ACT(kTp[kcur][:], v4(BT[:, 0:512]), AF.Copy), glue=True, r=["BT"], w=["kTp%d" % kcur])
                    T.add("act", ACT(qTp[:], v4(BT[:, 512:1024]), AF.Copy), glue=True, r=["BT"], w=["qTp"])
                    for g in range(2):
                        for (kk, b8, b8n, pTi, pTn, is_prev) in ((kprev, biasP8, "biasP8", pT[0], "pT0", True),
                                                               (kcur, biasC8, "biasC8", pT[1], "pT1", False)):
                            for j_ in range(4):
                                h = 4 * g + j_
                                T.add("pe", MM(B[0][:, j_ * 128:(j_ + 1) * 128], kTp[kk][:, g * 2 + h % 2, :], qTp[:, h // 2, :],
                                               j_ == 0, False, skip=True), r=["kTp%d" % kk, "qTp"], pw=["B0"])
                            T.add("pe", MM(B[0][:], identb[:], b8[:, g * 512:(g + 1) * 512], False, True, skip=True),
                                  r=["identb", b8n], pw=["B0"])
                            if is_prev and ob == 0:
                                T.add("act", ACT(pTi[:], B[0][:], AF.Exp, scale=0.125, bias=pm[:, 0:1]), r=["B0", "pm"], w=[pTn])
                            else:
                                T.add("act", ACT(pTi[:], B[0][:], AF.Exp, scale=0.125), r=["B0"], w=[pTn])
                        for j_ in range(4):
                            T.add("pe", MM(B[0][:, j_ * 65:(j_ + 1) * 65], pT[0][:, j_ * 128:(j_ + 1) * 128], V1[v1p][:, g, 0:65], True, False),
                                  r=["pT0", "V1_%d" % v1p], pw=["B0"])
                            T.add("pe", MM(B[0][:, j_ * 65:(j_ + 1) * 65], pT[1][:, j_ * 128:(j_ + 1) * 128], V1[v1c][:, g, 0:65], False, True),
                                  glue=True, r=["pT1", "V1_%d" % v1c], pw=["B0"])
                        pvv = B[0][:, 0:260].rearrange("p (h d) -> p h d", h=4)
                        T.add("dve", TT(den.unsqueeze(2), pvv[:, :, 64:65], esink[:, 4 * g:4 * g + 4].unsqueeze(2), ALU.add),
                              r=["B0", "esink"], w=["den"])
                        T.add("dve", RC(den, den), r=["den"], w=["den"])
                        T.add("dve", TT(yB[:, g * 256:(g + 1) * 256].rearrange("p (h d) -> p h d", h=4), pvv[:, :, 0:64],
                                        den.unsqueeze(2).to_broadcast([128, 4, 64]), ALU.mult), r=["B0", "den"], pw=["yB"])
                    for p_ in range(4):
                        T.add("pe", TR(BT[:, p_ * 128:(p_ + 1) * 128], yB[:, p_ * 128:(p_ + 1) * 128], identb[:]), glue=(p_ > 0),
                              r=["yB", "identb"], pw=["BT"])
                    T.add("act", ACT(yBT[:, :, tsl], v4(BT[:, 0:512]), AF.Copy), glue=True, r=["BT"], w=["yBT%d" % ob])

                prev = None
                for sbk in range(NSB):
                    x_in_t2 = (sbk + 1 < NSB)
                    for j in range(4):
                        t1 = Thr()
                        if j == 0:
                            if sbk == 0:
                                stage_X(t1, 0)
                            stage_F(t1, sbk)
                            if (not x_in_t2) and sbk + 1 < NSB:
                                stage_X(t1, sbk + 1)
                        stage_T1(t1, sbk, j)
                        if prev is None:
                            merge(t1)
                        else:
                            merge(t1, prev[0], prev[1])
                        prev = (Thr(), Thr())
                        stage_T2(prev[0], sbk, j)
                        stage_T2b(prev[1], sbk, j)
                        if x_in_t2 and j < 3:
                            stage_X(prev[0], sbk + 1, (j,) if j < 2 else (2, 3))
                if prev is not None:
                    merge(prev[0], prev[1])
                P.barrier()
                P.emit(nc, semf)

            sb1 = contextlib.ExitStack()
            with sb1:
                P = Prog("b")
                wgA = sbt(sb1, "wgA", [128, 8, D], BF16)
                wgB = sbt(sb1, "wgB", [128, 8, D], BF16)
                wA = sbt(sb1, "wA", [128, 4, D], BF16)
                wB = sbt(sb1, "wB", [128, 4, D], BF16)
                sA = sbt(sb1, "sA", [128, 512], F32)
                sB = sbt(sb1, "sB", [128, 512], F32)
                t1 = sbt(sb1, "t1", [128, 512], F32)
                t2 = sbt(sb1, "t2", [128, 512], F32)
                mtiles = [sbt(sb1, "mtile%d" % i, [128, 8, 512], BF16) for i in range(2)]
                w_in_v = w_in_d.rearrange("(kc p) n -> p kc n", p=128)
                w_a_v = w_a_d.rearrange("(kc p) n -> p kc n", p=128)
                w_b_v = w_b_d.rearrange("(kc p) n -> p kc n", p=128)
                pieces = ((0, 128), (128, 512), (512, 1024))
                def piece_of(m):
                    return 0 if m == 0 else (1 if m < 4 else 2)
                for pi, (c0, c1) in enumerate(pieces):
                    cs = slice(c0, c1)
                    P.add("pool", DMA(wgA[:, :, cs], w_in_v[:, :, 2824 + c0:2824 + c1]), w=["wgA%d" % pi], dma="wgA%d" % pi)
                    P.add("pool", DMA(wgB[:, :, cs], w_in_v[:, :, 3848 + c0:3848 + c1]), w=["wgB%d" % pi], dma="wgB%d" % pi)
                    P.add("pool", DMA(wA[:, :, cs], w_a_v[:, :, cs]), w=["wA%d" % pi], dma="wA%d" % pi)
                    P.add("pool", DMA(wB[:, :, cs], w_b_v[:, :, cs]), w=["wB%d" % pi], dma="wB%d" % pi)
                wg_vb = w_gate_d.rearrange("(kc p) n -> p kc n", p=128)
                wu_vb = w_up_d.rearrange("(kc p) n -> p kc n", p=128)
                for fb in range(NF):
                    fsl = slice(fb * 128, (fb + 1) * 128)
                    wcr = wcache[fb * 128:(fb + 1) * 128, :]
                    P.add("pool", DMA(wcr[:, 0:1024].rearrange("p (kc n) -> p kc n", kc=8), wg_vb[:, :, fsl]), dma="wcf")
                    P.add("pool", DMA(wcr[:, 1024:2048].rearrange("p (kc n) -> p kc n", kc=8), wu_vb[:, :, fsl]), dma="wcf")
                wo_vb = w_out_d.rearrange("(kc p) n -> p kc n", p=128)
                wd_vb = w_down_d.rearrange("(f p) n -> p f n", p=128)
                woc3 = wocache.rearrange("p (kc n) -> p kc n", kc=8)
                wdc3 = wdcache.rearrange("p (f n) -> p f n", f=NF)
                for hf in range(2):
                    cs_ = slice(hf * 512, (hf + 1) * 512)
                    P.add("pool", DMA(woc3[:, :, cs_], wo_vb[:, :, cs_]), dma="wcf")
                for hf in range(2):
                    cs_ = slice(hf * 512, (hf + 1) * 512)
                    for (f0, f1) in ((0, 8), (8, 16), (16, 22)):
                        P.add("pool", DMA(wdc3[:, f0:f1, cs_], wd_vb[:, f0:f1, cs_]), dma="wcf")
                it = 0
                for tt_ in range(int(os.environ.get('K_NB1', 4))):
                    ts_ = slice(tt_ * 512, (tt_ + 1) * 512)
                    for m in range(8):
                        ms = slice(m * 128, (m + 1) * 128)
                        hf = piece_of(m)
                        if it % 2 == 0:
                            bgA, bgB, bpA, bpB = B[0], B[1], B[2], B[6]
                            nA, nB, nPA, nPB = "b0", "b1", "b2", "b6"
                        else:
                            bgA, bgB, bpA, bpB = B[3], B[4], B[5], B[6]
                            nA, nB, nPA, nPB = "b3", "b4", "b5", "b6"
                        for kc in range(8):
                            P.add("pe", MM(bgA[:], wgA[:, kc, ms], hTm[:, kc, ts_], kc == 0, kc == 7), r=["wgA%d" % hf, "hTm%d" % tt_], pw=[nA])
                        for kc in range(8):
                            P.add("pe", MM(bgB[:], wgB[:, kc, ms], hTm[:, kc, ts_], kc == 0, kc == 7), r=["wgB%d" % hf, "hTm%d" % tt_], pw=[nB])
                        for kc in range(4):
                            P.add("pe", MM(bpA[:], wA[:, kc, ms], yAT[:, kc, ts_], kc == 0, kc == 3), r=["wA%d" % hf], pw=[nPA])
                        for kc in range(4):
                            P.add("pe", MM(bpB[:], wB[:, kc, ms], yBT[:, kc, ts_], kc == 0, kc == 3), r=["wB%d" % hf], pw=[nPB])
                        P.add("act", ACT(sA[:], bgA[:], AF.Sigmoid), r=[nA], w=["sA"])
                        P.add("act", ACT(sB[:], bgB[:], AF.Sigmoid), r=[nB], w=["sB"])
                        P.add("dve", TT(t1[:], sA[:], bpA[:], ALU.mult), r=["sA", nPA], w=["t1"])
                        P.add("dve", TT(t2[:], sB[:], bpB[:], ALU.mult), r=["sB", nPB], w=["t2"])
                        P.add("dve", TT(mtiles[tt_ % 2][:, m, :], t1[:], t2[:], ALU.add), r=["t1", "t2"], pw=["mtile%d" % (tt_ % 2)])
                        it += 1
                    P.add("act", ACT(hTm[:, :, ts_], mtiles[tt_ % 2][:], AF.Copy), r=["mtile%d" % (tt_ % 2)], w=["hTm%d" % tt_])
                P.barrier()
                P.emit(nc, semf)

        s2 = contextlib.ExitStack()
        with s2:
            P = Prog("c")
            wout = sbt(s2, "wout", [128, 8, D], BF16)
            wdown = sbt(s2, "wdown", [128, NF, D], BF16)
            gainF = sbt(s2, "gainF", [128, D], F32)
            xin2 = [sbt(s2, "xin2_%d" % i, [128, D], F32) for i in range(2)]
            x1 = [sbt(s2, "x1_%d" % i, [128, 4, D], F32) for i in range(2)]
            h2bf = [sbt(s2, "h2bf%d" % i, [128, D], BF16) for i in range(2)]
            h2T = [sbt(s2, "h2T%d" % i, [128, 8, 512], BF16) for i in range(2)]
            actT = sbt(s2, "actT", [128, NF, 512], BF16)
            wgu = [sbt(s2, "wgu%d" % i, [128, 16, 128], BF16) for i in range(3)]
            sg = [sbt(s2, "sg%d" % i, [128, 512], F32) for i in range(2)]
            ost = [sbt(s2, "ost%d" % i, [128, D], F32) for i in range(2)]
            cc = [sbt(s2, "cc%d" % i, [128, 16], F32) for i in range(3)]
            P.add("sp", DMA(gainF[:], ffn_norm_d[0:1, :].partition_broadcast(128)), w=["gainF"], dma="gF")
            wo_v = w_out_d.rearrange("(kc p) n -> p kc n", p=128)
            P.add("sp", DMA(wout[:].rearrange("p a b -> p (a b)"), wocache[:, :]), w=["wout"], dma="wout")
            wd_v = w_down_d.rearrange("(f p) n -> p f n", p=128)
            wg_v = w_gate_d.rearrange("(kc p) n -> p kc n", p=128)
            wu_v = w_up_d.rearrange("(kc p) n -> p kc n", p=128)
            NT2 = int(os.environ.get('K_NB2', 4))

            def x1_partA(t, blk):
                tb = t * 4 + blk
                xi, xn = xin2[tb % 2], "xin2_%d" % (tb % 2)
                x1t, hb = x1[t % 2], h2bf[tb % 2]
                x1n, hbn = "x1_%d_%d" % (t % 2, blk), "h2bf%d" % (tb % 2)
                P.add("sp", DMA(xi[:], x_d[(NPRE + tb) * 128:(NPRE + tb + 1) * 128, :]), w=[xn], dma=xn)
                for half in range(2):
                    hs = slice(half * 512, (half + 1) * 512)
                    for kc in range(8):
                        P.add("pe", MM(B[half][:], hTm[:, kc, tb * 128:(tb + 1) * 128], wout[:, kc, hs], kc == 0, kc == 7),
                              r=["wout"], pw=["b%d" % half])
                    P.add("dve", TT(x1t[:, blk, hs], xi[:, hs], B[half][:], ALU.add), r=[xn, "b%d" % half], pw=[x1n])
                P.add("act", ACT(hb[:], x1t[:, blk, :], AF.Square, accum_out=cc[0][:, 0:1]), r=[x1n], w=[hbn, "cc0"])
                P.add("act", ACT(cc[1][:, 0:1], cc[0][:, 0:1], AF.Sqrt, scale=1.0 / D, bias=EPS), r=["cc0"], w=["cc1"])
                P.add("dve", RC(cc[2][:, 0:1], cc[1][:, 0:1]), r=["cc1"], w=["cc2"])
                P.add("dve", STT(hb[:], x1t[:, blk, :], cc[2][:, 0:1], gainF[:], ALU.mult, ALU.mult),
                      r=[x1n, "cc2", "gainF"], w=[hbn])

            def x1_partB(t, blk):
                tb = t * 4 + blk
                hb, hbn = h2bf[tb % 2], "h2bf%d" % (tb % 2)
                for kc in range(8):
                    P.add("pe", TR(BT[:, kc * 128:(kc + 1) * 128], hb[:, kc * 128:(kc + 1) * 128], identb[:]),
                          r=[hbn], pw=["BT"])
                P.add("act", ACT(h2T[t % 2][:, :, blk * 128:(blk + 1) * 128], v4(BT[:], 8), AF.Copy), r=["BT"], pw=["h2T%d" % (t % 2)])

            gu_it = [0]

            def gu(t, f):
                it = gu_it[0]
                gu_it[0] += 1
                sl_ = it % 3
                wg, wn = wgu[sl_], "wgu%d" % sl_
                fs = slice(f * 128, (f + 1) * 128)
                wgf = wg[:].rearrange("p a b -> p (a b)")
                P.add("sp", DMA(wgf, wcache[f * 128:(f + 1) * 128, :]), w=[wn], dma=wn)
                bg, bu = (B[2], B[3]) if it % 2 == 0 else (B[4], B[5])
                ng, nu = ("b2", "b3") if it % 2 == 0 else ("b4", "b5")
                hT2, hT2n = h2T[t % 2], "h2T%d" % (t % 2)
                for kc in range(8):
                    P.add("pe", MM(bg[:], wg[:, kc, :], hT2[:, kc, :], kc == 0, kc == 7), r=[wn, hT2n], pw=[ng])
                for kc in range(8):
                    P.add("pe", MM(bu[:], wg[:, 8 + kc, :], hT2[:, kc, :], kc == 0, kc == 7), r=[wn, hT2n], pw=[nu])
                sgi = sg[it % 2]
                P.add("act", ACT(sgi[:], bg[:], AF.Silu), r=[ng], w=["sg%d" % (it % 2)])
                P.add("dve", TT(actT[:, f, :], sgi[:], bu[:], ALU.mult), r=["sg%d" % (it % 2), nu], pw=["actT"])

            def down(t, blk):
                tb = t * 4 + blk
                oi, on = ost[tb % 2], "ost%d" % (tb % 2)
                for half in range(2):
                    hs = slice(half * 512, (half + 1) * 512)
                    for f in range(NF):
                        P.add("pe", MM(B[half][:], actT[:, f, blk * 128:(blk + 1) * 128], wdown[:, f, hs], f == 0, f == NF - 1),
                              r=["actT", "wdown"], pw=["b%d" % half])
                    P.add("dve", TT(oi[:, hs], x1[t % 2][:, blk, hs], B[half][:], ALU.add),
                          r=["x1_%d_%d" % (t % 2, blk), "b%d" % half], pw=[on])
                P.add("sp", DMA(out_d[tb * 128:(tb + 1) * 128, :], oi[:]), r=[on], dma=on)

            for blk in range(4):
                x1_partA(0, blk)
                x1_partB(0, blk)
            wdf = wdown[:].rearrange("p a b -> p (a b)")
            for i_, (c0_, c1_) in enumerate(((0, 8 * D), (8 * D, 16 * D), (16 * D, NF * D))):
                P.add("sp", DMA(wdf[:, c0_:c1_], wdcache[:, c0_:c1_]), pw=["wdown"], dma="wd%d" % i_)
            for t in range(NT2):
                for f in range(NF):
                    gu(t, f)
                    if t + 1 < NT2:
                        if f in (1, 6, 11, 16):
                            x1_partA(t + 1, (f - 1) // 5)
                        if f in (4, 9, 14, 19):
                            x1_partB(t + 1, (f - 4) // 5)
                for blk in range(4):
                    down(t, blk)
            P.barrier()
            P.emit(nc, semf)
    return nc


def _t5_bucket(dist):
    n = np.maximum(dist, 0)
    max_exact = 16
    nf = np.maximum(n, 1).astype(np.float32)
    large = max_exact + (np.log(nf / max_exact) / math.log(128 / max_exact) * (32 - max_exact)).astype(np.int32)
    large = np.minimum(large, 31)
    return np.where(n < max_exact, n, large)


_NC_CACHE = {}


def kernel(x, attn_norm, w_in, dn_conv, dn_a_log, dn_dt_bias, dn_out_norm, swa_q_norm, swa_k_norm,
           swa_sinks, rel_bias, w_branch_dn, w_branch_swa, w_out, ffn_norm, w_gate, w_up, w_down):
    f = lambda a: np.ascontiguousarray(np.asarray(a, dtype=np.float32))
    x = f(x)
    if "nc" not in _NC_CACHE:
        _NC_CACHE["nc"] = build()
    nc = _NC_CACHE["nc"]
    idx = np.arange(128)
    same = (idx[:, None] // 64) == (idx[None, :] // 64)
    U = ((idx[:, None] <= idx[None, :]) & same).astype(np.float32)
    Urev = ((idx[:, None] > idx[None, :]) & same).astype(np.float32)
    NEGM = np.where((idx[None, :] >= idx[:, None]) & same, 0.0, -1e5).astype(np.float32)
    STRICT = ((idx[None, :] > idx[:, None]) & same).astype(np.float32)
    m01 = np.stack([(idx < 64), (idx >= 64)], 1).astype(np.float32)
    ident = np.eye(128, dtype=np.float32)
    rb = f(rel_bias)
    s_i = idx[:, None]
    q_i = idx[None, :]
    dist_prev = 128 + q_i - s_i
    dist_cur = q_i - s_i
    biasP = np.empty((128, 8, 128), np.float32)
    biasC = np.empty((128, 8, 128), np.float32)
    bk_p = _t5_bucket(dist_prev)
    bk_c = _t5_bucket(dist_cur)
    ok_p = (dist_prev >= 0) & (dist_prev < 128)
    ok_c = (dist_cur >= 0) & (dist_cur < 128)
    for h in range(8):
        biasP[:, h, :] = np.where(ok_p, rb[bk_p, h], NEGBIG)
        biasC[:, h, :] = np.where(ok_c, rb[bk_c, h], NEGBIG)
    biasP = biasP.reshape(128, 1024)
    biasC = biasC.reshape(128, 1024)
    common = {
        "w_in": f(w_in[0]), "w_a": f(w_branch_dn[0]), "w_b": f(w_branch_swa[0]), "w_out": f(w_out[0]),
        "w_gate": f(w_gate[0]), "w_up": f(w_up[0]), "w_down": f(w_down[0]),
        "attn_norm": f(attn_norm), "ffn_norm": f(ffn_norm), "dn_conv": f(dn_conv[0]),
        "a_log": f(dn_a_log), "dt_bias": f(dn_dt_bias), "out_norm": f(dn_out_norm),
        "q_norm": f(swa_q_norm), "k_norm": f(swa_k_norm), "sinks": f(swa_sinks),
        "biasP": biasP, "biasC": biasC, "ident": ident, "U": U, "Urev": Urev, "NEGM": NEGM,
        "STRICT": STRICT, "m01": m01,
    }
    in_maps = []
    for core in range(8):
        bt, half = core // 2, core % 2
        xe = np.zeros((NBLK * 128, D), np.float32)
        if half == 1:
            xe[:] = x[bt]
        else:
            xe[NPRE * 128:] = x[bt, :NOWN * 128]
        pmv = np.full((128, 1), NEGBIG if half == 0 else 0.0, np.float32)
        d = dict(common)
        d["x"] = xe
        d["pm"] = pmv
        in_maps.append(d)
    res = run_bass_kernel_spmd(nc, in_maps, core_ids=list(range(8)))
    out = np.empty((4, 4096, D), np.float32)
    for core in range(8):
        bt, half = core // 2, core % 2
        out[bt, half * 2048:(half + 1) * 2048] = res.results[core]["out"]
    return out
```

```python
import contextlib
import os
import math
import numpy as np
import ml_dtypes
import concourse.bass as bass
import concourse.mybir as mybir
from concourse.bass_utils import run_bass_kernel_spmd

F32 = mybir.dt.float32
BF16 = mybir.dt.bfloat16
AF = mybir.ActivationFunctionType
ALU = mybir.AluOpType
AX = mybir.AxisListType

D = 1024
DIN = 4872
DFF = 2816
NF = 22
NPRE = 16
NOWN = 16
NBLK = NPRE + NOWN
EPS = 1e-6
NEGBIG = -30000.0


class Op:
    __slots__ = ("eng", "fn", "dma", "dma_val", "waits", "signal", "sigval", "uid", "group")

    def __init__(self, eng, fn, dma):
        self.eng = eng
        self.fn = fn
        self.dma = dma
        self.dma_val = 0
        self.waits = []
        self.signal = False
        self.sigval = 0
        self.group = False


class Prog:
    ENGS = ("pe", "act", "dve", "pool", "sp")

    def __init__(self, tag):
        self.tag = tag
        self.ops = {e: [] for e in self.ENGS}
        self.res = {}
        self.dma_cnt = {}
        self.dma_last = {}
        self.dma_uses = {}
        self.bank_last = {}
        self.uid = 0

    def _st(self, name):
        st = self.res.get(name)
        if st is None:
            st = {"w": {}, "r": {}}
            self.res[name] = st
        return st

    def _dep(self, op, prev, kind):
        if prev is op:
            return
        if prev.dma is None and op.dma is None and prev.eng == op.eng and kind != "raw":
            return
        if prev in op.waits:
            return
        op.waits.append(prev)
        if prev.dma is None:
            prev.signal = True

    def add(self, eng, fn, r=(), w=(), pw=(), dma=None, group=False):
        if dma is not None and not group:
            n = self.dma_uses.get(dma, 0)
            self.dma_uses[dma] = n + 1
            dma = "%s_%d" % (dma, n // 14)
        op = Op(eng, fn, dma)
        op.group = group
        self.uid += 1
        op.uid = self.uid
        key = eng if dma is None else ("dma", self.uid)
        for name in r:
            for p in self._st(name)["w"].values():
                self._dep(op, p, "raw")
        for name in tuple(w) + tuple(pw):
            st = self._st(name)
            for p in st["w"].values():
                self._dep(op, p, "waw")
            for p in st["r"].values():
                self._dep(op, p, "war")
        banks = set()
        for name in tuple(r) + tuple(w) + tuple(pw):
            if name == "BT":
                banks.add("T")
            elif name[0] in "Bb" and name[1:2].isdigit():
                banks.add(name[1])
        for bk in banks:
            bl = self.bank_last.setdefault(bk, {})
            for e2, p in bl.items():
                if e2 != eng:
                    self._dep(op, p, "bank")
            bl[eng] = op
        for name in r:
            self._st(name)["r"][key] = op
        for name in w:
            st = self._st(name)
            st["w"] = {key: op}
            st["r"] = {}
        for name in pw:
            st = self._st(name)
            st["w"][key] = op
            st["r"] = {}
        if dma is not None:
            self.dma_cnt[dma] = self.dma_cnt.get(dma, 0) + 16
            op.dma_val = self.dma_cnt[dma]
            self.dma_last[dma] = op
        self.ops[eng].append(op)
        return op

    def barrier(self):
        lasts = {}
        for e in self.ENGS:
            for op in reversed(self.ops[e]):
                if op.fn is not None and op.dma is None:
                    lasts[e] = op
                    break
        dl = list(self.dma_last.values())
        for e in self.ENGS:
            b = Op(e, None, None)
            for e2, op in lasts.items():
                if e2 != e:
                    b.waits.append(op)
                    op.signal = True
            b.waits.extend(dl)
            self.ops[e].append(b)

    def emit(self, nc, semf):
        for e in self.ENGS:
            cnt = 0
            for op in self.ops[e]:
                if op.dma is None and op.signal:
                    cnt += 1
                    op.sigval = cnt
        if os.environ.get('K_DBG'):
            print('SIG', self.tag, {e: max([o.sigval for o in self.ops[e]] + [0]) for e in self.ENGS}, {e: len(self.ops[e]) for e in self.ENGS}, self.dma_cnt)
        esem = {e: semf(self.tag + "e" + e) for e in self.ENGS}
        dsem = {k: semf(self.tag + "d" + str(k)) for k in self.dma_cnt}
        prog = self

        def run(en, eng):
            known = {}
            for op in prog.ops[en]:
                for p in op.waits:
                    if p.dma is None:
                        sem, val, k = esem[p.eng], p.sigval, "e" + p.eng
                    else:
                        sem, val, k = dsem[p.dma], (prog.dma_cnt[p.dma] if p.group else p.dma_val), "d" + str(p.dma)
                    if known.get(k, 0) >= val:
                        continue
                    eng.wait_ge(sem, val)
                    known[k] = val
                if op.fn is None:
                    continue
                ins = op.fn(eng)
                if op.dma is not None:
                    ins.then_inc(dsem[op.dma], 16)
                elif op.signal:
                    ins.then_inc(esem[en], 1)

        with nc.Block() as block:
            @block.tensor
            def _(t):
                run("pe", t)

            @block.scalar
            def _(t):
                run("act", t)

            @block.vector
            def _(t):
                run("dve", t)

            @block.gpsimd
            def _(t):
                run("pool", t)

            @block.sync
            def _(t):
                run("sp", t)


def MM(out, lhsT, rhs, st=True, sp=True, skip=False):
    if skip:
        return lambda e: e.matmul(out, lhsT=lhsT, rhs=rhs, start=st, stop=sp, skip_group_check=True)
    return lambda e: e.matmul(out, lhsT=lhsT, rhs=rhs, start=st, stop=sp)


def TR(out, in_, ident):
    return lambda e: e.transpose(out=out, in_=in_, identity=ident)


def ACT(out, in_, func, **kw):
    return lambda e: e.activation(out=out, in_=in_, func=func, **kw)


def TT(out, a, b, op):
    return lambda e: e.tensor_tensor(out=out, in0=a, in1=b, op=op)


def TS(out, a, s1, op0, s2=None, op1=None):
    if op1 is None:
        return lambda e: e.tensor_scalar(out=out, in0=a, scalar1=s1, scalar2=None, op0=op0)
    return lambda e: e.tensor_scalar(out=out, in0=a, scalar1=s1, scalar2=s2, op0=op0, op1=op1)


def STT(out, a, s, b, op0, op1):
    return lambda e: e.scalar_tensor_tensor(out=out, in0=a, scalar=s, in1=b, op0=op0, op1=op1)


def CP(out, in_):
    return lambda e: e.tensor_copy(out=out, in_=in_)


def RC(out, in_):
    return lambda e: e.reciprocal(out=out, in_=in_)


def RED(out, in_):
    return lambda e: e.tensor_reduce(out=out, in_=in_, axis=AX.X, op=ALU.add)


def DMA(out, in_, **kw):
    return lambda e: e.dma_start(out=out, in_=in_, **kw)


def MSET(ap, v):
    return lambda e: e.memset(ap, v)


def v4(ap, h=4):
    return ap.rearrange("p (h d) -> p h d", h=h)


def build():
    nc = bass.Bass("TRN2", target_bir_lowering=False)

    def din(name, shape):
        return nc.dram_tensor(name, shape, F32, kind="ExternalInput").ap()

    x_d = din("x", [NBLK * 128, D])
    w_in_d = din("w_in", [D, DIN])
    w_a_d = din("w_a", [512, D])
    w_b_d = din("w_b", [512, D])
    w_out_d = din("w_out", [D, D])
    w_gate_d = din("w_gate", [D, DFF])
    w_up_d = din("w_up", [D, DFF])
    w_down_d = din("w_down", [DFF, D])
    attn_norm_d = din("attn_norm", [1, D])
    ffn_norm_d = din("ffn_norm", [1, D])
    conv_d = din("dn_conv", [4, 1536])
    alog_d = din("a_log", [1, 4])
    dtb_d = din("dt_bias", [1, 4])
    onorm_d = din("out_norm", [1, 128])
    qg_d = din("q_norm", [1, 64])
    kg_d = din("k_norm", [1, 64])
    sinks_d = din("sinks", [1, 8])
    biasP_d = din("biasP", [128, 1024])
    biasC_d = din("biasC", [128, 1024])
    pm_d = din("pm", [128, 1])
    ident_d = din("ident", [128, 128])
    U_d = din("U", [128, 128])
    Urev_d = din("Urev", [128, 128])
    NEG_d = din("NEGM", [128, 128])
    STR_d = din("STRICT", [128, 128])
    m01_d = din("m01", [128, 2])
    out_d = nc.dram_tensor("out", [NOWN * 128, D], F32, kind="ExternalOutput").ap()
    wcache = nc.dram_tensor("wcache", [NF * 128, 16 * 128], BF16, kind="Internal").ap()
    wocache = nc.dram_tensor("wocache", [128, 8 * D], BF16, kind="Internal").ap()
    wdcache = nc.dram_tensor("wdcache", [128, NF * D], BF16, kind="Internal").ap()

    top = contextlib.ExitStack()
    with top:
        def sbt(es, name, shape, dt):
            return es.enter_context(nc.sbuf_tensor("s_" + name, shape, dt))

        def semf(name):
            return top.enter_context(nc.semaphore(name))

        B = [top.enter_context(nc.psum_tensor("B%d" % i, [128, 512], F32)) for i in range(7)]
        BT = top.enter_context(nc.psum_tensor("BT", [128, 1024], BF16))

        hTm = sbt(top, "hTm", [128, 8, NOWN * 128], BF16)
        identf = sbt(top, "identf", [128, 128], F32)
        identb = sbt(top, "identb", [128, 128], BF16)

        s1 = contextlib.ExitStack()
        with s1:
            yAT = sbt(s1, "yAT", [128, 4, NOWN * 128], BF16)
            yBT = sbt(s1, "yBT", [128, 4, NOWN * 128], BF16)

            sa = contextlib.ExitStack()
            with sa:
                P = Prog("a")
                NWT = 1288
                CZ, CBA, CQ, CKV = 0, 512, 520, 1032
                winT = sbt(sa, "winT", [128, 8, NWT], BF16)
                wqs = [sbt(sa, "wqs%d" % i, [128, 8, 128], BF16) for i in range(3)]
                Uc = sbt(sa, "Uc", [128, 128], F32)
                Urev = sbt(sa, "Urev", [128, 128], F32)
                NEGM = sbt(sa, "NEGM", [128, 128], F32)
                STR = sbt(sa, "STR", [128, 128], F32)
                m01 = sbt(sa, "m01", [128, 2], F32)
                onesb = sbt(sa, "onesb", [128, 128], BF16)
                gainT = sbt(sa, "gainT", [128, 8], F32)
                wconv = sbt(sa, "wconv", [128, 4, 12], F32)
                alog = sbt(sa, "alog", [128, 4], F32)
                dtb = sbt(sa, "dtb", [128, 4], F32)
                negA = sbt(sa, "negA", [128, 4], F32)
                onorm = sbt(sa, "onorm", [128, 128], F32)
                qg = sbt(sa, "qg", [128, 64], F32)
                kg = sbt(sa, "kg", [128, 64], F32)
                sinks = sbt(sa, "sinks", [128, 8], F32)
                esink = sbt(sa, "esink", [128, 8], F32)
                biasP8 = sbt(sa, "biasP8", [128, 1024], BF16)
                biasC8 = sbt(sa, "biasC8", [128, 1024], BF16)
                pm = sbt(sa, "pm", [128, 1], F32)
                dg = [sbt(sa, "dg%d" % i, [128, 4, 128], BF16) for i in range(3)]
                xin = sbt(sa, "xin", [128, D], F32)
                hbf = sbt(sa, "hbf", [128, D], BF16)
                xct = [sbt(sa, "xct%d" % i, [128, 516], BF16) for i in range(2)]
                halo = sbt(sa, "halo", [128, 12, 4], BF16)
                sil = [sbt(sa, "sil%d" % i, [128, 512], F32) for i in range(4)]
                vT = sbt(sa, "vT", [128, 4, 512], BF16)
                sqs = [sbt(sa, "sq%d" % i, [128, 512], BF16) for i in range(2)]
                rt = [sbt(sa, "rt%d" % i, [128, 512], F32) for i in range(2)]
                qkn = sbt(sa, "qkn", [128, 8, 512], BF16)
                kvraws = [sbt(sa, "kvraw%d" % i, [128, 256], F32) for i in range(2)]
                qraws = [sbt(sa, "qraw%d" % i, [128, 512], F32) for i in range(2)]
                Fb = [sbt(sa, "F%d" % i, [128, 4, 128], F32) for i in range(4)]
                gstage = Fb[0][0:8, 0, :]
                cstage = Fb[1][0:48, 0, :]
                SBt = sbt(sa, "SBt", [128, 4, 128], F32)
                Gs = [sbt(sa, "G%d" % i, [128, 4, 128], F32) for i in range(4)]
                EGBs = [sbt(sa, "EGB%d" % i, [128, 4, 128], F32) for i in range(2)]
                Xc = [sbt(sa, "Xc%d" % i, [128, 4, 128], F32) for i in range(2)]
                Yc = [sbt(sa, "Yc%d" % i, [128, 4, 128], F32) for i in range(2)]
                Pm = sbt(sa, "Pm", [128, 4, 128], F32)
                zss = [sbt(sa, "zs%d" % i, [128, 4, 128], F32) for i in range(2)]
                KgTs = [sbt(sa, "KgT%d" % i, [128, 4, 128], BF16) for i in range(2)]
                qgTs = [sbt(sa, "qgT%d" % i, [128, 4, 128], BF16) for i in range(2)]
                QKTs = [sbt(sa, "QKT%d" % i, [128, 4, 128], BF16) for i in range(2)]
                TTbs = [sbt(sa, "TTb%d" % i, [128, 4, 128], BF16) for i in range(2)]
                kvtoks = [sbt(sa, "kvtok%d" % i, [128, 8, 128], BF16) for i in range(2)]
                Rb = sbt(sa, "Rb", [128, 4, 128], BF16)
                vd = sbt(sa, "vd", [128, 4, 128], BF16)
                vn = sbt(sa, "vn", [128, 4, 128], BF16)
                S = sbt(sa, "S", [128, 4, 128], F32)
                Sbf = sbt(sa, "Sbf", [128, 4, 128], BF16)
                yA = sbt(sa, "yA", [128, 512], BF16)
                qn1 = sbt(sa, "qn0", [128, 512], BF16)
                qns = [qn1, qn1]
                kt = sbt(sa, "kt", [128, 128], F32)
                kpad1 = sbt(sa, "kpad0", [128, 2, 2, 128], BF16)
                kpads = [kpad1, kpad1]
                V1 = [sbt(sa, "V1_%d" % i, [128, 2, 66], BF16) for i in range(3)]
                kTp = [sbt(sa, "kTp%d" % i, [128, 4, 128], BF16) for i in range(2)]
                qTp = sbt(sa, "qTp", [128, 4, 128], BF16)
                pT = [sbt(sa, "pT%d" % i, [128, 512], BF16) for i in range(2)]
                yB = sbt(sa, "yB", [128, 512], BF16)

                def colt(name, n):
                    return sbt(sa, name, [128, 16], F32)[:, 0:n]
                c_ss, c_rt, c_rstd = colt("c_ss", 1), colt("c_rt", 1), colt("c_rstd", 1)
                betas = [colt("beta0", 4), colt("beta1", 4)]
                gcol = colt("gcol", 4)
                xa = colt("xa", 4)
                ngc = colt("ngc", 4)
                dk = colt("dk", 4)
                bdms = [[colt("bdm0_0", 4), colt("bdm1_0", 4)], [colt("bdm0_1", 4), colt("bdm1_1", 4)]]
                oss = colt("oss", 4)
                ors = colt("ors", 4)
                qss = colt("qss", 8)
                qrs = colt("qrs", 8)
                kss = colt("kss", 2)
                krs = colt("krs", 2)
                den = colt("den", 4)
                if os.environ.get("K_DBG"):
                    print("SBUF remaining phase A", nc.sbuf_bytes_remaining)

                P.add("sp", DMA(identf[:], ident_d), w=["identf"], dma="cs0", group=True)
                P.add("sp", DMA(Uc[:], U_d), w=["Uc"], dma="cs0", group=True)
                P.add("sp", DMA(Urev[:], Urev_d), w=["Urev"], dma="cs0", group=True)
                P.add("sp", DMA(NEGM[:], NEG_d), w=["NEGM"], dma="cs0", group=True)
                P.add("sp", DMA(STR[:], STR_d), w=["STR"], dma="cs0", group=True)
                P.add("sp", DMA(m01[:], m01_d), w=["m01"], dma="cs0", group=True)
                P.add("sp", DMA(pm[:], pm_d), w=["pm"], dma="cs0", group=True)
                P.add("sp", DMA(gstage, attn_norm_d.rearrange("o (kc p) -> (o kc) p", p=128)), w=["gstage", "F0"], dma="cs0", group=True)
                P.add("sp", DMA(cstage, conv_d.rearrange("j (c p) -> (j c) p", p=128)), w=["cstage", "F1"], dma="cs0", group=True)
                P.add("pe", TR(B[6][:, 0:8], gstage, identf[0:8, 0:8]), r=["gstage", "identf"], pw=["B6"])
                P.add("pe", TR(B[6][:, 64:112], cstage, identf[0:48, 0:48]), r=["cstage", "identf"], pw=["B6"])
                P.add("act", ACT(gainT[:], B[6][:, 0:8], AF.Copy), r=["B6"], w=["gainT"])
                P.add("act", ACT(wconv[:].rearrange("p j c -> p (j c)"), B[6][:, 64:112], AF.Copy), r=["B6"], w=["wconv"])
                P.add("sp", DMA(alog[:], alog_d[0:1, :].partition_broadcast(128)), w=["alog"], dma="cs0", group=True)
                P.add("sp", DMA(dtb[:], dtb_d[0:1, :].partition_broadcast(128)), w=["dtb"], dma="cs0", group=True)
                P.add("sp", DMA(onorm[:], onorm_d[0:1, :].partition_broadcast(128)), w=["onorm"], dma="cs0", group=True)
                P.add("sp", DMA(qg[:], qg_d[0:1, :].partition_broadcast(128)), w=["qg"], dma="cs1", group=True)
                P.add("sp", DMA(kg[:], kg_d[0:1, :].partition_broadcast(128)), w=["kg"], dma="cs1", group=True)
                P.add("sp", DMA(sinks[:], sinks_d[0:1, :].partition_broadcast(128)), w=["sinks"], dma="cs1", group=True)
                for i_, (bd_, b8_, b8n_) in enumerate(((biasP_d, biasP8, "biasP8"), (biasC_d, biasC8, "biasC8"))):
                    for hf_ in range(2):
                        g_ = Gs[i_ * 2 + hf_]
                        gn_ = "G%d" % (i_ * 2 + hf_)
                        P.add("sp", DMA(g_[:].rearrange("p h d -> p (h d)"), bd_[:, hf_ * 512:(hf_ + 1) * 512]), w=[gn_], dma="cs1", group=True)
                        P.add("dve", TS(b8_[:, hf_ * 512:(hf_ + 1) * 512], g_[:].rearrange("p h d -> p (h d)"), 8.0, ALU.mult),
                              r=[gn_], pw=[b8n_])
                w_in_v = w_in_d.rearrange("(kc p) n -> p kc n", p=128)
                for i, c0 in enumerate(range(0, NWT, 512)):
                    c1 = min(c0 + 512, NWT)
                    P.add("pool", DMA(winT[:, :, c0:c1], w_in_v[:, :, 1536 + c0:1536 + c1]), pw=["winT"], dma="w%d" % i)
                P.add("dve", CP(identb[:], identf[:]), r=["identf"], w=["identb"])
                P.add("dve", MSET(onesb[:], 1.0), w=["onesb"])
                P.add("act", ACT(negA[:], alog[:], AF.Exp), r=["alog"], w=["negA"])
                P.add("dve", TS(negA[:], negA[:], -1.0, ALU.mult), r=["negA"], w=["negA"])
                P.add("act", ACT(esink[:], sinks[:], AF.Exp), r=["sinks"], w=["esink"])
                P.add("pool", MSET(kpad1[:], 0.0), w=["kpad0"])
                for i in range(2):
                    P.add("pool", MSET(kTp[i][:], 0.0), w=["kTp%d" % i])
                for i in range(3):
                    P.add("pool", MSET(V1[i][:], 1.0), w=["V1_%d" % i])
                P.add("pool", MSET(S[:], 0.0), w=["S"])
                P.add("pool", MSET(Sbf[:], 0.0), w=["Sbf"])
                P.add("pool", MSET(Rb[:], 0.0), w=["Rb"])
                P.add("pool", MSET(vd[:], 0.0), w=["vd"])
                P.add("pool", MSET(vn[:], 0.0), w=["vn"])
                P.add("pool", MSET(halo[:], 0.0), w=["halo"])

                NEG4 = NEGM[:].unsqueeze(1).to_broadcast([128, 4, 128])
                STR4 = STR[:].unsqueeze(1).to_broadcast([128, 4, 128])
                ID4 = identf[:].unsqueeze(1).to_broadcast([128, 4, 128])
                gainO4 = onorm[:].unsqueeze(1).to_broadcast([128, 4, 128])
                gainQ8 = qg[:].unsqueeze(1).to_broadcast([128, 8, 64])
                gainK2 = kg[:].unsqueeze(1).to_broadcast([128, 2, 64])
                GT8 = gainT[:].unsqueeze(2).to_broadcast([128, 8, 128])
                NSB = int(os.environ.get('K_NSB', NBLK // 4))
                b4 = lambda col: col.unsqueeze(2).to_broadcast([128, 4, 128])

                class Thr:
                    def __init__(self):
                        self.q = []

                    def add(self, eng, fn, glue=False, **kw):
                        self.q.append((eng, fn, kw, glue))

                def take(q, n):
                    k_ = 0
                    while q and (k_ < n or q[0][3]):
                        e_, f_, kw_, _g = q.pop(0)
                        P.add(e_, f_, **kw_)
                        k_ += 1

                def groups(q):
                    out = []
                    for op in q:
                        if out and (op[3] or op[0] == out[-1][-1][0]):
                            out[-1].append(op)
                        else:
                            out.append([op])
                    return out

                def merge(*thrs):
                    gs = [groups(t.q) for t in thrs]
                    for t in thrs:
                        t.q = []
                    idx = [0] * len(gs)
                    while True:
                        best, bestf = -1, 2.0
                        for i_, g_ in enumerate(gs):
                            if idx[i_] < len(g_):
                                f_ = idx[i_] / float(len(g_))
                                if f_ < bestf:
                                    best, bestf = i_, f_
                        if best < 0:
                            break
                        for e_, f_, kw_, _g in gs[best][idx[best]]:
                            P.add(e_, f_, **kw_)
                        idx[best] += 1

                def hT_of(sbk):
                    if sbk >= NPRE // 4:
                        r_ = sbk - NPRE // 4
                    else:
                        r_ = sbk % 2
                    return hTm[:, :, r_ * 512:(r_ + 1) * 512], "hTr%d" % r_

                def stage_X(T, sbk, js=(0, 1, 2, 3)):
                    hT, hTn = hT_of(sbk)
                    for j in js:
                        b = sbk * 4 + j
                        jsl = slice(j * 128, (j + 1) * 128)
                        T.add("sp", DMA(xin[:], x_d[b * 128:(b + 1) * 128, :]), w=["xin"], dma="xin")
                        T.add("act", ACT(hbf[:], xin[:], AF.Square, accum_out=c_ss), r=["xin"], w=["hbf", "c_ss"])
                        T.add("act", ACT(c_rt, c_ss, AF.Ln, scale=1.0 / D, bias=EPS), r=["c_ss"], w=["c_rt"])
                        T.add("act", ACT(c_rstd, c_rt, AF.Exp, scale=-0.5), r=["c_rt"], w=["c_rstd"])
                        T.add("act", ACT(hbf[:], xin[:], AF.Copy, scale=c_rstd), r=["xin", "c_rstd"], w=["hbf"])
                        for kc in range(8):
                            T.add("pe", TR(BT[:, kc * 128:(kc + 1) * 128], hbf[:, kc * 128:(kc + 1) * 128], identb[:]),
                                  glue=(kc > 0), r=["hbf", "identb"], pw=["BT"])
                        T.add("dve", TT(hT[:, :, jsl], v4(BT[:], 8), GT8, ALU.mult), glue=True, r=["BT", "gainT"], pw=[hTn])

                fcnt = [0]

                def stage_F(T, sbk):
                    hT, hTn = hT_of(sbk)
                    own_sb = sbk >= NPRE // 4
                    clist = list(range(12) if (own_sb or sbk == NPRE // 4 - 1) else range(4, 12))
                    cis = []
                    for c in clist:
                        cis.append(fcnt[0])
                        fcnt[0] += 1

                    def st0(c, ci):
                        p2, p3 = ci % 2, ci % 3
                        wq, wqn = wqs[p3], "wqs%d" % p3
                        T.add("pool", DMA(wq[:], w_in_v[:, :, c * 128:(c + 1) * 128]), w=[wqn], dma=wqn)
                        dgi, dgn = dg[p3], "dg%d" % p3
                        for j in range(4):
                            T.add("dve", TS(dgi[:, j, :], identf[:], wconv[:, j, c:c + 1], ALU.mult),
                                  r=["identf", "wconv"], pw=[dgn])
                        pf, pfn = B[2 + p2], "B%d" % (2 + p2)
                        for kc in range(8):
                            T.add("pe", MM(pf[:], wq[:, kc, :], hT[:, kc, :], kc == 0, kc == 7), r=[wqn, hTn], pw=[pfn])

                    def st1(c, ci):
                        p2 = ci % 2
                        pf, pfn = B[2 + p2], "B%d" % (2 + p2)
                        xt, xtn = xct[p2], "xct%d" % p2
                        if ci % 3 == 2:
                            T.add("dve", CP(xt[:, 3:515], pf[:]), r=[pfn], pw=[xtn])
                        else:
                            T.add("act", ACT(xt[:, 3:515], pf[:], AF.Copy), r=[pfn], pw=[xtn])
                        T.add("dve", CP(xt[:, 0:3], halo[:, c, 0:3]), r=["halo", "halo%d" % c], pw=[xtn])

                    def st2(c, ci):
                        p2, p3 = ci % 2, ci % 3
                        dgi, dgn = dg[p3], "dg%d" % p3
                        xt, xtn = xct[p2], "xct%d" % p2
                        pc, pcn = B[4 + p2], "B%d" % (4 + p2)
                        for j in range(4):
                            T.add("pe", MM(pc[:], dgi[:, j, :], xt[:, j:j + 512], j == 0, j == 3), r=[dgn, xtn], pw=[pcn])
                        T.add("dve", CP(halo[:, c, 0:3], xt[:, 512:515]), r=[xtn], w=["halo%d" % c])

                    def st3(c, ci):
                        p2, p4 = ci % 2, ci % 4
                        pc, pcn = B[4 + p2], "B%d" % (4 + p2)
                        sl, sn = sil[p4], "sil%d" % p4
                        T.add("act", ACT(sl[:], pc[:], AF.Exp, scale=-1.0), r=[pcn], w=[sn])
                        T.add("act", ACT(sl[:], sl[:], AF.Ln, bias=1.0), r=[sn], w=[sn])
                        T.add("act", ACT(sl[:], sl[:], AF.Exp, scale=-1.0), r=[sn], w=[sn])

                    def st4(c, ci):
                        p2, p4 = ci % 2, ci % 4
                        pc, pcn = B[4 + p2], "B%d" % (4 + p2)
                        sl, sn = sil[p4], "sil%d" % p4
                        if c >= 8:
                            T.add("dve", TT(vT[:, c - 8, :], sl[:], pc[:], ALU.mult), r=[sn, pcn], pw=["vT"])
                        else:
                            T.add("dve", TT(sl[:], sl[:], pc[:], ALU.mult), r=[sn, pcn], w=[sn])
                            T.add("dve", TT(sqs[p2][:], sl[:], sl[:], ALU.mult), r=[sn], w=["sq%d" % p2])

                    def st5(c, ci):
                        if c >= 8:
                            return
                        p2 = ci % 2
                        T.add("pe", MM(B[6][:], onesb[:], sqs[p2][:]), r=["onesb", "sq%d" % p2], pw=["B6"])
                        rti, rtn = rt[p2], "rt%d" % p2
                        T.add("act", ACT(rti[:], B[6][:], AF.Ln, bias=EPS), r=["B6"], w=[rtn])
                        T.add("act", ACT(rti[:], rti[:], AF.Exp, scale=-0.5), r=[rtn], w=[rtn])

                    def st6(c, ci):
                        if c >= 8:
                            return
                        p2, p4 = ci % 2, ci % 4
                        sl, sn = sil[p4], "sil%d" % p4
                        rti, rtn = rt[p2], "rt%d" % p2
                        scl = (128.0 ** -0.5) if c < 4 else 1.0
                        T.add("dve", STT(qkn[:, c, :], sl[:], scl, rti[:], ALU.mult, ALU.mult), r=[sn, rtn], pw=["qkn"])

                    stages = (st0, st1, st2, st3, st4, st5, st6)
                    n = len(clist)
                    for i in range(n + len(stages) - 1):
                        for k in range(len(stages) - 1, -1, -1):
                            if 0 <= i - k < n:
                                stages[k](clist[i - k], cis[i - k])

                def stage_T1(T, sbk, j):
                    b = sbk * 4 + j
                    own = b >= NPRE
                    par = b % 2
                    jsl = slice(j * 128, (j + 1) * 128)
                    hTf, hTn = hT_of(sbk)
                    hT = hTf[:, :, jsl]
                    has_kv = own or b == NPRE - 1
                    beta, bdm = betas[par], bdms[par]
                    zs, KgT, qgT, QKT, TTb, EGB = zss[par], KgTs[par], qgTs[par], QKTs[par], TTbs[par], EGBs[par]
                    zsn, KgTn, qgTn, QKTn, TTbn, EGBn = ["%s%d" % (n_, par) for n_ in ("zs", "KgT", "qgT", "QKT", "TTb", "EGB")]
                    betan = "beta%d" % par
                    qn, qnn = qns[par], "qn0"
                    kpad, kpadn = kpads[par], "kpad0"
                    kvtok, kvtokn = kvtoks[par], "kvtok%d" % par
                    v1i = b % 3
                    side = []

                    def SD(eng, fn, **kw):
                        side.append((eng, fn, kw))

                    def drain(n):
                        for _ in range(min(n, len(side))):
                            e_, f_, kw_ = side.pop(0)
                            T.add(e_, f_, **kw_)
                    for kc in range(8):
                        T.add("pe", MM(B[2][:, 256:264], hT[:, kc, :], winT[:, kc, CBA:CBA + 8], kc == 0, kc == 7),
                              r=["winT", hTn], pw=["B2"])
                    if has_kv:
                        for kc in range(8):
                            T.add("pe", MM(B[2][:, 0:256], hT[:, kc, :], winT[:, kc, CKV:CKV + 256], kc == 0, kc == 7),
                                  r=["winT", hTn], pw=["B2"])
                    if own:
                        for kc in range(8):
                            T.add("pe", MM(B[3][:], hT[:, kc, :], winT[:, kc, CQ:CQ + 512], kc == 0, kc == 7),
                                  r=["winT", hTn], pw=["B3"])
                        for kc in range(8):
                            T.add("pe", MM(B[4][:], hT[:, kc, :], winT[:, kc, CZ:CZ + 512], kc == 0, kc == 7),
                                  r=["winT", hTn], pw=["B4"])
                    T.add("act", ACT(beta, B[2][:, 256:260], AF.Exp, scale=-1.0), r=["B2"], w=[betan])
                    T.add("act", ACT(beta, beta, AF.Ln, bias=1.0), r=[betan], w=[betan])
                    T.add("act", ACT(beta, beta, AF.Exp, scale=-1.0), r=[betan], w=[betan])
                    T.add("dve", TT(xa, B[2][:, 260:264], dtb[:], ALU.add), r=["B2", "dtb"], w=["xa"])
                    T.add("act", ACT(xa, xa, AF.Exp), r=["xa"], w=["xa"])
                    T.add("act", ACT(xa, xa, AF.Ln, bias=1.0), r=["xa"], w=["xa"])
                    T.add("dve", TT(gcol, xa, negA[:], ALU.mult), r=["xa", "negA"], w=["gcol"])
                    T.add("dve", TT(SBt[:], STR4, b4(beta), ALU.mult), r=["STR", betan], w=["SBt"])

                    Gb, gm, ET, EsT = Fb[0], Fb[1], Fb[2], Fb[3]
                    T.add("dve", CP(Gb[:], b4(gcol)), r=["gcol"], w=["F0"])
                    for h in range(4):
                        T.add("pe", MM(B[5][:, h * 128:(h + 1) * 128], Gb[:, h, :], Uc[:]), r=["F0", "Uc"], pw=["B5"])
                    T.add("pe", MM(B[6][:, 0:4], Uc[:], gcol), r=["Uc", "gcol"], pw=["B6"])
                    T.add("pe", MM(B[6][:, 4:8], Urev[:], gcol), r=["Urev", "gcol"], pw=["B6"])
                    T.add("act", ACT(EGB[:], v4(B[5][:]), AF.Exp), r=["B5"], w=[EGBn])
                    T.add("dve", TS(ngc, B[6][:, 0:4], -1.0, ALU.mult), r=["B6"], w=["ngc"])
                    T.add("act", ACT(dk, B[6][:, 4:8], AF.Exp), r=["B6"], w=["dk"])
                    T.add("dve", TT(gm[:], v4(B[5][:]), NEG4, ALU.add), r=["B5", "NEGM"], w=["F1"])
                    T.add("dve", TT(gm[:], gm[:], b4(ngc), ALU.add), r=["F1", "ngc"], w=["F1"])
                    for h in range(4):
                        T.add("pe", MM(B[6][:, h * 128:(h + 1) * 128], qkn[:, 4 + h, jsl], qkn[:, 4 + h, jsl]),
                              r=["qkn"], pw=["B6"])
                    T.add("act", ACT(ET[:], gm[:], AF.Exp), r=["F1"], w=["F2"])
                    T.add("dve", TT(EsT[:], v4(B[6][:]), SBt[:], ALU.mult), r=["B6", "SBt"], w=["F3"])
                    if own:
                        T.add("act", ACT(zs[:], v4(B[4][:]), AF.Copy), r=["B4"], w=[zsn])
                        T.add("act", ACT(qraws[par][:], B[3][:], AF.Copy), r=["B3"], w=["qraw%d" % par])
                    if has_kv:
                        T.add("act", ACT(kvraws[par][:], B[2][:, 0:256], AF.Copy), r=["B2"], w=["kvraw%d" % par])
                    T.add("dve", TT(KgT[:], qkn[:, 4:8, jsl], EGB[:], ALU.mult), r=["qkn", EGBn], w=[KgTn])
                    if own:
                        T.add("dve", TT(qgT[:], qkn[:, 0:4, jsl], EGB[:], ALU.mult), r=["qkn", EGBn], w=[qgTn])
                    if own:
                        for h in range(4):
                            T.add("pe", MM(B[3][:, h * 128:(h + 1) * 128], qkn[:, 4 + h, jsl], qkn[:, h, jsl]),
                                  r=["qkn"], pw=["B3"])
                    X0, Y0 = Xc[0], Yc[0]
                    T.add("dve", TT(X0[:], ET[:], EsT[:], ALU.mult), r=["F2", "F3"], w=["Xc0"])
                    if own:
                        T.add("dve", TT(QKT[:], v4(B[3][:]), ET[:], ALU.mult), r=["B3", "F2"], w=[QKTn])
                    for c in range(2):
                        T.add("dve", STT(bdm[c], dk, m01[:, c:c + 1], beta, ALU.mult, ALU.mult),
                              r=["dk", "m01", betan], w=["bdm%d_%d" % (c, par)])
                    for h in range(4):
                        T.add("pe", TR(B[6][:, h * 128:(h + 1) * 128], X0[:, h, :], identf[:]), r=["Xc0", "identf"], pw=["B6"])
                    T.add("act", ACT(Y0[:], v4(B[6][:]), AF.Copy), r=["B6"], w=["Yc0"])
                    T.add("dve", TT(Pm[:], ID4, X0[:], ALU.subtract), r=["identf", "Xc0"], w=["Pm"])
                    for h in range(4):
                        T.add("pe", TR(BT[:, h * 128:(h + 1) * 128], qkn[:, 4 + h, jsl], identb[:]), glue=(h > 0),
                              r=["qkn", "identb"], pw=["BT"])
                    for h in range(4):
                        T.add("pe", TR(BT[:, 512 + h * 128:512 + (h + 1) * 128], vT[:, h, jsl], identb[:]), glue=True,
                              r=["vT", "identb"], pw=["BT"])
                    T.add("act", ACT(kvtok[:], v4(BT[:], 8), AF.Copy), glue=True, r=["BT"], w=[kvtokn])
                    def emit_P(k):
                        Yn_, ynn_ = Yc[(k + 1) % 2], "Yc%d" % ((k + 1) % 2)
                        for h in range(4):
                            T.add("pe", MM(B[2][:, h * 128:(h + 1) * 128], Yn_[:, h, :], Pm[:, h, :]), r=[ynn_, "Pm"], pw=["B2"])
                        if k < 4:
                            T.add("dve", TT(Pm[:], v4(B[2][:]), Pm[:], ALU.add), r=["B2", "Pm"], w=["Pm"])
                        else:
                            T.add("dve", TT(TTb[:], v4(B[2][:]), Pm[:], ALU.add), r=["B2", "Pm"], w=[TTbn])

                    for k in range(5):
                        Xk, Yk = Xc[k % 2], Yc[k % 2]
                        Xn, Yn = Xc[(k + 1) % 2], Yc[(k + 1) % 2]
                        xkn, ykn = "Xc%d" % (k % 2), "Yc%d" % (k % 2)
                        xnn, ynn = "Xc%d" % ((k + 1) % 2), "Yc%d" % ((k + 1) % 2)
                        for h in range(4):
                            T.add("pe", MM(B[6][:, h * 128:(h + 1) * 128], Xk[:, h, :], Yk[:, h, :]), r=[xkn, ykn], pw=["B6"])
                        if k < 4:
                            for h in range(4):
                                T.add("pe", MM(B[5][:, h * 128:(h + 1) * 128], Yk[:, h, :], Xk[:, h, :]), r=[xkn, ykn], pw=["B5"])
                        T.add("dve", CP(Yn[:], v4(B[6][:])), r=["B6"], w=[ynn])
                        if k < 4:
                            T.add("act", ACT(Xn[:], v4(B[5][:]), AF.Copy), r=["B5"], w=[xnn])
                        if k > 0:
                            emit_P(k - 1)
                        drain(3)
                    emit_P(4)
                    drain(len(side))

                def stage_T2(T, sbk, j):
                    b = sbk * 4 + j
                    own = b >= NPRE
                    ob = b - NPRE
                    par = b % 2
                    tsl = slice(ob * 128, (ob + 1) * 128)
                    has_kv = own or b == NPRE - 1
                    beta, bdm = betas[par], bdms[par]
                    zs, KgT, qgT, QKT, TTb, EGB = zss[par], KgTs[par], qgTs[par], QKTs[par], TTbs[par], EGBs[par]
                    zsn, KgTn, qgTn, QKTn, TTbn, EGBn = ["%s%d" % (n_, par) for n_ in ("zs", "KgT", "qgT", "QKT", "TTb", "EGB")]
                    betan = "beta%d" % par
                    qn, qnn = qns[par], "qn0"
                    kpad, kpadn = kpads[par], "kpad0"
                    kvtok, kvtokn = kvtoks[par], "kvtok%d" % par
                    osb = Gs[0]
                    if own:
                        T.add("act", ACT(Gs[2][:], zs[:], AF.Exp, scale=-1.0), r=[zsn], w=["G2"])
                        T.add("act", ACT(Gs[2][:], Gs[2][:], AF.Ln, bias=1.0), r=["G2"], w=["G2"])
                        T.add("act", ACT(Gs[2][:], Gs[2][:], AF.Exp, scale=-1.0), r=["G2"], w=["G2"])
                        T.add("pool", TT(Gs[2][:], Gs[2][:], zs[:], ALU.mult), r=["G2", zsn], w=["G2"])
                        T.add("pool", TT(Gs[2][:], Gs[2][:], gainO4, ALU.mult), r=["G2", "onorm"], w=["G2"])
                    for c in range(2):
                        for h in range(4):
                            T.add("pe", MM(B[1][:, h * 128:(h + 1) * 128], KgT[:, h, :], Sbf[:, h, :]), r=[KgTn, "Sbf"], pw=["B1"])
                        T.add("dve", TT(Rb[:], kvtok[:, 4:8, :], v4(B[1][:]), ALU.subtract), r=[kvtokn, "B1"], w=["Rb"])
                        for h in range(4):
                            T.add("pe", MM(B[1][:, h * 128:(h + 1) * 128], TTb[:, h, :], Rb[:, h, :]), r=[TTbn, "Rb"], pw=["B1"])
                        T.add("dve", TT(vd[:], v4(B[1][:]), b4(bdm[c]), ALU.mult), r=["B1", "bdm%d_%d" % (c, par)], w=["vd"])
                        if own:
                            T.add("dve", TT(vn[:], v4(B[1][:]), b4(beta), ALU.mult), r=["B1", betan], w=["vn"])
                            for h in range(4):
                                T.add("pe", MM(B[1][:, h * 128:(h + 1) * 128], qgT[:, h, :], Sbf[:, h, :], True, False),
                                      r=[qgTn, "Sbf"], pw=["B1"])
                                T.add("pe", MM(B[1][:, h * 128:(h + 1) * 128], QKT[:, h, :], vn[:, h, :], False, True),
                                      glue=True, r=[QKTn, "vn"], pw=["B1"])
                            rs = slice(c * 64, (c + 1) * 64)
                            T.add("act", ACT(osb[rs].rearrange("p h d -> p (h d)"), B[1][rs, :], AF.Copy), r=["B1"], pw=["G0"])
                        for h in range(4):
                            T.add("pe", MM(B[1][:, h * 128:(h + 1) * 128], kvtok[:, h, :], vd[:, h, :]), r=[kvtokn, "vd"], pw=["B1"])
                        col = 63 + 64 * c
                        T.add("dve", TT(S[:], S[:], EGB[:, :, col:col + 1].to_broadcast([128, 4, 128]), ALU.mult),
                              r=["S", EGBn], w=["S"])
                        T.add("dve", TT(S[:], S[:], v4(B[1][:]), ALU.add), r=["S", "B1"], w=["S"])
                        T.add("act", ACT(Sbf[:], S[:], AF.Copy), r=["S"], w=["Sbf"])
                    if own:
                        T.add("act", ACT(Gs[1][:], osb[:], AF.Square), r=["G0"], w=["G1"])
                        T.add("dve", RED(oss, Gs[1][:]), r=["G1"], w=["oss"])
                        T.add("act", ACT(oss, oss, AF.Ln, scale=1.0 / 128, bias=EPS), r=["oss"], w=["oss"])
                        T.add("act", ACT(ors, oss, AF.Exp, scale=-0.5), r=["oss"], w=["ors"])
                        T.add("dve", TT(Gs[3][:], osb[:], b4(ors), ALU.mult), r=["G0", "ors"], w=["G3"])
                        T.add("dve", TT(v4(yA[:]), Gs[3][:], Gs[2][:], ALU.mult), r=["G3", "G2"], w=["yA"])
                        for p_ in range(4):
                            T.add("pe", TR(BT[:, p_ * 128:(p_ + 1) * 128], yA[:, p_ * 128:(p_ + 1) * 128], identb[:]), glue=(p_ > 0),
                                  r=["yA", "identb"], pw=["BT"])
                        T.add("act", ACT(yAT[:, :, tsl], v4(BT[:, 0:512]), AF.Copy), glue=True, r=["BT"], w=["yAT%d" % ob])

                def stage_T2b(T, sbk, j):
                    b = sbk * 4 + j
                    own = b >= NPRE
                    ob = b - NPRE
                    par = b % 2
                    tsl = slice(ob * 128, (ob + 1) * 128)
                    has_kv = own or b == NPRE - 1
                    if not has_kv:
                        return
                    qn, qnn = qns[par], "qn0"
                    kpad, kpadn = kpads[par], "kpad0"
                    kcur, kprev = b % 2, (b - 1) % 2
                    v1c, v1p = b % 3, (b - 1) % 3
                    kvr, kvrn = kvraws[par], "kvraw%d" % par
                    k2 = lambda ap: ap.rearrange("p (h d) -> p h d", h=2)
                    for kv_ in range(2):
                        T.add("act", ACT(pT[0][:, kv_ * 64:(kv_ + 1) * 64], kvr[:, kv_ * 64:(kv_ + 1) * 64], AF.Square,
                                         accum_out=kss[:, kv_:kv_ + 1]), r=[kvrn], pw=["pT0", "kss"])
                    T.add("act", ACT(kss, kss, AF.Ln, scale=1.0 / 64, bias=EPS), r=["kss"], w=["kss"])
                    T.add("act", ACT(krs, kss, AF.Exp, scale=-0.5), r=["kss"], w=["krs"])
                    T.add("pool", TT(k2(kt[:]), k2(kvr[:, 0:128]), krs.unsqueeze(2).to_broadcast([128, 2, 64]), ALU.mult),
                          r=[kvrn, "krs"], w=["kt"])
                    T.add("pool", TT(kpad[:, :, 0, 0:64], k2(kt[:]), gainK2, ALU.mult), r=["kt", "kg"], pw=[kpadn])
                    T.add("pool", TT(kpad[:, :, 1, 64:128], k2(kt[:]), gainK2, ALU.mult), r=["kt", "kg"], pw=[kpadn])
                    T.add("act", ACT(V1[v1c][:, :, 0:64], k2(kvr[:, 128:256]), AF.Copy), r=[kvrn], pw=["V1_%d" % v1c])
                    if own:
                        qr, qrn = qraws[par], "qraw%d" % par
                        q8 = lambda ap: ap.rearrange("p (h d) -> p h d", h=8)
                        for h_ in range(8):
                            T.add("act", ACT(pT[0][:, h_ * 64:(h_ + 1) * 64], qr[:, h_ * 64:(h_ + 1) * 64], AF.Square,
                                             accum_out=qss[:, h_:h_ + 1]), r=[qrn], pw=["pT0", "qss"])
                        T.add("act", ACT(qss, qss, AF.Ln, scale=1.0 / 64, bias=EPS), r=["qss"], w=["qss"])
                        T.add("act", ACT(qrs, qss, AF.Exp, scale=-0.5), r=["qss"], w=["qrs"])
                        T.add("pool", TT(q8(qr[:]), q8(qr[:]), qrs.unsqueeze(2).to_broadcast([128, 8, 64]), ALU.mult),
                              r=[qrn, "qrs"], w=[qrn])
                        T.add("pool", TT(q8(qn[:]), q8(qr[:]), gainQ8, ALU.mult), r=[qrn, "qg"], w=[qnn])
                    first = True
                    for g in range(2):
                        for var in range(2):
                            i = g * 2 + var
                            T.add("pe", TR(BT[:, i * 128:(i + 1) * 128], kpad[:, g, var, :], identb[:]), glue=(not first),
                                  r=[kpadn, "identb"], pw=["BT"])
                            first = False
                    if not own:
                        T.add("act", ACT(kTp[kcur][:], v4(BT[:, 0:512]), AF.Copy), glue=True, r=["BT"], w=["kTp%d" % kcur])
                        return
                    for p_ in range(4):
                        T.add("pe", TR(BT[:, 512 + p_ * 128:512 + (p_ + 1) * 128], qn[:, p_ * 128:(p_ + 1) * 128], identb[:]), glue=True,
                              r=[qnn, "identb"], pw=["BT"])
                    T.add("act", ACT(kTp[kcur][:], v4(BT[:, 0:512]), AF.Copy), glue=True, r=["BT"], w=["kTp%d" % kcur])
                    T.add("act", ACT(qTp[:], v4(BT[:, 512:1024]), AF.Copy), glue=True, r=["BT"], w=["qTp"])
                    for g in range(2):
                        for (kk, b8, b8n, pTi, pTn, is_prev) in ((kprev, biasP8, "biasP8", pT[0], "pT0", True),
                                                               (kcur, biasC8, "biasC8", pT[1], "pT1", False)):
                            for j_ in range(4):
                                h = 4 * g + j_
                                T.add("pe", MM(B[0][:, j_ * 128:(j_ + 1) * 128], kTp[kk][:, g * 2 + h % 2, :], qTp[:, h // 2, :],
                                               j_ == 0, False, skip=True), r=["kTp%d" % kk, "qTp"], pw=["B0"])
                            T.add("pe", MM(B[0][:], identb[:], b8[:, g * 512:(g + 1) * 512], False, True, skip=True),
                                  r=["identb", b8n], pw=["B0"])
                            if is_prev and ob == 0:
                                T.add("act", ACT(pTi[:], B[0][:], AF.Exp, scale=0.125, bias=pm[:, 0:1]), r=["B0", "pm"], w=[pTn])
                            else:
                                T.add("act", ACT(pTi[:], B[0][:], AF.Exp, scale=0.125), r=["B0"], w=[pTn])
                        for j_ in range(4):
                            T.add("pe", MM(B[0][:, j_ * 65:(j_ + 1) * 65], pT[0][:, j_ * 128:(j_ + 1) * 128], V1[v1p][:, g, 0:65], True, False),
                                  r=["pT0", "V1_%d" % v1p], pw=["B0"])
                            T.add("pe", MM(B[0][:, j_ * 65:(j_ + 1) * 65], pT[1][:, j_ * 128:(j_ + 1) * 128], V1[v1c][:, g, 0:65], False, True),
                                  glue=True, r=["pT1", "V1_%d" % v1c], pw=["B0"])
                        pvv = B[0][:, 0:260].rearrange("p (h d) -> p h d", h=4)
                        T.add("dve", TT(den.unsqueeze(2), pvv[:, :, 64:65], esink[:, 4 * g:4 * g + 4].unsqueeze(2), ALU.add),
                              r=["B0", "esink"], w=["den"])
                        T.add("dve", RC(den, den), r=["den"], w=["den"])
                        T.add("dve", TT(yB[:, g * 256:(g + 1) * 256].rearrange("p (h d) -> p h d", h=4), pvv[:, :, 0:64],
                                        den.unsqueeze(2).to_broadcast([128, 4, 64]), ALU.mult), r=["B0", "den"], pw=["yB"])
                    for p_ in range(4):
                        T.add("pe", TR(BT[:, p_ * 128:(p_ + 1) * 128], yB[:, p_ * 128:(p_ + 1) * 128], identb[:]), glue=(p_ > 0),
                              r=["yB", "identb"], pw=["BT"])
                    T.add("act", ACT(yBT[:, :, tsl], v4(BT[:, 0:512]), AF.Copy), glue=True, r=["BT"], w=["yBT%d" % ob])

                prev = None
                for sbk in range(NSB):
                    x_in_t2 = (sbk + 1 < NSB)
                    for j in range(4):
                        t1 = Thr()
                        if j == 0:
                            if sbk == 0:
                                stage_X(t1, 0)
                            stage_F(t1, sbk)
                            if (not x_in_t2) and sbk + 1 < NSB:
                                stage_X(t1, sbk + 1)
                        stage_T1(t1, sbk, j)
                        if prev is None:
                            merge(t1)
                        else:
                            merge(t1, prev[0], prev[1])
                        prev = (Thr(), Thr())
                        stage_T2(prev[0], sbk, j)
                        stage_T2b(prev[1], sbk, j)
                        if x_in_t2 and j < 3:
                            stage_X(prev[0], sbk + 1, (j,) if j < 2 else (2, 3))
                if prev is not None:
                    merge(prev[0], prev[1])
                P.barrier()
                P.emit(nc, semf)

            sb1 = contextlib.ExitStack()
            with sb1:
                P = Prog("b")
                wgA = sbt(sb1, "wgA", [128, 8, D], BF16)
                wgB = sbt(sb1, "wgB", [128, 8, D], BF16)
                wA = sbt(sb1, "wA", [128, 4, D], BF16)
                wB = sbt(sb1, "wB", [128, 4, D], BF16)
                sA = sbt(sb1, "sA", [128, 512], F32)
                sB = sbt(sb1, "sB", [128, 512], F32)
                t1 = sbt(sb1, "t1", [128, 512], F32)
                t2 = sbt(sb1, "t2", [128, 512], F32)
                mtiles = [sbt(sb1, "mtile%d" % i, [128, 8, 512], BF16) for i in range(2)]
                w_in_v = w_in_d.rearrange("(kc p) n -> p kc n", p=128)
                w_a_v = w_a_d.rearrange("(kc p) n -> p kc n", p=128)
                w_b_v = w_b_d.rearrange("(kc p) n -> p kc n", p=128)
                pieces = ((0, 128), (128, 512), (512, 1024))
                def piece_of(m):
                    return 0 if m == 0 else (1 if m < 4 else 2)
                for pi, (c0, c1) in enumerate(pieces):
                    cs = slice(c0, c1)
                    P.add("pool", DMA(wgA[:, :, cs], w_in_v[:, :, 2824 + c0:2824 + c1]), w=["wgA%d" % pi], dma="wgA%d" % pi)
                    P.add("pool", DMA(wgB[:, :, cs], w_in_v[:, :, 3848 + c0:3848 + c1]), w=["wgB%d" % pi], dma="wgB%d" % pi)
                    P.add("pool", DMA(wA[:, :, cs], w_a_v[:, :, cs]), w=["wA%d" % pi], dma="wA%d" % pi)
                    P.add("pool", DMA(wB[:, :, cs], w_b_v[:, :, cs]), w=["wB%d" % pi], dma="wB%d" % pi)
                wg_vb = w_gate_d.rearrange("(kc p) n -> p kc n", p=128)
                wu_vb = w_up_d.rearrange("(kc p) n -> p kc n", p=128)
                for fb in range(NF):
                    fsl = slice(fb * 128, (fb + 1) * 128)
                    wcr = wcache[fb * 128:(fb + 1) * 128, :]
                    P.add("pool", DMA(wcr[:, 0:1024].rearrange("p (kc n) -> p kc n", kc=8), wg_vb[:, :, fsl]), dma="wcf")
                    P.add("pool", DMA(wcr[:, 1024:2048].rearrange("p (kc n) -> p kc n", kc=8), wu_vb[:, :, fsl]), dma="wcf")
                wo_vb = w_out_d.rearrange("(kc p) n -> p kc n", p=128)
                wd_vb = w_down_d.rearrange("(f p) n -> p f n", p=128)
                woc3 = wocache.rearrange("p (kc n) -> p kc n", kc=8)
                wdc3 = wdcache.rearrange("p (f n) -> p f n", f=NF)
                for hf in range(2):
                    cs_ = slice(hf * 512, (hf + 1) * 512)
                    P.add("pool", DMA(woc3[:, :, cs_], wo_vb[:, :, cs_]), dma="wcf")
                for hf in range(2):
                    cs_ = slice(hf * 512, (hf + 1) * 512)
                    for (f0, f1) in ((0, 8), (8, 16), (16, 22)):
                        P.add("pool", DMA(wdc3[:, f0:f1, cs_], wd_vb[:, f0:f1, cs_]), dma="wcf")
                it = 0
                for tt_ in range(int(os.environ.get('K_NB1', 4))):
                    ts_ = slice(tt_ * 512, (tt_ + 1) * 512)
                    for m in range(8):
                        ms = slice(m * 128, (m + 1) * 128)
                        hf = piece_of(m)
                        if it % 2 == 0:
                            bgA, bgB, bpA, bpB = B[0], B[1], B[2], B[6]
                            nA, nB, nPA, nPB = "b0", "b1", "b2", "b6"
                        else:
                            bgA, bgB, bpA, bpB = B[3], B[4], B[5], B[6]
                            nA, nB, nPA, nPB = "b3", "b4", "b5", "b6"
                        for kc in range(8):
                            P.add("pe", MM(bgA[:], wgA[:, kc, ms], hTm[:, kc, ts_], kc == 0, kc == 7), r=["wgA%d" % hf, "hTm%d" % tt_], pw=[nA])
                        for kc in range(8):
                            P.add("pe", MM(bgB[:], wgB[:, kc, ms], hTm[:, kc, ts_], kc == 0, kc == 7), r=["wgB%d" % hf, "hTm%d" % tt_], pw=[nB])
                        for kc in range(4):
                            P.add("pe", MM(bpA[:], wA[:, kc, ms], yAT[:, kc, ts_], kc == 0, kc == 3), r=["wA%d" % hf], pw=[nPA])
                        for kc in range(4):
                            P.add("pe", MM(bpB[:], wB[:, kc, ms], yBT[:, kc, ts_], kc == 0, kc == 3), r=["wB%d" % hf], pw=[nPB])
                        P.add("act", ACT(sA[:], bgA[:], AF.Sigmoid), r=[nA], w=["sA"])
                        P.add("act", ACT(sB[:], bgB[:], AF.Sigmoid), r=[nB], w=["sB"])
                        P.add("dve", TT(t1[:], sA[:], bpA[:], ALU.mult), r=["sA", nPA], w=["t1"])
                        P.add("dve", TT(t2[:], sB[:], bpB[:], ALU.mult), r=["sB", nPB], w=["t2"])
                        P.add("dve", TT(mtiles[tt_ % 2][:, m, :], t1[:], t2[:], ALU.add), r=["t1", "t2"], pw=["mtile%d" % (tt_ % 2)])
                        it += 1
                    P.add("act", ACT(hTm[:, :, ts_], mtiles[tt_ % 2][:], AF.Copy), r=["mtile%d" % (tt_ % 2)], w=["hTm%d" % tt_])
                P.barrier()
                P.emit(nc, semf)

        s2 = contextlib.ExitStack()
        with s2:
            P = Prog("c")
            wout = sbt(s2, "wout", [128, 8, D], BF16)
            wdown = sbt(s2, "wdown", [128, NF, D], BF16)
            gainF = sbt(s2, "gainF", [128, D], F32)
            xin2 = [sbt(s2, "xin2_%d" % i, [128, D], F32) for i in range(2)]
            x1 = [sbt(s2, "x1_%d" % i, [128, 4, D], F32) for i in range(2)]
            h2bf = [sbt(s2, "h2bf%d" % i, [128, D], BF16) for i in range(2)]
            h2T = [sbt(s2, "h2T%d" % i, [128, 8, 512], BF16) for i in range(2)]
            actT = sbt(s2, "actT", [128, NF, 512], BF16)
            wgu = [sbt(s2, "wgu%d" % i, [128, 16, 128], BF16) for i in range(3)]
            sg = [sbt(s2, "sg%d" % i, [128, 512], F32) for i in range(2)]
            ost = [sbt(s2, "ost%d" % i, [128, D], F32) for i in range(2)]
            cc = [sbt(s2, "cc%d" % i, [128, 16], F32) for i in range(3)]
            P.add("sp", DMA(gainF[:], ffn_norm_d[0:1, :].partition_broadcast(128)), w=["gainF"], dma="gF")
            wo_v = w_out_d.rearrange("(kc p) n -> p kc n", p=128)
            P.add("sp", DMA(wout[:].rearrange("p a b -> p (a b)"), wocache[:, :]), w=["wout"], dma="wout")
            wd_v = w_down_d.rearrange("(f p) n -> p f n", p=128)
            wg_v = w_gate_d.rearrange("(kc p) n -> p kc n", p=128)
            wu_v = w_up_d.rearrange("(kc p) n -> p kc n", p=128)
            NT2 = int(os.environ.get('K_NB2', 4))

            def x1_partA(t, blk):
                tb = t * 4 + blk
                xi, xn = xin2[tb % 2], "xin2_%d" % (tb % 2)
                x1t, hb = x1[t % 2], h2bf[tb % 2]
                x1n, hbn = "x1_%d_%d" % (t % 2, blk), "h2bf%d" % (tb % 2)
                P.add("sp", DMA(xi[:], x_d[(NPRE + tb) * 128:(NPRE + tb + 1) * 128, :]), w=[xn], dma=xn)
                for half in range(2):
                    hs = slice(half * 512, (half + 1) * 512)
                    for kc in range(8):
                        P.add("pe", MM(B[half][:], hTm[:, kc, tb * 128:(tb + 1) * 128], wout[:, kc, hs], kc == 0, kc == 7),
                              r=["wout"], pw=["b%d" % half])
                    P.add("dve", TT(x1t[:, blk, hs], xi[:, hs], B[half][:], ALU.add), r=[xn, "b%d" % half], pw=[x1n])
                P.add("act", ACT(hb[:], x1t[:, blk, :], AF.Square, accum_out=cc[0][:, 0:1]), r=[x1n], w=[hbn, "cc0"])
                P.add("act", ACT(cc[1][:, 0:1], cc[0][:, 0:1], AF.Sqrt, scale=1.0 / D, bias=EPS), r=["cc0"], w=["cc1"])
                P.add("dve", RC(cc[2][:, 0:1], cc[1][:, 0:1]), r=["cc1"], w=["cc2"])
                P.add("dve", STT(hb[:], x1t[:, blk, :], cc[2][:, 0:1], gainF[:], ALU.mult, ALU.mult),
                      r=[x1n, "cc2", "gainF"], w=[hbn])

            def x1_partB(t, blk):
                tb = t * 4 + blk
                hb, hbn = h2bf[tb % 2], "h2bf%d" % (tb % 2)
                for kc in range(8):
                    P.add("pe", TR(BT[:, kc * 128:(kc + 1) * 128], hb[:, kc * 128:(kc + 1) * 128], identb[:]),
                          r=[hbn], pw=["BT"])
                P.add("act", ACT(h2T[t % 2][:, :, blk * 128:(blk + 1) * 128], v4(BT[:], 8), AF.Copy), r=["BT"], pw=["h2T%d" % (t % 2)])

            gu_it = [0]

            def gu(t, f):
                it = gu_it[0]
                gu_it[0] += 1
                sl_ = it % 3
                wg, wn = wgu[sl_], "wgu%d" % sl_
                fs = slice(f * 128, (f + 1) * 128)
                wgf = wg[:].rearrange("p a b -> p (a b)")
                P.add("sp", DMA(wgf, wcache[f * 128:(f + 1) * 128, :]), w=[wn], dma=wn)
                bg, bu = (B[2], B[3]) if it % 2 == 0 else (B[4], B[5])
                ng, nu = ("b2", "b3") if it % 2 == 0 else ("b4", "b5")
                hT2, hT2n = h2T[t % 2], "h2T%d" % (t % 2)
                for kc in range(8):
                    P.add("pe", MM(bg[:], wg[:, kc, :], hT2[:, kc, :], kc == 0, kc == 7), r=[wn, hT2n], pw=[ng])
                for kc in range(8):
                    P.add("pe", MM(bu[:], wg[:, 8 + kc, :], hT2[:, kc, :], kc == 0, kc == 7), r=[wn, hT2n], pw=[nu])
                sgi = sg[it % 2]
                P.add("act", ACT(sgi[:], bg[:], AF.Silu), r=[ng], w=["sg%d" % (it % 2)])
                P.add("dve", TT(actT[:, f, :], sgi[:], bu[:], ALU.mult), r=["sg%d" % (it % 2), nu], pw=["actT"])

            def down(t, blk):
                tb = t * 4 + blk
                oi, on = ost[tb % 2], "ost%d" % (tb % 2)
                for half in range(2):
                    hs = slice(half * 512, (half + 1) * 512)
                    for f in range(NF):
                        P.add("pe", MM(B[half][:], actT[:, f, blk * 128:(blk + 1) * 128], wdown[:, f, hs], f == 0, f == NF - 1),
                              r=["actT", "wdown"], pw=["b%d" % half])
                    P.add("dve", TT(oi[:, hs], x1[t % 2][:, blk, hs], B[half][:], ALU.add),
                          r=["x1_%d_%d" % (t % 2, blk), "b%d" % half], pw=[on])
                P.add("sp", DMA(out_d[tb * 128:(tb + 1) * 128, :], oi[:]), r=[on], dma=on)

            for blk in range(4):
                x1_partA(0, blk)
                x1_partB(0, blk)
            wdf = wdown[:].rearrange("p a b -> p (a b)")
            for i_, (c0_, c1_) in enumerate(((0, 8 * D), (8 * D, 16 * D), (16 * D, NF * D))):
                P.add("sp", DMA(wdf[:, c0_:c1_], wdcache[:, c0_:c1_]), pw=["wdown"], dma="wd%d" % i_)
            for t in range(NT2):
                for f in range(NF):
                    gu(t, f)
                    if t + 1 < NT2:
                        if f in (1, 6, 11, 16):
                            x1_partA(t + 1, (f - 1) // 5)
                        if f in (4, 9, 14, 19):
                            x1_partB(t + 1, (f - 4) // 5)
                for blk in range(4):
                    down(t, blk)
            P.barrier()
            P.emit(nc, semf)
    return nc


def _t5_bucket(dist):
    n = np.maximum(dist, 0)
    max_exact = 16
    nf = np.maximum(n, 1).astype(np.float32)
    large = max_exact + (np.log(nf / max_exact) / math.log(128 / max_exact) * (32 - max_exact)).astype(np.int32)
    large = np.minimum(large, 31)
    return np.where(n < max_exact, n, large)


_NC_CACHE = {}


def kernel(x, attn_norm, w_in, dn_conv, dn_a_log, dn_dt_bias, dn_out_norm, swa_q_norm, swa_k_norm,
           swa_sinks, rel_bias, w_branch_dn, w_branch_swa, w_out, ffn_norm, w_gate, w_up, w_down):
    f = lambda a: np.ascontiguousarray(np.asarray(a, dtype=np.float32))
    x = f(x)
    if "nc" not in _NC_CACHE:
        _NC_CACHE["nc"] = build()
    nc = _NC_CACHE["nc"]
    idx = np.arange(128)
    same = (idx[:, None] // 64) == (idx[None, :] // 64)
    U = ((idx[:, None] <= idx[None, :]) & same).astype(np.float32)
    Urev = ((idx[:, None] > idx[None, :]) & same).astype(np.float32)
    NEGM = np.where((idx[None, :] >= idx[:, None]) & same, 0.0, -1e5).astype(np.float32)
    STRICT = ((idx[None, :] > idx[:, None]) & same).astype(np.float32)
    m01 = np.stack([(idx < 64), (idx >= 64)], 1).astype(np.float32)
    ident = np.eye(128, dtype=np.float32)
    rb = f(rel_bias)
    s_i = idx[:, None]
    q_i = idx[None, :]
    dist_prev = 128 + q_i - s_i
    dist_cur = q_i - s_i
    biasP = np.empty((128, 8, 128), np.float32)
    biasC = np.empty((128, 8, 128), np.float32)
    bk_p = _t5_bucket(dist_prev)
    bk_c = _t5_bucket(dist_cur)
    ok_p = (dist_prev >= 0) & (dist_prev < 128)
    ok_c = (dist_cur >= 0) & (dist_cur < 128)
    for h in range(8):
        biasP[:, h, :] = np.where(ok_p, rb[bk_p, h], NEGBIG)
        biasC[:, h, :] = np.where(ok_c, rb[bk_c, h], NEGBIG)
    biasP = biasP.reshape(128, 1024)
    biasC = biasC.reshape(128, 1024)
    common = {
        "w_in": f(w_in[0]), "w_a": f(w_branch_dn[0]), "w_b": f(w_branch_swa[0]), "w_out": f(w_out[0]),
        "w_gate": f(w_gate[0]), "w_up": f(w_up[0]), "w_down": f(w_down[0]),
        "attn_norm": f(attn_norm), "ffn_norm": f(ffn_norm), "dn_conv": f(dn_conv[0]),
        "a_log": f(dn_a_log), "dt_bias": f(dn_dt_bias), "out_norm": f(dn_out_norm),
        "q_norm": f(swa_q_norm), "k_norm": f(swa_k_norm), "sinks": f(swa_sinks),
        "biasP": biasP, "biasC": biasC, "ident": ident, "U": U, "Urev": Urev, "NEGM": NEGM,
        "STRICT": STRICT, "m01": m01,
    }
    in_maps = []
    for core in range(8):
        bt, half = core // 2, core % 2
        xe = np.zeros((NBLK * 128, D), np.float32)
        if half == 1:
            xe[:] = x[bt]
        else:
            xe[NPRE * 128:] = x[bt, :NOWN * 128]
        pmv = np.full((128, 1), NEGBIG if half == 0 else 0.0, np.float32)
        d = dict(common)
        d["x"] = xe
        d["pm"] = pmv
        in_maps.append(d)
    res = run_bass_kernel_spmd(nc, in_maps, core_ids=list(range(8)))
    out = np.empty((4, 4096, D), np.float32)
    for core in range(8):
        bt, half = core // 2, core % 2
        out[bt, half * 2048:(half + 1) * 2048] = res.results[core]["out"]
    return out
```

```python
import contextlib
import os
import math
import numpy as np
import ml_dtypes
import concourse.bass as bass
import concourse.mybir as mybir
from concourse.bass_utils import run_bass_kernel_spmd

F32 = mybir.dt.float32
BF16 = mybir.dt.bfloat16
AF = mybir.ActivationFunctionType
ALU = mybir.AluOpType
AX = mybir.AxisListType

D = 1024
DIN = 4872
DFF = 2816
NF = 22
NPRE = 16
NOWN = 16
NBLK = NPRE + NOWN
EPS = 1e-6
NEGBIG = -30000.0


class Op:
    __slots__ = ("eng", "fn", "dma", "dma_val", "waits", "signal", "sigval", "uid", "group")

    def __init__(self, eng, fn, dma):
        self.eng = eng
        self.fn = fn
        self.dma = dma
        self.dma_val = 0
        self.waits = []
        self.signal = False
        self.sigval = 0
        self.group = False


class Prog:
    ENGS = ("pe", "act", "dve", "pool", "sp")

    def __init__(self, tag):
        self.tag = tag
        self.ops = {e: [] for e in self.ENGS}
        self.res = {}
        self.dma_cnt = {}
        self.dma_last = {}
        self.dma_uses = {}
        self.bank_last = {}
        self.uid = 0

    def _st(self, name):
        st = self.res.get(name)
        if st is None:
            st = {"w": {}, "r": {}}
            self.res[name] = st
        return st

    def _dep(self, op, prev, kind):
        if prev is op:
            return
        if prev.dma is None and op.dma is None and prev.eng == op.eng and kind != "raw":
            return
        if prev in op.waits:
            return
        op.waits.append(prev)
        if prev.dma is None:
            prev.signal = True

    def add(self, eng, fn, r=(), w=(), pw=(), dma=None, group=False):
        if dma is not None and not group:
            n = self.dma_uses.get(dma, 0)
            self.dma_uses[dma] = n + 1
            dma = "%s_%d" % (dma, n // 14)
        op = Op(eng, fn, dma)
        op.group = group
        self.uid += 1
        op.uid = self.uid
        key = eng if dma is None else ("dma", self.uid)
        for name in r:
            for p in self._st(name)["w"].values():
                self._dep(op, p, "raw")
        for name in tuple(w) + tuple(pw):
            st = self._st(name)
            for p in st["w"].values():
                self._dep(op, p, "waw")
            for p in st["r"].values():
                self._dep(op, p, "war")
        banks = set()
        for name in tuple(r) + tuple(w) + tuple(pw):
            if name == "BT":
                banks.add("T")
            elif name[0] in "Bb" and name[1:2].isdigit():
                banks.add(name[1])
        for bk in banks:
            bl = self.bank_last.setdefault(bk, {})
            for e2, p in bl.items():
                if e2 != eng:
                    self._dep(op, p, "bank")
            bl[eng] = op
        for name in r:
            self._st(name)["r"][key] = op
        for name in w:
            st = self._st(name)
            st["w"] = {key: op}
            st["r"] = {}
        for name in pw:
            st = self._st(name)
            st["w"][key] = op
            st["r"] = {}
        if dma is not None:
            self.dma_cnt[dma] = self.dma_cnt.get(dma, 0) + 16
            op.dma_val = self.dma_cnt[dma]
            self.dma_last[dma] = op
        self.ops[eng].append(op)
        return op

    def barrier(self):
        lasts = {}
        for e in self.ENGS:
            for op in reversed(self.ops[e]):
                if op.fn is not None and op.dma is None:
                    lasts[e] = op
                    break
        dl = list(self.dma_last.values())
        for e in self.ENGS:
            b = Op(e, None, None)
            for e2, op in lasts.items():
                if e2 != e:
                    b.waits.append(op)
                    op.signal = True
            b.waits.extend(dl)
            self.ops[e].append(b)

    def emit(self, nc, semf):
        for e in self.ENGS:
            cnt = 0
            for op in self.ops[e]:
                if op.dma is None and op.signal:
                    cnt += 1
                    op.sigval = cnt
        if os.environ.get('K_DBG'):
            print('SIG', self.tag, {e: max([o.sigval for o in self.ops[e]] + [0]) for e in self.ENGS}, {e: len(self.ops[e]) for e in self.ENGS}, self.dma_cnt)
        esem = {e: semf(self.tag + "e" + e) for e in self.ENGS}
        dsem = {k: semf(self.tag + "d" + str(k)) for k in self.dma_cnt}
        prog = self

        def run(en, eng):
            known = {}
            for op in prog.ops[en]:
                for p in op.waits:
                    if p.dma is None:
                        sem, val, k = esem[p.eng], p.sigval, "e" + p.eng
                    else:
                        sem, val, k = dsem[p.dma], (prog.dma_cnt[p.dma] if p.group else p.dma_val), "d" + str(p.dma)
                    if known.get(k, 0) >= val:
                        continue
                    eng.wait_ge(sem, val)
                    known[k] = val
                if op.fn is None:
                    continue
                ins = op.fn(eng)
                if op.dma is not None:
                    ins.then_inc(dsem[op.dma], 16)
                elif op.signal:
                    ins.then_inc(esem[en], 1)

        with nc.Block() as block:
            @block.tensor
            def _(t):
                run("pe", t)

            @block.scalar
            def _(t):
                run("act", t)

            @block.vector
            def _(t):
                run("dve", t)

            @block.gpsimd
            def _(t):
                run("pool", t)

            @block.sync
            def _(t):
                run("sp", t)


def MM(out, lhsT, rhs, st=True, sp=True, skip=False):
    if skip:
        return lambda e: e.matmul(out, lhsT=lhsT, rhs=rhs, start=st, stop=sp, skip_group_check=True)
    return lambda e: e.matmul(out, lhsT=lhsT, rhs=rhs, start=st, stop=sp)


def TR(out, in_, ident):
    return lambda e: e.transpose(out=out, in_=in_, identity=ident)


def ACT(out, in_, func, **kw):
    return lambda e: e.activation(out=out, in_=in_, func=func, **kw)


def TT(out, a, b, op):
    return lambda e: e.tensor_tensor(out=out, in0=a, in1=b, op=op)


def TS(out, a, s1, op0, s2=None, op1=None):
    if op1 is None:
        return lambda e: e.tensor_scalar(out=out, in0=a, scalar1=s1, scalar2=None, op0=op0)
    return lambda e: e.tensor_scalar(out=out, in0=a, scalar1=s1, scalar2=s2, op0=op0, op1=op1)


def STT(out, a, s, b, op0, op1):
    return lambda e: e.scalar_tensor_tensor(out=out, in0=a, scalar=s, in1=b, op0=op0, op1=op1)


def CP(out, in_):
    return lambda e: e.tensor_copy(out=out, in_=in_)


def RC(out, in_):
    return lambda e: e.reciprocal(out=out, in_=in_)


def RED(out, in_):
    return lambda e: e.tensor_reduce(out=out, in_=in_, axis=AX.X, op=ALU.add)


def DMA(out, in_, **kw):
    return lambda e: e.dma_start(out=out, in_=in_, **kw)


def MSET(ap, v):
    return lambda e: e.memset(ap, v)


def v4(ap, h=4):
    return ap.rearrange("p (h d) -> p h d", h=h)


def build():
    nc = bass.Bass("TRN2", target_bir_lowering=False)

    def din(name, shape):
        return nc.dram_tensor(name, shape, F32, kind="ExternalInput").ap()

    x_d = din("x", [NBLK * 128, D])
    w_in_d = din("w_in", [D, DIN])
    w_a_d = din("w_a", [512, D])
    w_b_d = din("w_b", [512, D])
    w_out_d = din("w_out", [D, D])
    w_gate_d = din("w_gate", [D, DFF])
    w_up_d = din("w_up", [D, DFF])
    w_down_d = din("w_down", [DFF, D])
    attn_norm_d = din("attn_norm", [1, D])
    ffn_norm_d = din("ffn_norm", [1, D])
    conv_d = din("dn_conv", [4, 1536])
    alog_d = din("a_log", [1, 4])
    dtb_d = din("dt_bias", [1, 4])
    onorm_d = din("out_norm", [1, 128])
    qg_d = din("q_norm", [1, 64])
    kg_d = din("k_norm", [1, 64])
    sinks_d = din("sinks", [1, 8])
    biasP_d = din("biasP", [128, 1024])
    biasC_d = din("biasC", [128, 1024])
    pm_d = din("pm", [128, 1])
    ident_d = din("ident", [128, 128])
    U_d = din("U", [128, 128])
    Urev_d = din("Urev", [128, 128])
    NEG_d = din("NEGM", [128, 128])
    STR_d = din("STRICT", [128, 128])
    m01_d = din("m01", [128, 2])
    out_d = nc.dram_tensor("out", [NOWN * 128, D], F32, kind="ExternalOutput").ap()
    wcache = nc.dram_tensor("wcache", [NF * 128, 16 * 128], BF16, kind="Internal").ap()
    wocache = nc.dram_tensor("wocache", [128, 8 * D], BF16, kind="Internal").ap()
    wdcache = nc.dram_tensor("wdcache", [128, NF * D], BF16, kind="Internal").ap()

    top = contextlib.ExitStack()
    with top:
        def sbt(es, name, shape, dt):
            return es.enter_context(nc.sbuf_tensor("s_" + name, shape, dt))

        def semf(name):
            return top.enter_context(nc.semaphore(name))

        B = [top.enter_context(nc.psum_tensor("B%d" % i, [128, 512], F32)) for i in range(7)]
        BT = top.enter_context(nc.psum_tensor("BT", [128, 1024], BF16))

        hTm = sbt(top, "hTm", [128, 8, NOWN * 128], BF16)
        identf = sbt(top, "identf", [128, 128], F32)
        identb = sbt(top, "identb", [128, 128], BF16)

        s1 = contextlib.ExitStack()
        with s1:
            yAT = sbt(s1, "yAT", [128, 4, NOWN * 128], BF16)
            yBT = sbt(s1, "yBT", [128, 4, NOWN * 128], BF16)

            sa = contextlib.ExitStack()
            with sa:
                P = Prog("a")
                NWT = 1288
                CZ, CBA, CQ, CKV = 0, 512, 520, 1032
                winT = sbt(sa, "winT", [128, 8, NWT], BF16)
                wqs = [sbt(sa, "wqs%d" % i, [128, 8, 128], BF16) for i in range(3)]
                Uc = sbt(sa, "Uc", [128, 128], F32)
                Urev = sbt(sa, "Urev", [128, 128], F32)
                NEGM = sbt(sa, "NEGM", [128, 128], F32)
                STR = sbt(sa, "STR", [128, 128], F32)
                m01 = sbt(sa, "m01", [128, 2], F32)
                onesb = sbt(sa, "onesb", [128, 128], BF16)
                gainT = sbt(sa, "gainT", [128, 8], F32)
                wconv = sbt(sa, "wconv", [128, 4, 12], F32)
                alog = sbt(sa, "alog", [128, 4], F32)
                dtb = sbt(sa, "dtb", [128, 4], F32)
                negA = sbt(sa, "negA", [128, 4], F32)
                onorm = sbt(sa, "onorm", [128, 128], F32)
                qg = sbt(sa, "qg", [128, 64], F32)
                kg = sbt(sa, "kg", [128, 64], F32)
                sinks = sbt(sa, "sinks", [128, 8], F32)
                esink = sbt(sa, "esink", [128, 8], F32)
                biasP8 = sbt(sa, "biasP8", [128, 1024], BF16)
                biasC8 = sbt(sa, "biasC8", [128, 1024], BF16)
                pm = sbt(sa, "pm", [128, 1], F32)
                dg = [sbt(sa, "dg%d" % i, [128, 4, 128], BF16) for i in range(3)]
                xin = sbt(sa, "xin", [128, D], F32)
                hbf = sbt(sa, "hbf", [128, D], BF16)
                xct = [sbt(sa, "xct%d" % i, [128, 516], BF16) for i in range(2)]
                halo = sbt(sa, "halo", [128, 12, 4], BF16)
                sil = [sbt(sa, "sil%d" % i, [128, 512], F32) for i in range(4)]
                vT = sbt(sa, "vT", [128, 4, 512], BF16)
                sqs = [sbt(sa, "sq%d" % i, [128, 512], BF16) for i in range(2)]
                rt = [sbt(sa, "rt%d" % i, [128, 512], F32) for i in range(2)]
                qkn = sbt(sa, "qkn", [128, 8, 512], BF16)
                kvraws = [sbt(sa, "kvraw%d" % i, [128, 256], F32) for i in range(2)]
                qraws = [sbt(sa, "qraw%d" % i, [128, 512], F32) for i in range(2)]
                Fb = [sbt(sa, "F%d" % i, [128, 4, 128], F32) for i in range(4)]
                gstage = Fb[0][0:8, 0, :]
                cstage = Fb[1][0:48, 0, :]
                SBt = sbt(sa, "SBt", [128, 4, 128], F32)
                Gs = [sbt(sa, "G%d" % i, [128, 4, 128], F32) for i in range(4)]
                EGBs = [sbt(sa, "EGB%d" % i, [128, 4, 128], F32) for i in range(2)]
                Xc = [sbt(sa, "Xc%d" % i, [128, 4, 128], F32) for i in range(2)]
                Yc = [sbt(sa, "Yc%d" % i, [128, 4, 128], F32) for i in range(2)]
                Pm = sbt(sa, "Pm", [128, 4, 128], F32)
                zss = [sbt(sa, "zs%d" % i, [128, 4, 128], F32) for i in range(2)]
                KgTs = [sbt(sa, "KgT%d" % i, [128, 4, 128], BF16) for i in range(2)]
                qgTs = [sbt(sa, "qgT%d" % i, [128, 4, 128], BF16) for i in range(2)]
                QKTs = [sbt(sa, "QKT%d" % i, [128, 4, 128], BF16) for i in range(2)]
                TTbs = [sbt(sa, "TTb%d" % i, [128, 4, 128], BF16) for i in range(2)]
                kvtoks = [sbt(sa, "kvtok%d" % i, [128, 8, 128], BF16) for i in range(2)]
                Rb = sbt(sa, "Rb", [128, 4, 128], BF16)
                vd = sbt(sa, "vd", [128, 4, 128], BF16)
                vn = sbt(sa, "vn", [128, 4, 128], BF16)
                S = sbt(sa, "S", [128, 4, 128], F32)
                Sbf = sbt(sa, "Sbf", [128, 4, 128], BF16)
                yA = sbt(sa, "yA", [128, 512], BF16)
                qn1 = sbt(sa, "qn0", [128, 512], BF16)
                qns = [qn1, qn1]
                kt = sbt(sa, "kt", [128, 128], F32)
                kpad1 = sbt(sa, "kpad0", [128, 2, 2, 128], BF16)
                kpads = [kpad1, kpad1]
                V1 = [sbt(sa, "V1_%d" % i, [128, 2, 66], BF16) for i in range(3)]
                kTp = [sbt(sa, "kTp%d" % i, [128, 4, 128], BF16) for i in range(2)]
                qTp = sbt(sa, "qTp", [128, 4, 128], BF16)
                pT = [sbt(sa, "pT%d" % i, [128, 512], BF16) for i in range(2)]
                yB = sbt(sa, "yB", [128, 512], BF16)

                def colt(name, n):
                    return sbt(sa, name, [128, 16], F32)[:, 0:n]
                c_ss, c_rt, c_rstd = colt("c_ss", 1), colt("c_rt", 1), colt("c_rstd", 1)
                betas = [colt("beta0", 4), colt("beta1", 4)]
                gcol = colt("gcol", 4)
                xa = colt("xa", 4)
                ngc = colt("ngc", 4)
                dk = colt("dk", 4)
                bdms = [[colt("bdm0_0", 4), colt("bdm1_0", 4)], [colt("bdm0_1", 4), colt("bdm1_1", 4)]]
                oss = colt("oss", 4)
                ors = colt("ors", 4)
                qss = colt("qss", 8)
                qrs = colt("qrs", 8)
                kss = colt("kss", 2)
                krs = colt("krs", 2)
                den = colt("den", 4)
                if os.environ.get("K_DBG"):
                    print("SBUF remaining phase A", nc.sbuf_bytes_remaining)

                P.add("sp", DMA(identf[:], ident_d), w=["identf"], dma="cs0", group=True)
                P.add("sp", DMA(Uc[:], U_d), w=["Uc"], dma="cs0", group=True)
                P.add("sp", DMA(Urev[:], Urev_d), w=["Urev"], dma="cs0", group=True)
                P.add("sp", DMA(NEGM[:], NEG_d), w=["NEGM"], dma="cs0", group=True)
                P.add("sp", DMA(STR[:], STR_d), w=["STR"], dma="cs0", group=True)
                P.add("sp", DMA(m01[:], m01_d), w=["m01"], dma="cs0", group=True)
                P.add("sp", DMA(pm[:], pm_d), w=["pm"], dma="cs0", group=True)
                P.add("sp", DMA(gstage, attn_norm_d.rearrange("o (kc p) -> (o kc) p", p=128)), w=["gstage", "F0"], dma="cs0", group=True)
                P.add("sp", DMA(cstage, conv_d.rearrange("j (c p) -> (j c) p", p=128)), w=["cstage", "F1"], dma="cs0", group=True)
                P.add("pe", TR(B[6][:, 0:8], gstage, identf[0:8, 0:8]), r=["gstage", "identf"], pw=["B6"])
                P.add("pe", TR(B[6][:, 64:112], cstage, identf[0:48, 0:48]), r=["cstage", "identf"], pw=["B6"])
                P.add("act", ACT(gainT[:], B[6][:, 0:8], AF.Copy), r=["B6"], w=["gainT"])
                P.add("act", ACT(wconv[:].rearrange("p j c -> p (j c)"), B[6][:, 64:112], AF.Copy), r=["B6"], w=["wconv"])
                P.add("sp", DMA(alog[:], alog_d[0:1, :].partition_broadcast(128)), w=["alog"], dma="cs0", group=True)
                P.add("sp", DMA(dtb[:], dtb_d[0:1, :].partition_broadcast(128)), w=["dtb"], dma="cs0", group=True)
                P.add("sp", DMA(onorm[:], onorm_d[0:1, :].partition_broadcast(128)), w=["onorm"], dma="cs0", group=True)
                P.add("sp", DMA(qg[:], qg_d[0:1, :].partition_broadcast(128)), w=["qg"], dma="cs1", group=True)
                P.add("sp", DMA(kg[:], kg_d[0:1, :].partition_broadcast(128)), w=["kg"], dma="cs1", group=True)
                P.add("sp", DMA(sinks[:], sinks_d[0:1, :].partition_broadcast(128)), w=["sinks"], dma="cs1", group=True)
                for i_, (bd_, b8_, b8n_) in enumerate(((biasP_d, biasP8, "biasP8"), (biasC_d, biasC8, "biasC8"))):
                    for hf_ in range(2):
                        g_ = Gs[i_ * 2 + hf_]
                        gn_ = "G%d" % (i_ * 2 + hf_)
                        P.add("sp", DMA(g_[:].rearrange("p h d -> p (h d)"), bd_[:, hf_ * 512:(hf_ + 1) * 512]), w=[gn_], dma="cs1", group=True)
                        P.add("dve", TS(b8_[:, hf_ * 512:(hf_ + 1) * 512], g_[:].rearrange("p h d -> p (h d)"), 8.0, ALU.mult),
                              r=[gn_], pw=[b8n_])
                w_in_v = w_in_d.rearrange("(kc p) n -> p kc n", p=128)
                for i, c0 in enumerate(range(0, NWT, 512)):
                    c1 = min(c0 + 512, NWT)
                    P.add("pool", DMA(winT[:, :, c0:c1], w_in_v[:, :, 1536 + c0:1536 + c1]), pw=["winT"], dma="w%d" % i)
                P.add("dve", CP(identb[:], identf[:]), r=["identf"], w=["identb"])
                P.add("dve", MSET(onesb[:], 1.0), w=["onesb"])
                P.add("act", ACT(negA[:], alog[:], AF.Exp), r=["alog"], w=["negA"])
                P.add("dve", TS(negA[:], negA[:], -1.0, ALU.mult), r=["negA"], w=["negA"])
                P.add("act", ACT(esink[:], sinks[:], AF.Exp), r=["sinks"], w=["esink"])
                P.add("pool", MSET(kpad1[:], 0.0), w=["kpad0"])
                for i in range(2):
                    P.add("pool", MSET(kTp[i][:], 0.0), w=["kTp%d" % i])
                for i in range(3):
                    P.add("pool", MSET(V1[i][:], 1.0), w=["V1_%d" % i])
                P.add("pool", MSET(S[:], 0.0), w=["S"])
                P.add("pool", MSET(Sbf[:], 0.0), w=["Sbf"])
                P.add("pool", MSET(Rb[:], 0.0), w=["Rb"])
                P.add("pool", MSET(vd[:], 0.0), w=["vd"])
                P.add("pool", MSET(vn[:], 0.0), w=["vn"])
                P.add("pool", MSET(halo[:], 0.0), w=["halo"])

                NEG4 = NEGM[:].unsqueeze(1).to_broadcast([128, 4, 128])
                STR4 = STR[:].unsqueeze(1).to_broadcast([128, 4, 128])
                ID4 = identf[:].unsqueeze(1).to_broadcast([128, 4, 128])
                gainO4 = onorm[:].unsqueeze(1).to_broadcast([128, 4, 128])
                gainQ8 = qg[:].unsqueeze(1).to_broadcast([128, 8, 64])
                gainK2 = kg[:].unsqueeze(1).to_broadcast([128, 2, 64])
                GT8 = gainT[:].unsqueeze(2).to_broadcast([128, 8, 128])
                NSB = int(os.environ.get('K_NSB', NBLK // 4))
                b4 = lambda col: col.unsqueeze(2).to_broadcast([128, 4, 128])

                class Thr:
                    def __init__(self):
                        self.q = []

                    def add(self, eng, fn, glue=False, **kw):
                        self.q.append((eng, fn, kw, glue))

                def take(q, n):
                    k_ = 0
                    while q and (k_ < n or q[0][3]):
                        e_, f_, kw_, _g = q.pop(0)
                        P.add(e_, f_, **kw_)
                        k_ += 1

                def groups(q):
                    out = []
                    for op in q:
                        if out and (op[3] or op[0] == out[-1][-1][0]):
                            out[-1].append(op)
                        else:
                            out.append([op])
                    return out

                def merge(*thrs):
                    gs = [groups(t.q) for t in thrs]
                    for t in thrs:
                        t.q = []
                    idx = [0] * len(gs)
                    while True:
                        best, bestf = -1, 2.0
                        for i_, g_ in enumerate(gs):
                            if idx[i_] < len(g_):
                                f_ = idx[i_] / float(len(g_))
                                if f_ < bestf:
                                    best, bestf = i_, f_
                        if best < 0:
                            break
                        for e_, f_, kw_, _g in gs[best][idx[best]]:
                            P.add(e_, f_, **kw_)
                        idx[best] += 1

                def hT_of(sbk):
                    if sbk >= NPRE // 4:
                        r_ = sbk - NPRE // 4
                    else:
                        r_ = sbk % 2
                    return hTm[:, :, r_ * 512:(r_ + 1) * 512], "hTr%d" % r_

                def stage_X(T, sbk, js=(0, 1, 2, 3)):
                    hT, hTn = hT_of(sbk)
                    for j in js:
                        b = sbk * 4 + j
                        jsl = slice(j * 128, (j + 1) * 128)
                        T.add("sp", DMA(xin[:], x_d[b * 128:(b + 1) * 128, :]), w=["xin"], dma="xin")
                        T.add("act", ACT(hbf[:], xin[:], AF.Square, accum_out=c_ss), r=["xin"], w=["hbf", "c_ss"])
                        T.add("act", ACT(c_rt, c_ss, AF.Ln, scale=1.0 / D, bias=EPS), r=["c_ss"], w=["c_rt"])
                        T.add("act", ACT(c_rstd, c_rt, AF.Exp, scale=-0.5), r=["c_rt"], w=["c_rstd"])
                        T.add("act", ACT(hbf[:], xin[:], AF.Copy, scale=c_rstd), r=["xin", "c_rstd"], w=["hbf"])
                        for kc in range(8):
                            T.add("pe", TR(BT[:, kc * 128:(kc + 1) * 128], hbf[:, kc * 128:(kc + 1) * 128], identb[:]),
                                  glue=(kc > 0), r=["hbf", "identb"], pw=["BT"])
                        T.add("dve", TT(hT[:, :, jsl], v4(BT[:], 8), GT8, ALU.mult), glue=True, r=["BT", "gainT"], pw=[hTn])

                fcnt = [0]

                def stage_F(T, sbk):
                    hT, hTn = hT_of(sbk)
                    own_sb = sbk >= NPRE // 4
                    clist = list(range(12) if (own_sb or sbk == NPRE // 4 - 1) else range(4, 12))
                    cis = []
                    for c in clist:
                        cis.append(fcnt[0])
                        fcnt[0] += 1

                    def st0(c, ci):
                        p2, p3 = ci % 2, ci % 3
                        wq, wqn = wqs[p3], "wqs%d" % p3
                        T.add("pool", DMA(wq[:], w_in_v[:, :, c * 128:(c + 1) * 128]), w=[wqn], dma=wqn)
                        dgi, dgn = dg[p3], "dg%d" % p3
                        for j in range(4):
                            T.add("dve", TS(dgi[:, j, :], identf[:], wconv[:, j, c:c + 1], ALU.mult),
                                  r=["identf", "wconv"], pw=[dgn])
                        pf, pfn = B[2 + p2], "B%d" % (2 + p2)
                        for kc in range(8):
                            T.add("pe", MM(pf[:], wq[:, kc, :], hT[:, kc, :], kc == 0, kc == 7), r=[wqn, hTn], pw=[pfn])

                    def st1(c, ci):
                        p2 = ci % 2
                        pf, pfn = B[2 + p2], "B%d" % (2 + p2)
                        xt, xtn = xct[p2], "xct%d" % p2
                        if ci % 3 == 2:
                            T.add("dve", CP(xt[:, 3:515], pf[:]), r=[pfn], pw=[xtn])
                        else:
                            T.add("act", ACT(xt[:, 3:515], pf[:], AF.Copy), r=[pfn], pw=[xtn])
                        T.add("dve", CP(xt[:, 0:3], halo[:, c, 0:3]), r=["halo", "halo%d" % c], pw=[xtn])

                    def st2(c, ci):
                        p2, p3 = ci % 2, ci % 3
                        dgi, dgn = dg[p3], "dg%d" % p3
                        xt, xtn = xct[p2], "xct%d" % p2
                        pc, pcn = B[4 + p2], "B%d" % (4 + p2)
                        for j in range(4):
                            T.add("pe", MM(pc[:], dgi[:, j, :], xt[:, j:j + 512], j == 0, j == 3), r=[dgn, xtn], pw=[pcn])
                        T.add("dve", CP(halo[:, c, 0:3], xt[:, 512:515]), r=[xtn], w=["halo%d" % c])

                    def st3(c, ci):
                        p2, p4 = ci % 2, ci % 4
                        pc, pcn = B[4 + p2], "B%d" % (4 + p2)
                        sl, sn = sil[p4], "sil%d" % p4
                        T.add("act", ACT(sl[:], pc[:], AF.Exp, scale=-1.0), r=[pcn], w=[sn])
                        T.add("act", ACT(sl[:], sl[:], AF.Ln, bias=1.0), r=[sn], w=[sn])
                        T.add("act", ACT(sl[:], sl[:], AF.Exp, scale=-1.0), r=[sn], w=[sn])

                    def st4(c, ci):
                        p2, p4 = ci % 2, ci % 4
                        pc, pcn = B[4 + p2], "B%d" % (4 + p2)
                        sl, sn = sil[p4], "sil%d" % p4
                        if c >= 8:
                            T.add("dve", TT(vT[:, c - 8, :], sl[:], pc[:], ALU.mult), r=[sn, pcn], pw=["vT"])
                        else:
                            T.add("dve", TT(sl[:], sl[:], pc[:], ALU.mult), r=[sn, pcn], w=[sn])
                            T.add("dve", TT(sqs[p2][:], sl[:], sl[:], ALU.mult), r=[sn], w=["sq%d" % p2])

                    def st5(c, ci):
                        if c >= 8:
                            return
                        p2 = ci % 2
                        T.add("pe", MM(B[6][:], onesb[:], sqs[p2][:]), r=["onesb", "sq%d" % p2], pw=["B6"])
                        rti, rtn = rt[p2], "rt%d" % p2
                        T.add("act", ACT(rti[:], B[6][:], AF.Ln, bias=EPS), r=["B6"], w=[rtn])
                        T.add("act", ACT(rti[:], rti[:], AF.Exp, scale=-0.5), r=[rtn], w=[rtn])

                    def st6(c, ci):
                        if c >= 8:
                            return
                        p2, p4 = ci % 2, ci % 4
                        sl, sn = sil[p4], "sil%d" % p4
                        rti, rtn = rt[p2], "rt%d" % p2
                        scl = (128.0 ** -0.5) if c < 4 else 1.0
                        T.add("dve", STT(qkn[:, c, :], sl[:], scl, rti[:], ALU.mult, ALU.mult), r=[sn, rtn], pw=["qkn"])

                    stages = (st0, st1, st2, st3, st4, st5, st6)
                    n = len(clist)
                    for i in range(n + len(stages) - 1):
                        for k in range(len(stages) - 1, -1, -1):
                            if 0 <= i - k < n:
                                stages[k](clist[i - k], cis[i - k])

                def stage_T1(T, sbk, j):
                    b = sbk * 4 + j
                    own = b >= NPRE
                    par = b % 2
                    jsl = slice(j * 128, (j + 1) * 128)
                    hTf, hTn = hT_of(sbk)
                    hT = hTf[:, :, jsl]
                    has_kv = own or b == NPRE - 1
                    beta, bdm = betas[par], bdms[par]
                    zs, KgT, qgT, QKT, TTb, EGB = zss[par], KgTs[par], qgTs[par], QKTs[par], TTbs[par], EGBs[par]
                    zsn, KgTn, qgTn, QKTn, TTbn, EGBn = ["%s%d" % (n_, par) for n_ in ("zs", "KgT", "qgT", "QKT", "TTb", "EGB")]
                    betan = "beta%d" % par
                    qn, qnn = qns[par], "qn0"
                    kpad, kpadn = kpads[par], "kpad0"
                    kvtok, kvtokn = kvtoks[par], "kvtok%d" % par
                    v1i = b % 3
                    side = []

                    def SD(eng, fn, **kw):
                        side.append((eng, fn, kw))

                    def drain(n):
                        for _ in range(min(n, len(side))):
                            e_, f_, kw_ = side.pop(0)
                            T.add(e_, f_, **kw_)
                    for kc in range(8):
                        T.add("pe", MM(B[2][:, 256:264], hT[:, kc, :], winT[:, kc, CBA:CBA + 8], kc == 0, kc == 7),
                              r=["winT", hTn], pw=["B2"])
                    if has_kv:
                        for kc in range(8):
                            T.add("pe", MM(B[2][:, 0:256], hT[:, kc, :], winT[:, kc, CKV:CKV + 256], kc == 0, kc == 7),
                                  r=["winT", hTn], pw=["B2"])
                    if own:
                        for kc in range(8):
                            T.add("pe", MM(B[3][:], hT[:, kc, :], winT[:, kc, CQ:CQ + 512], kc == 0, kc == 7),
                                  r=["winT", hTn], pw=["B3"])
                        for kc in range(8):
                            T.add("pe", MM(B[4][:], hT[:, kc, :], winT[:, kc, CZ:CZ + 512], kc == 0, kc == 7),
                                  r=["winT", hTn], pw=["B4"])
                    T.add("act", ACT(beta, B[2][:, 256:260], AF.Exp, scale=-1.0), r=["B2"], w=[betan])
                    T.add("act", ACT(beta, beta, AF.Ln, bias=1.0), r=[betan], w=[betan])
                    T.add("act", ACT(beta, beta, AF.Exp, scale=-1.0), r=[betan], w=[betan])
                    T.add("dve", TT(xa, B[2][:, 260:264], dtb[:], ALU.add), r=["B2", "dtb"], w=["xa"])
                    T.add("act", ACT(xa, xa, AF.Exp), r=["xa"], w=["xa"])
                    T.add("act", ACT(xa, xa, AF.Ln, bias=1.0), r=["xa"], w=["xa"])
                    T.add("dve", TT(SBt[:], STR4, b4(beta), ALU.mult), r=["STR", betan], w=["SBt"])
                    T.add("dve", TT(gcol, xa, negA[:], ALU.mult), r=["xa", "negA"], w=["gcol"])

                    Gb, gm, ET, EsT = Fb[0], Fb[1], Fb[2], Fb[3]
                    T.add("dve", CP(Gb[:], b4(gcol)), r=["gcol"], w=["F0"])
                    for h in range(4):
                        T.add("pe", MM(B[5][:, h * 128:(h + 1) * 128], Gb[:, h, :], Uc[:]), r=["F0", "Uc"], pw=["B5"])
                    T.add("pe", MM(B[6][:, 0:4], Uc[:], gcol), r=["Uc", "gcol"], pw=["B6"])
                    T.add("pe", MM(B[6][:, 4:8], Urev[:], gcol), r=["Urev", "gcol"], pw=["B6"])
                    T.add("act", ACT(EGB[:], v4(B[5][:]), AF.Exp), r=["B5"], w=[EGBn])
                    T.add("dve", TS(ngc, B[6][:, 0:4], -1.0, ALU.mult), r=["B6"], w=["ngc"])
                    T.add("act", ACT(dk, B[6][:, 4:8], AF.Exp), r=["B6"], w=["dk"])
                    T.add("dve", TT(gm[:], v4(B[5][:]), NEG4, ALU.add), r=["B5", "NEGM"], w=["F1"])
                    T.add("dve", TT(gm[:], gm[:], b4(ngc), ALU.add), r=["F1", "ngc"], w=["F1"])
                    for h in range(4):
                        T.add("pe", MM(B[6][:, h * 128:(h + 1) * 128], qkn[:, 4 + h, jsl], qkn[:, 4 + h, jsl]),
                              r=["qkn"], pw=["B6"])
                    T.add("act", ACT(ET[:], gm[:], AF.Exp), r=["F1"], w=["F2"])
                    T.add("dve", TT(EsT[:], v4(B[6][:]), SBt[:], ALU.mult), r=["B6", "SBt"], w=["F3"])
                    if own:
                        T.add("act", ACT(zs[:], v4(B[4][:]), AF.Copy), r=["B4"], w=[zsn])
                        T.add("act", ACT(qraws[par][:], B[3][:], AF.Copy), r=["B3"], w=["qraw%d" % par])
                    if has_kv:
                        T.add("act", ACT(kvraws[par][:], B[2][:, 0:256], AF.Copy), r=["B2"], w=["kvraw%d" % par])
                    T.add("dve", TT(KgT[:], qkn[:, 4:8, jsl], EGB[:], ALU.mult), r=["qkn", EGBn], w=[KgTn])
                    if own:
                        T.add("dve", TT(qgT[:], qkn[:, 0:4, jsl], EGB[:], ALU.mult), r=["qkn", EGBn], w=[qgTn])
                    if own:
                        for h in range(4):
                            T.add("pe", MM(B[3][:, h * 128:(h + 1) * 128], qkn[:, 4 + h, jsl], qkn[:, h, jsl]),
                                  r=["qkn"], pw=["B3"])
                    X0, Y0 = Xc[0], Yc[0]
                    T.add("dve", TT(X0[:], ET[:], EsT[:], ALU.mult), r=["F2", "F3"], w=["Xc0"])
                    if own:
                        T.add("dve", TT(QKT[:], v4(B[3][:]), ET[:], ALU.mult), r=["B3", "F2"], w=[QKTn])
                    for c in range(2):
                        T.add("dve", STT(bdm[c], dk, m01[:, c:c + 1], beta, ALU.mult, ALU.mult),
                              r=["dk", "m01", betan], w=["bdm%d_%d" % (c, par)])
                    for h in range(4):
                        T.add("pe", TR(B[6][:, h * 128:(h + 1) * 128], X0[:, h, :], identf[:]), r=["Xc0", "identf"], pw=["B6"])
                    T.add("act", ACT(Y0[:], v4(B[6][:]), AF.Copy), r=["B6"], w=["Yc0"])
                    T.add("dve", TT(Pm[:], ID4, X0[:], ALU.subtract), r=["identf", "Xc0"], w=["Pm"])
                    for h in range(4):
                        T.add("pe", TR(BT[:, h * 128:(h + 1) * 128], qkn[:, 4 + h, jsl], identb[:]), glue=(h > 0),
                              r=["qkn", "identb"], pw=["BT"])
                    for h in range(4):
                        T.add("pe", TR(BT[:, 512 + h * 128:512 + (h + 1) * 128], vT[:, h, jsl], identb[:]), glue=True,
                              r=["vT", "identb"], pw=["BT"])
                    T.add("act", ACT(kvtok[:], v4(BT[:], 8), AF.Copy), glue=True, r=["BT"], w=[kvtokn])
                    def emit_P(k):
                        Yn_, ynn_ = Yc[(k + 1) % 2], "Yc%d" % ((k + 1) % 2)
                        for h in range(4):
                            T.add("pe", MM(B[2][:, h * 128:(h + 1) * 128], Yn_[:, h, :], Pm[:, h, :]), r=[ynn_, "Pm"], pw=["B2"])
                        if k < 4:
                            T.add("dve", TT(Pm[:], v4(B[2][:]), Pm[:], ALU.add), r=["B2", "Pm"], w=["Pm"])
                        else:
                            T.add("dve", TT(TTb[:], v4(B[2][:]), Pm[:], ALU.add), r=["B2", "Pm"], w=[TTbn])

                    for k in range(5):
                        Xk, Yk = Xc[k % 2], Yc[k % 2]
                        Xn, Yn = Xc[(k + 1) % 2], Yc[(k + 1) % 2]
                        xkn, ykn = "Xc%d" % (k % 2), "Yc%d" % (k % 2)
                        xnn, ynn = "Xc%d" % ((k + 1) % 2), "Yc%d" % ((k + 1) % 2)
                        for h in range(4):
                            T.add("pe", MM(B[6][:, h * 128:(h + 1) * 128], Xk[:, h, :], Yk[:, h, :]), r=[xkn, ykn], pw=["B6"])
                        if k < 4:
                            for h in range(4):
                                T.add("pe", MM(B[5][:, h * 128:(h + 1) * 128], Yk[:, h, :], Xk[:, h, :]), r=[xkn, ykn], pw=["B5"])
                        T.add("dve", CP(Yn[:], v4(B[6][:])), r=["B6"], w=[ynn])
                        if k < 4:
                            T.add("act", ACT(Xn[:], v4(B[5][:]), AF.Copy), r=["B5"], w=[xnn])
                        if k > 0:
                            emit_P(k - 1)
                        drain(3)
                    emit_P(4)
                    drain(len(side))

                def stage_T2(T, sbk, j):
                    b = sbk * 4 + j
                    own = b >= NPRE
                    ob = b - NPRE
                    par = b % 2
                    tsl = slice(ob * 128, (ob + 1) * 128)
                    has_kv = own or b == NPRE - 1
                    beta, bdm = betas[par], bdms[par]
                    zs, KgT, qgT, QKT, TTb, EGB = zss[par], KgTs[par], qgTs[par], QKTs[par], TTbs[par], EGBs[par]
                    zsn, KgTn, qgTn, QKTn, TTbn, EGBn = ["%s%d" % (n_, par) for n_ in ("zs", "KgT", "qgT", "QKT", "TTb", "EGB")]
                    betan = "beta%d" % par
                    qn, qnn = qns[par], "qn0"
                    kpad, kpadn = kpads[par], "kpad0"
                    kvtok, kvtokn = kvtoks[par], "kvtok%d" % par
                    osb = Gs[0]
                    if own:
                        T.add("act", ACT(Gs[2][:], zs[:], AF.Exp, scale=-1.0), r=[zsn], w=["G2"])
                        T.add("act", ACT(Gs[2][:], Gs[2][:], AF.Ln, bias=1.0), r=["G2"], w=["G2"])
                        T.add("act", ACT(Gs[2][:], Gs[2][:], AF.Exp, scale=-1.0), r=["G2"], w=["G2"])
                        T.add("pool", TT(Gs[2][:], Gs[2][:], zs[:], ALU.mult), r=["G2", zsn], w=["G2"])
                        T.add("pool", TT(Gs[2][:], Gs[2][:], gainO4, ALU.mult), r=["G2", "onorm"], w=["G2"])
                    for c in range(2):
                        for h in range(4):
                            T.add("pe", MM(B[1][:, h * 128:(h + 1) * 128], KgT[:, h, :], Sbf[:, h, :]), r=[KgTn, "Sbf"], pw=["B1"])
                        T.add("dve", TT(Rb[:], kvtok[:, 4:8, :], v4(B[1][:]), ALU.subtract), r=[kvtokn, "B1"], w=["Rb"])
                        for h in range(4):
                            T.add("pe", MM(B[1][:, h * 128:(h + 1) * 128], TTb[:, h, :], Rb[:, h, :]), r=[TTbn, "Rb"], pw=["B1"])
                        T.add("dve", TT(vd[:], v4(B[1][:]), b4(bdm[c]), ALU.mult), r=["B1", "bdm%d_%d" % (c, par)], w=["vd"])
                        if own:
                            T.add("dve", TT(vn[:], v4(B[1][:]), b4(beta), ALU.mult), r=["B1", betan], w=["vn"])
                            for h in range(4):
                                T.add("pe", MM(B[1][:, h * 128:(h + 1) * 128], qgT[:, h, :], Sbf[:, h, :], True, False),
                                      r=[qgTn, "Sbf"], pw=["B1"])
                                T.add("pe", MM(B[1][:, h * 128:(h + 1) * 128], QKT[:, h, :], vn[:, h, :], False, True),
                                      glue=True, r=[QKTn, "vn"], pw=["B1"])
                            rs = slice(c * 64, (c + 1) * 64)
                            T.add("act", ACT(osb[rs].rearrange("p h d -> p (h d)"), B[1][rs, :], AF.Copy), r=["B1"], pw=["G0"])
                        for h in range(4):
                            T.add("pe", MM(B[1][:, h * 128:(h + 1) * 128], kvtok[:, h, :], vd[:, h, :]), r=[kvtokn, "vd"], pw=["B1"])
                        col = 63 + 64 * c
                        T.add("dve", TT(S[:], S[:], EGB[:, :, col:col + 1].to_broadcast([128, 4, 128]), ALU.mult),
                              r=["S", EGBn], w=["S"])
                        T.add("dve", TT(S[:], S[:], v4(B[1][:]), ALU.add), r=["S", "B1"], w=["S"])
                        T.add("act", ACT(Sbf[:], S[:], AF.Copy), r=["S"], w=["Sbf"])
                    if own:
                        T.add("act", ACT(Gs[1][:], osb[:], AF.Square), r=["G0"], w=["G1"])
                        T.add("dve", RED(oss, Gs[1][:]), r=["G1"], w=["oss"])
                        T.add("act", ACT(oss, oss, AF.Ln, scale=1.0 / 128, bias=EPS), r=["oss"], w=["oss"])
                        T.add("act", ACT(ors, oss, AF.Exp, scale=-0.5), r=["oss"], w=["ors"])
                        T.add("dve", TT(Gs[3][:], osb[:], b4(ors), ALU.mult), r=["G0", "ors"], w=["G3"])
                        T.add("dve", TT(v4(yA[:]), Gs[3][:], Gs[2][:], ALU.mult), r=["G3", "G2"], w=["yA"])
                        for p_ in range(4):
                            T.add("pe", TR(BT[:, p_ * 128:(p_ + 1) * 128], yA[:, p_ * 128:(p_ + 1) * 128], identb[:]), glue=(p_ > 0),
                                  r=["yA", "identb"], pw=["BT"])
                        T.add("act", ACT(yAT[:, :, tsl], v4(BT[:, 0:512]), AF.Copy), glue=True, r=["BT"], w=["yAT%d" % ob])

                def stage_T2b(T, sbk, j):
                    b = sbk * 4 + j
                    own = b >= NPRE
                    ob = b - NPRE
                    par = b % 2
                    tsl = slice(ob * 128, (ob + 1) * 128)
                    has_kv = own or b == NPRE - 1
                    if not has_kv:
                        return
                    qn, qnn = qns[par], "qn0"
                    kpad, kpadn = kpads[par], "kpad0"
                    kcur, kprev = b % 2, (b - 1) % 2
                    v1c, v1p = b % 3, (b - 1) % 3
                    kvr, kvrn = kvraws[par], "kvraw%d" % par
                    k2 = lambda ap: ap.rearrange("p (h d) -> p h d", h=2)
                    for kv_ in range(2):
                        T.add("act", ACT(pT[0][:, kv_ * 64:(kv_ + 1) * 64], kvr[:, kv_ * 64:(kv_ + 1) * 64], AF.Square,
                                         accum_out=kss[:, kv_:kv_ + 1]), r=[kvrn], pw=["pT0", "kss"])
                    T.add("act", ACT(kss, kss, AF.Ln, scale=1.0 / 64, bias=EPS), r=["kss"], w=["kss"])
                    T.add("act", ACT(krs, kss, AF.Exp, scale=-0.5), r=["kss"], w=["krs"])
                    T.add("pool", TT(k2(kt[:]), k2(kvr[:, 0:128]), krs.unsqueeze(2).to_broadcast([128, 2, 64]), ALU.mult),
                          r=[kvrn, "krs"], w=["kt"])
                    T.add("pool", TT(kpad[:, :, 0, 0:64], k2(kt[:]), gainK2, ALU.mult), r=["kt", "kg"], pw=[kpadn])
                    T.add("pool", TT(kpad[:, :, 1, 64:128], k2(kt[:]), gainK2, ALU.mult), r=["kt", "kg"], pw=[kpadn])
                    T.add("act", ACT(V1[v1c][:, :, 0:64], k2(kvr[:, 128:256]), AF.Copy), r=[kvrn], pw=["V1_%d" % v1c])
                    if own:
                        qr, qrn = qraws[par], "qraw%d" % par
                        q8 = lambda ap: ap.rearrange("p (h d) -> p h d", h=8)
                        for h_ in range(8):
                            T.add("act", ACT(pT[0][:, h_ * 64:(h_ + 1) * 64], qr[:, h_ * 64:(h_ + 1) * 64], AF.Square,
                                             accum_out=qss[:, h_:h_ + 1]), r=[qrn], pw=["pT0", "qss"])
                        T.add("act", ACT(qss, qss, AF.Ln, scale=1.0 / 64, bias=EPS), r=["qss"], w=["qss"])
                        T.add("act", ACT(qrs, qss, AF.Exp, scale=-0.5), r=["qss"], w=["qrs"])
                        T.add("pool", TT(q8(qr[:]), q8(qr[:]), qrs.unsqueeze(2).to_broadcast([128, 8, 64]), ALU.mult),
                              r=[qrn, "qrs"], w=[qrn])
                        T.add("pool", TT(q8(qn[:]), q8(qr[:]), gainQ8, ALU.mult), r=[qrn, "qg"], w=[qnn])
                    first = True
                    for g in range(2):
                        for var in range(2):
                            i = g * 2 + var
                            T.add("pe", TR(BT[:, i * 128:(i + 1) * 128], kpad[:, g, var, :], identb[:]), glue=(not first),
                                  r=[kpadn, "identb"], pw=["BT"])
                            first = False
                    if not own:
                        T.add("act", ACT(kTp[kcur][:], v4(BT[:, 0:512]), AF.Copy), glue=True, r=["BT"], w=["kTp%d" % kcur])
                        return
                    for p_ in range(4):
                        T.add("pe", TR(BT[:, 512 + p_ * 128:512 + (p_ + 1) * 128], qn[:, p_ * 128:(p_ + 1) * 128], identb[:]), glue=True,
                              r=[qnn, "identb"], pw=["BT"])
                    T.add("act", ACT(kTp[kcur][:], v4(BT[:, 0:512]), AF.Copy), glue=True, r=["BT"], w=["kTp%d" % kcur])
                    T.add("act", ACT(qTp[:], v4(BT[:, 512:1024]), AF.Copy), glue=True, r=["BT"], w=["qTp"])
                    for g in range(2):
                        for (kk, b8, b8n, pTi, pTn, is_prev) in ((kprev, biasP8, "biasP8", pT[0], "pT0", True),
                                                               (kcur, biasC8, "biasC8", pT[1], "pT1", False)):
                            for j_ in range(4):
                                h = 4 * g + j_
                                T.add("pe", MM(B[0][:, j_ * 128:(j_ + 1) * 128], kTp[kk][:, g * 2 + h % 2, :], qTp[:, h // 2, :],
                                               j_ == 0, False, skip=True), r=["kTp%d" % kk, "qTp"], pw=["B0"])
                            T.add("pe", MM(B[0][:], identb[:], b8[:, g * 512:(g + 1) * 512], False, True, skip=True),
                                  r=["identb", b8n], pw=["B0"])
                            if is_prev and ob == 0:
                                T.add("act", ACT(pTi[:], B[0][:], AF.Exp, scale=0.125, bias=pm[:, 0:1]), r=["B0", "pm"], w=[pTn])
                            else:
                                T.add("act", ACT(pTi[:], B[0][:], AF.Exp, scale=0.125), r=["B0"], w=[pTn])
                        for j_ in range(4):
                            T.add("pe", MM(B[0][:, j_ * 65:(j_ + 1) * 65], pT[0][:, j_ * 128:(j_ + 1) * 128], V1[v1p][:, g, 0:65], True, False),
                                  r=["pT0", "V1_%d" % v1p], pw=["B0"])
                            T.add("pe", MM(B[0][:, j_ * 65:(j_ + 1) * 65], pT[1][:, j_ * 128:(j_ + 1) * 128], V1[v1c][:, g, 0:65], False, True),
                                  glue=True, r=["pT1", "V1_%d" % v1c], pw=["B0"])
                        pvv = B[0][:, 0:260].rearrange("p (h d) -> p h d", h=4)
                        T.add("dve", TT(den.unsqueeze(2), pvv[:, :, 64:65], esink[:, 4 * g:4 * g + 4].unsqueeze(2), ALU.add),
                              r=["B0", "esink"], w=["den"])
                        T.add("dve", RC(den, den), r=["den"], w=["den"])
                        T.add("dve", TT(yB[:, g * 256:(g + 1) * 256].rearrange("p (h d) -> p h d", h=4), pvv[:, :, 0:64],
                                        den.unsqueeze(2).to_broadcast([128, 4, 64]), ALU.mult), r=["B0", "den"], pw=["yB"])
                    for p_ in range(4):
                        T.add("pe", TR(BT[:, p_ * 128:(p_ + 1) * 128], yB[:, p_ * 128:(p_ + 1) * 128], identb[:]), glue=(p_ > 0),
                              r=["yB", "identb"], pw=["BT"])
                    T.add("act", ACT(yBT[:, :, tsl], v4(BT[:, 0:512]), AF.Copy), glue=True, r=["BT"], w=["yBT%d" % ob])

                prev = None
                for sbk in range(NSB):
                    x_in_t2 = (sbk + 1 < NSB)
                    for j in range(4):
                        t1 = Thr()
                        if j == 0:
                            if sbk == 0:
                                stage_X(t1, 0)
                            stage_F(t1, sbk)
                            if (not x_in_t2) and sbk + 1 < NSB:
                                stage_X(t1, sbk + 1)
                        stage_T1(t1, sbk, j)
                        if prev is None:
                            merge(t1)
                        else:
                            merge(t1, prev[0], prev[1])
                        prev = (Thr(), Thr())
                        stage_T2(prev[0], sbk, j)
                        stage_T2b(prev[1], sbk, j)
                        if x_in_t2 and j < 3:
                            stage_X(prev[0], sbk + 1, (j,) if j < 2 else (2, 3))
                if prev is not None:
                    merge(prev[0], prev[1])
                P.barrier()
                P.emit(nc, semf)

            sb1 = contextlib.ExitStack()
            with sb1:
                P = Prog("b")
                wgA = sbt(sb1, "wgA", [128, 8, D], BF16)
                wgB = sbt(sb1, "wgB", [128, 8, D], BF16)
                wA = sbt(sb1, "wA", [128, 4, D], BF16)
                wB = sbt(sb1, "wB", [128, 4, D], BF16)
                sA = sbt(sb1, "sA", [128, 512], F32)
                sB = sbt(sb1, "sB", [128, 512], F32)
                t1 = sbt(sb1, "t1", [128, 512], F32)
                t2 = sbt(sb1, "t2", [128, 512], F32)
                mtiles = [sbt(sb1, "mtile%d" % i, [128, 8, 512], BF16) for i in range(2)]
                w_in_v = w_in_d.rearrange("(kc p) n -> p kc n", p=128)
                w_a_v = w_a_d.rearrange("(kc p) n -> p kc n", p=128)
                w_b_v = w_b_d.rearrange("(kc p) n -> p kc n", p=128)
                pieces = ((0, 128), (128, 512), (512, 1024))
                def piece_of(m):
                    return 0 if m == 0 else (1 if m < 4 else 2)
                for pi, (c0, c1) in enumerate(pieces):
                    cs = slice(c0, c1)
                    P.add("pool", DMA(wgA[:, :, cs], w_in_v[:, :, 2824 + c0:2824 + c1]), w=["wgA%d" % pi], dma="wgA%d" % pi)
                    P.add("pool", DMA(wgB[:, :, cs], w_in_v[:, :, 3848 + c0:3848 + c1]), w=["wgB%d" % pi], dma="wgB%d" % pi)
                    P.add("pool", DMA(wA[:, :, cs], w_a_v[:, :, cs]), w=["wA%d" % pi], dma="wA%d" % pi)
                    P.add("pool", DMA(wB[:, :, cs], w_b_v[:, :, cs]), w=["wB%d" % pi], dma="wB%d" % pi)
                wg_vb = w_gate_d.rearrange("(kc p) n -> p kc n", p=128)
                wu_vb = w_up_d.rearrange("(kc p) n -> p kc n", p=128)
                for fb in range(NF):
                    fsl = slice(fb * 128, (fb + 1) * 128)
                    wcr = wcache[fb * 128:(fb + 1) * 128, :]
                    P.add("pool", DMA(wcr[:, 0:1024].rearrange("p (kc n) -> p kc n", kc=8), wg_vb[:, :, fsl]), dma="wcf")
                    P.add("pool", DMA(wcr[:, 1024:2048].rearrange("p (kc n) -> p kc n", kc=8), wu_vb[:, :, fsl]), dma="wcf")
                wo_vb = w_out_d.rearrange("(kc p) n -> p kc n", p=128)
                wd_vb = w_down_d.rearrange("(f p) n -> p f n", p=128)
                woc3 = wocache.rearrange("p (kc n) -> p kc n", kc=8)
                wdc3 = wdcache.rearrange("p (f n) -> p f n", f=NF)
                for hf in range(2):
                    cs_ = slice(hf * 512, (hf + 1) * 512)
                    P.add("pool", DMA(woc3[:, :, cs_], wo_vb[:, :, cs_]), dma="wcf")
                for hf in range(2):
                    cs_ = slice(hf * 512, (hf + 1) * 512)
                    for (f0, f1) in ((0, 8), (8, 16), (16, 22)):
                        P.add("pool", DMA(wdc3[:, f0:f1, cs_], wd_vb[:, f0:f1, cs_]), dma="wcf")
                it = 0
                for tt_ in range(int(os.environ.get('K_NB1', 4))):
                    ts_ = slice(tt_ * 512, (tt_ + 1) * 512)
                    for m in range(8):
                        ms = slice(m * 128, (m + 1) * 128)
                        hf = piece_of(m)
                        if it % 2 == 0:
                            bgA, bgB, bpA, bpB = B[0], B[1], B[2], B[6]
                            nA, nB, nPA, nPB = "b0", "b1", "b2", "b6"
                        else:
                            bgA, bgB, bpA, bpB = B[3], B[4], B[5], B[6]
                            nA, nB, nPA, nPB = "b3", "b4", "b5", "b6"
                        for kc in range(8):
                            P.add("pe", MM(bgA[:], wgA[:, kc, ms], hTm[:, kc, ts_], kc == 0, kc == 7), r=["wgA%d" % hf, "hTm%d" % tt_], pw=[nA])
                        for kc in range(8):
                            P.add("pe", MM(bgB[:], wgB[:, kc, ms], hTm[:, kc, ts_], kc == 0, kc == 7), r=["wgB%d" % hf, "hTm%d" % tt_], pw=[nB])
                        for kc in range(4):
                            P.add("pe", MM(bpA[:], wA[:, kc, ms], yAT[:, kc, ts_], kc == 0, kc == 3), r=["wA%d" % hf], pw=[nPA])
                        for kc in range(4):
                            P.add("pe", MM(bpB[:], wB[:, kc, ms], yBT[:, kc, ts_], kc == 0, kc == 3), r=["wB%d" % hf], pw=[nPB])
                        P.add("act", ACT(sA[:], bgA[:], AF.Sigmoid), r=[nA], w=["sA"])
                        P.add("act", ACT(sB[:], bgB[:], AF.Sigmoid), r=[nB], w=["sB"])
                        P.add("dve", TT(t1[:], sA[:], bpA[:], ALU.mult), r=["sA", nPA], w=["t1"])
                        P.add("dve", TT(t2[:], sB[:], bpB[:], ALU.mult), r=["sB", nPB], w=["t2"])
                        P.add("dve", TT(mtiles[tt_ % 2][:, m, :], t1[:], t2[:], ALU.add), r=["t1", "t2"], pw=["mtile%d" % (tt_ % 2)])
                        it += 1
                    P.add("act", ACT(hTm[:, :, ts_], mtiles[tt_ % 2][:], AF.Copy), r=["mtile%d" % (tt_ % 2)], w=["hTm%d" % tt_])
                P.barrier()
                P.emit(nc, semf)

        s2 = contextlib.ExitStack()
        with s2:
            P = Prog("c")
            wout = sbt(s2, "wout", [128, 8, D], BF16)
            wdown = sbt(s2, "wdown", [128, NF, D], BF16)
            gainF = sbt(s2, "gainF", [128, D], F32)
            xin2 = [sbt(s2, "xin2_%d" % i, [128, D], F32) for i in range(2)]
            x1 = [sbt(s2, "x1_%d" % i, [128, 4, D], F32) for i in range(2)]
            h2bf = [sbt(s2, "h2bf%d" % i, [128, D], BF16) for i in range(2)]
            h2T = [sbt(s2, "h2T%d" % i, [128, 8, 512], BF16) for i in range(2)]
            actT = sbt(s2, "actT", [128, NF, 512], BF16)
            wgu = [sbt(s2, "wgu%d" % i, [128, 16, 128], BF16) for i in range(3)]
            sg = [sbt(s2, "sg%d" % i, [128, 512], F32) for i in range(2)]
            ost = [sbt(s2, "ost%d" % i, [128, D], F32) for i in range(2)]
            cc = [sbt(s2, "cc%d" % i, [128, 16], F32) for i in range(3)]
            P.add("sp", DMA(gainF[:], ffn_norm_d[0:1, :].partition_broadcast(128)), w=["gainF"], dma="gF")
            wo_v = w_out_d.rearrange("(kc p) n -> p kc n", p=128)
            P.add("sp", DMA(wout[:].rearrange("p a b -> p (a b)"), wocache[:, :]), w=["wout"], dma="wout")
            wd_v = w_down_d.rearrange("(f p) n -> p f n", p=128)
            wg_v = w_gate_d.rearrange("(kc p) n -> p kc n", p=128)
            wu_v = w_up_d.rearrange("(kc p) n -> p kc n", p=128)
            NT2 = int(os.environ.get('K_NB2', 4))

            def x1_partA(t, blk):
                tb = t * 4 + blk
                xi, xn = xin2[tb % 2], "xin2_%d" % (tb % 2)
                x1t, hb = x1[t % 2], h2bf[tb % 2]
                x1n, hbn = "x1_%d_%d" % (t % 2, blk), "h2bf%d" % (tb % 2)
                P.add("sp", DMA(xi[:], x_d[(NPRE + tb) * 128:(NPRE + tb + 1) * 128, :]), w=[xn], dma=xn)
                for half in range(2):
                    hs = slice(half * 512, (half + 1) * 512)
                    for kc in range(8):
                        P.add("pe", MM(B[half][:], hTm[:, kc, tb * 128:(tb + 1) * 128], wout[:, kc, hs], kc == 0, kc == 7),
                              r=["wout"], pw=["b%d" % half])
                    P.add("dve", TT(x1t[:, blk, hs], xi[:, hs], B[half][:], ALU.add), r=[xn, "b%d" % half], pw=[x1n])
                P.add("act", ACT(hb[:], x1t[:, blk, :], AF.Square, accum_out=cc[0][:, 0:1]), r=[x1n], w=[hbn, "cc0"])
                P.add("act", ACT(cc[1][:, 0:1], cc[0][:, 0:1], AF.Sqrt, scale=1.0 / D, bias=EPS), r=["cc0"], w=["cc1"])
                P.add("dve", RC(cc[2][:, 0:1], cc[1][:, 0:1]), r=["cc1"], w=["cc2"])
                P.add("dve", STT(hb[:], x1t[:, blk, :], cc[2][:, 0:1], gainF[:], ALU.mult, ALU.mult),
                      r=[x1n, "cc2", "gainF"], w=[hbn])

            def x1_partB(t, blk):
                tb = t * 4 + blk
                hb, hbn = h2bf[tb % 2], "h2bf%d" % (tb % 2)
                for kc in range(8):
                    P.add("pe", TR(BT[:, kc * 128:(kc + 1) * 128], hb[:, kc * 128:(kc + 1) * 128], identb[:]),
                          r=[hbn], pw=["BT"])
                P.add("act", ACT(h2T[t % 2][:, :, blk * 128:(blk + 1) * 128], v4(BT[:], 8), AF.Copy), r=["BT"], pw=["h2T%d" % (t % 2)])

            gu_it = [0]

            def gu(t, f):
                it = gu_it[0]
                gu_it[0] += 1
                sl_ = it % 3
                wg, wn = wgu[sl_], "wgu%d" % sl_
                fs = slice(f * 128, (f + 1) * 128)
                wgf = wg[:].rearrange("p a b -> p (a b)")
                P.add("sp", DMA(wgf, wcache[f * 128:(f + 1) * 128, :]), w=[wn], dma=wn)
                bg, bu = (B[2], B[3]) if it % 2 == 0 else (B[4], B[5])
                ng, nu = ("b2", "b3") if it % 2 == 0 else ("b4", "b5")
                hT2, hT2n = h2T[t % 2], "h2T%d" % (t % 2)
                for kc in range(8):
                    P.add("pe", MM(bg[:], wg[:, kc, :], hT2[:, kc, :], kc == 0, kc == 7), r=[wn, hT2n], pw=[ng])
                for kc in range(8):
                    P.add("pe", MM(bu[:], wg[:, 8 + kc, :], hT2[:, kc, :], kc == 0, kc == 7), r=[wn, hT2n], pw=[nu])
                sgi = sg[it % 2]
                P.add("act", ACT(sgi[:], bg[:], AF.Silu), r=[ng], w=["sg%d" % (it % 2)])
                P.add("dve", TT(actT[:, f, :], sgi[:], bu[:], ALU.mult), r=["sg%d" % (it % 2), nu], pw=["actT"])

            def down(t, blk):
                tb = t * 4 + blk
                oi, on = ost[tb % 2], "ost%d" % (tb % 2)
                for half in range(2):
                    hs = slice(half * 512, (half + 1) * 512)
                    for f in range(NF):
                        P.add("pe", MM(B[half][:], actT[:, f, blk * 128:(blk + 1) * 128], wdown[:, f, hs], f == 0, f == NF - 1),
                              r=["actT", "wdown"], pw=["b%d" % half])
                    P.add("dve", TT(oi[:, hs], x1[t % 2][:, blk, hs], B[half][:], ALU.add),
                          r=["x1_%d_%d" % (t % 2, blk), "b%d" % half], pw=[on])
                P.add("sp", DMA(out_d[tb * 128:(tb + 1) * 128, :], oi[:]), r=[on], dma=on)

            for blk in range(4):
                x1_partA(0, blk)
                x1_partB(0, blk)
            wdf = wdown[:].rearrange("p a b -> p (a b)")
            for i_, (c0_, c1_) in enumerate(((0, 8 * D), (8 * D, 16 * D), (16 * D, NF * D))):
                P.add("sp", DMA(wdf[:, c0_:c1_], wdcache[:, c0_:c1_]), pw=["wdown"], dma="wd%d" % i_)
            for t in range(NT2):
                for f in range(NF):
                    gu(t, f)
                    if t + 1 < NT2:
                        if f in (1, 6, 11, 16):
                            x1_partA(t + 1, (f - 1) // 5)
                        if f in (4, 9, 14, 19):
                            x1_partB(t + 1, (f - 4) // 5)
                for blk in range(4):
                    down(t, blk)
            P.barrier()
            P.emit(nc, semf)
    return nc


def _t5_bucket(dist):
    n = np.maximum(dist, 0)
    max_exact = 16
    nf = np.maximum(n, 1).astype(np.float32)
    large = max_exact + (np.log(nf / max_exact) / math.log(128 / max_exact) * (32 - max_exact)).astype(np.int32)
    large = np.minimum(large, 31)
    return np.where(n < max_exact, n, large)


_NC_CACHE = {}


def kernel(x, attn_norm, w_in, dn_conv, dn_a_log, dn_dt_bias, dn_out_norm, swa_q_norm, swa_k_norm,
           swa_sinks, rel_bias, w_branch_dn, w_branch_swa, w_out, ffn_norm, w_gate, w_up, w_down):
    f = lambda a: np.ascontiguousarray(np.asarray(a, dtype=np.float32))
    x = f(x)
    if "nc" not in _NC_CACHE:
        _NC_CACHE["nc"] = build()
    nc = _NC_CACHE["nc"]
    idx = np.arange(128)
    same = (idx[:, None] // 64) == (idx[None, :] // 64)
    U = ((idx[:, None] <= idx[None, :]) & same).astype(np.float32)
    Urev = ((idx[:, None] > idx[None, :]) & same).astype(np.float32)
    NEGM = np.where((idx[None, :] >= idx[:, None]) & same, 0.0, -1e5).astype(np.float32)
    STRICT = ((idx[None, :] > idx[:, None]) & same).astype(np.float32)
    m01 = np.stack([(idx < 64), (idx >= 64)], 1).astype(np.float32)
    ident = np.eye(128, dtype=np.float32)
    rb = f(rel_bias)
    s_i = idx[:, None]
    q_i = idx[None, :]
    dist_prev = 128 + q_i - s_i
    dist_cur = q_i - s_i
    biasP = np.empty((128, 8, 128), np.float32)
    biasC = np.empty((128, 8, 128), np.float32)
    bk_p = _t5_bucket(dist_prev)
    bk_c = _t5_bucket(dist_cur)
    ok_p = (dist_prev >= 0) & (dist_prev < 128)
    ok_c = (dist_cur >= 0) & (dist_cur < 128)
    for h in range(8):
        biasP[:, h, :] = np.where(ok_p, rb[bk_p, h], NEGBIG)
        biasC[:, h, :] = np.where(ok_c, rb[bk_c, h], NEGBIG)
    biasP = biasP.reshape(128, 1024)
    biasC = biasC.reshape(128, 1024)
    common = {
        "w_in": f(w_in[0]), "w_a": f(w_branch_dn[0]), "w_b": f(w_branch_swa[0]), "w_out": f(w_out[0]),
        "w_gate": f(w_gate[0]), "w_up": f(w_up[0]), "w_down": f(w_down[0]),
        "attn_norm": f(attn_norm), "ffn_norm": f(ffn_norm), "dn_conv": f(dn_conv[0]),
        "a_log": f(dn_a_log), "dt_bias": f(dn_dt_bias), "out_norm": f(dn_out_norm),
        "q_norm": f(swa_q_norm), "k_norm": f(swa_k_norm), "sinks": f(swa_sinks),
        "biasP": biasP, "biasC": biasC, "ident": ident, "U": U, "Urev": Urev, "NEGM": NEGM,
        "STRICT": STRICT, "m01": m01,
    }
    in_maps = []
    for core in range(8):
        bt, half = core // 2, core % 2
        xe = np.zeros((NBLK * 128, D), np.float32)
        if half == 1:
            xe[:] = x[bt]
        else:
            xe[NPRE * 128:] = x[bt, :NOWN * 128]
        pmv = np.full((128, 1), NEGBIG if half == 0 else 0.0, np.float32)
        d = dict(common)
        d["x"] = xe
        d["pm"] = pmv
        in_maps.append(d)
    res = run_bass_kernel_spmd(nc, in_maps, core_ids=list(range(8)))
    out = np.empty((4, 4096, D), np.float32)
    for core in range(8):
        bt, half = core // 2, core % 2
        out[bt, half * 2048:(half + 1) * 2048] = res.results[core]["out"]
    return out
```
